# Optimizing a Trainium2 kernel written in Bass

```python
import math
import jax
import jax.numpy as jnp
from jax import lax
import numpy as np

D_MODEL = 1024
BATCH = 1
SEQ = 16384
DEPTH = 4

GRID_W = 64
CTX_LEN = 256
N_BRANCH = 3
BRANCH_WIDTH = 512
EPS = 1e-6

LRU_WIDTH = 512
LRU_HEADS = 8
LRU_HEAD_DIM = LRU_WIDTH // LRU_HEADS
LRU_C = 8.0
CONV_W = 4
CONV_LEFT = 2
CONV_RIGHT = CONV_W - 1 - CONV_LEFT

RWKV_WIDTH = 512
RWKV_HEAD_DIM = 64
RWKV_HEADS = RWKV_WIDTH // RWKV_HEAD_DIM
DECAY_LORA = 64
ICLR_LORA = 64
GATE_LORA = 128
RWKV_IN = 3 * RWKV_WIDTH + 2 * DECAY_LORA + 2 * ICLR_LORA + GATE_LORA
RWKV_GN_EPS = 64e-5

S5_WIDTH = 512
S5_GROUP = 16
S5_GROUPS = S5_WIDTH // S5_GROUP
S5_STATE = 64

N_IN = 2 * LRU_WIDTH + RWKV_IN + S5_WIDTH + N_BRANCH * D_MODEL

FFN_HIDDEN = -(-8 * D_MODEL // (3 * 256)) * 256

kernel_name = 'hybrid_rglru_rwkv7_s5_prefix_ctx_dit'


def rms_norm(x, g):
    xf = x.astype(jnp.float32)
    y = xf * lax.rsqrt(jnp.mean(xf * xf, axis=-1, keepdims=True) + EPS)
    return (y * g.astype(jnp.float32)).astype(x.dtype)


def modulate(xn, shift, scale):
    return xn * (1.0 + scale) + shift


def to_col_major(t, rows):
    b, l, ch = t.shape
    return t.reshape(b, rows, GRID_W, ch).transpose(0, 2, 1, 3).reshape(b, l, ch)


def to_raster(t, rows):
    b, l, ch = t.shape
    return t.reshape(b, GRID_W, rows, ch).transpose(0, 2, 1, 3).reshape(b, l, ch)


def conv_centred(x, w, bias):
    l = x.shape[1]
    xp = jnp.pad(x, ((0, 0), (CONV_LEFT, CONV_RIGHT), (0, 0)))
    out = bias + w[0] * xp[:, 0:l]
    for j in range(1, CONV_W):
        out = out + w[j] * xp[:, j:j + l]
    return out


def _lin_combine(e1, e2):
    a1, b1 = e1
    a2, b2 = e2
    return a1 * a2, a2 * b1 + b2


def linear_scan(a, b, h0, reverse):
    j = -1 if reverse else 0
    b = b.at[:, j].add(a[:, j] * h0)
    _, h = lax.associative_scan(_lin_combine, (a, b), reverse=reverse, axis=1)
    return h, h[:, 0] if reverse else h[:, -1]


def _complex_combine(e1, e2):
    ar1, ai1, br1, bi1 = e1
    ar2, ai2, br2, bi2 = e2
    return (ar2 * ar1 - ai2 * ai1, ar2 * ai1 + ai2 * ar1,
            ar2 * br1 - ai2 * bi1 + br2, ar2 * bi1 + ai2 * br1 + bi2)


def complex_linear_scan(a_re, a_im, b_re, b_im, h0_re, h0_im, reverse):
    j = -1 if reverse else 0
    b_re = b_re.at[:, j].add(a_re[:, j] * h0_re - a_im[:, j] * h0_im)
    b_im = b_im.at[:, j].add(a_re[:, j] * h0_im + a_im[:, j] * h0_re)
    _, _, h_re, h_im = lax.associative_scan(_complex_combine, (a_re, a_im, b_re, b_im), reverse=reverse, axis=1)
    k = 0 if reverse else -1
    return h_re, h_im, h_re[:, k], h_im[:, k]


def rglru_direction(xc, w_a, b_a, w_x, b_x, lam, h0, reverse):
    b, l, _ = xc.shape
    xh = xc.reshape(b, l, LRU_HEADS, LRU_HEAD_DIM)
    gate_r = jax.nn.sigmoid(jnp.einsum('blhi,hij->blhj', xh, w_a).reshape(b, l, LRU_WIDTH) + b_a)
    gate_i = jax.nn.sigmoid(jnp.einsum('blhi,hij->blhj', xh, w_x).reshape(b, l, LRU_WIDTH) + b_x)
    log_a = -LRU_C * gate_r * jax.nn.softplus(-lam)
    a = jnp.exp(log_a)
    u = jnp.sqrt(-jnp.expm1(2.0 * log_a)) * (gate_i * xc)
    return linear_scan(a, u, h0, reverse)


def mixer_rglru(xa, ga, conv_w, conv_b, w_a, b_a, w_x, b_x, lam, h0):
    xc = conv_centred(xa, conv_w, conv_b)
    hs, fins = [], []
    for d in range(2):
        h, f = rglru_direction(xc, w_a[d], b_a[d], w_x[d], b_x[d], lam[d], h0[d], d == 1)
        hs.append(h)
        fins.append(f)
    return jax.nn.gelu(ga) * (hs[0] + hs[1]), jnp.stack(fins)


def token_shift(z, mu):
    zp = jnp.pad(z, ((0, 0), (1, 1), (0, 0)))
    return z + mu * (0.5 * (zp[:, :-2] + zp[:, 2:]) - z)


def wkv7_scan(r, w, k, v, kk, iclr, s0, reverse):
    def step(s, inp):
        r_t, w_t, k_t, v_t, kk_t, a_t = inp
        sa = jnp.einsum('bhvk,bhk->bhv', s, kk_t)
        s = (s * w_t[:, :, None, :] - sa[..., None] * (kk_t * a_t)[:, :, None, :]
             + v_t[..., None] * k_t[:, :, None, :])
        return s, jnp.einsum('bhvk,bhk->bhv', s, r_t)
    xs = tuple(jnp.moveaxis(t, 1, 0) for t in (r, w, k, v, kk, iclr))
    s_fin, y = lax.scan(step, s0, xs, reverse=reverse)
    return jnp.moveaxis(y, 0, 1), s_fin


def mixer_rwkv7(zb, mu, w0, w2, a0, a2, g2, k_k, k_a, r_k, ln_w, ln_b, s0):
    b, l, _ = zb.shape
    zb = token_shift(zb, mu)
    wdt = RWKV_WIDTH
    r, k, v, wd, ad, gd = jnp.split(
        zb, [wdt, 2 * wdt, 3 * wdt, 3 * wdt + 2 * DECAY_LORA, 3 * wdt + 2 * DECAY_LORA + 2 * ICLR_LORA], axis=-1)

    def heads(t):
        return t.reshape(b, l, RWKV_HEADS, RWKV_HEAD_DIM)

    kk = heads(k * k_k)
    kk = kk * lax.rsqrt(jnp.sum(kk * kk, axis=-1, keepdims=True) + 1e-12)
    r_h, v_h = heads(r), heads(v)
    wd = wd.reshape(b, l, 2, DECAY_LORA)
    ad = ad.reshape(b, l, 2, ICLR_LORA)
    ys, bonuses, fins = [], [], []
    for d in range(2):
        w_log = -jax.nn.softplus(-(w0[d] + jnp.tanh(wd[:, :, d]) @ w2[d])) - 0.5
        decay = jnp.exp(-jnp.exp(w_log))
        iclr = jax.nn.sigmoid(a0[d] + ad[:, :, d] @ a2[d])
        k_d = heads(k * (1.0 + (iclr - 1.0) * k_a))
        y_d, s_d = wkv7_scan(r_h, heads(decay), k_d, v_h, kk, heads(iclr), s0[d], d == 1)
        ys.append(y_d)
        bonuses.append(jnp.sum(r_h * k_d * r_k, axis=-1, keepdims=True) * v_h)
        fins.append(s_d)
    y = ys[0] + ys[1]
    mean = jnp.mean(y, axis=-1, keepdims=True)
    var = jnp.mean(jnp.square(y - mean), axis=-1, keepdims=True)
    y = ((y - mean) * lax.rsqrt(var + RWKV_GN_EPS)).reshape(b, l, wdt) * ln_w + ln_b
    y = y + (bonuses[0] + bonuses[1]).reshape(b, l, wdt)
    g = jax.nn.sigmoid(gd) @ g2
    return y * g, jnp.stack(fins)


def s5_direction(u, lam_re, lam_im, log_step, b_re, b_im, c_re, c_im, h0_re, h0_im, reverse):
    step = jnp.exp(log_step)[:, None]
    x_re, ang = lam_re * step, lam_im * step
    mag = jnp.exp(x_re)
    lb_re, lb_im = mag * jnp.cos(ang), mag * jnp.sin(ang)
    nr = jnp.expm1(x_re) * jnp.cos(ang) - 2.0 * jnp.square(jnp.sin(0.5 * ang))
    den = lam_re * lam_re + lam_im * lam_im
    f_re = (nr * lam_re + lb_im * lam_im) / den
    f_im = (lb_im * lam_re - nr * lam_im) / den
    bb_re = f_re[..., None] * b_re - f_im[..., None] * b_im
    bb_im = f_re[..., None] * b_im + f_im[..., None] * b_re
    bu_re = jnp.einsum('blgc,gpc->blgp', u, bb_re)
    bu_im = jnp.einsum('blgc,gpc->blgp', u, bb_im)
    a_re = jnp.broadcast_to(lb_re, bu_re.shape)
    a_im = jnp.broadcast_to(lb_im, bu_im.shape)
    h_re, h_im, fin_re, fin_im = complex_linear_scan(a_re, a_im, bu_re, bu_im, h0_re, h0_im, reverse)
    y = jnp.einsum('blgp,gcp->blgc', h_re, c_re) - jnp.einsum('blgp,gcp->blgc', h_im, c_im)
    return y, jnp.stack([fin_re, fin_im])


def mixer_s5(u, lam_re, lam_im, log_step, b_re, b_im, c_re, c_im, d_skip, w_glu, b_glu, h0):
    b, l, _ = u.shape
    ug = u.reshape(b, l, S5_GROUPS, S5_GROUP)
    ys, fins = [], []
    for d in range(2):
        y_d, f_d = s5_direction(ug, lam_re[d], lam_im[d], log_step[d], b_re[d], b_im[d],
                                c_re[d], c_im[d], h0[d, 0], h0[d, 1], d == 1)
        ys.append(y_d)
        fins.append(f_d)
    y = jax.nn.gelu((ys[0] + ys[1]).reshape(b, l, S5_WIDTH) + d_skip * u)
    return y * jax.nn.sigmoid(y @ w_glu + b_glu), jnp.stack(fins)


def token_mixers(z, lp, h_lru, s_rwkv, h_s5, rows):
    z = z.astype(jnp.float32)
    o1 = LRU_WIDTH
    o2 = 2 * LRU_WIDTH
    o3 = o2 + RWKV_IN
    o4 = o3 + S5_WIDTH
    xa, ga, zb, uc, zg = jnp.split(z, [o1, o2, o3, o4], axis=-1)
    y_a, fin_lru = mixer_rglru(xa, ga, lp['lru_conv_w'], lp['lru_conv_b'], lp['lru_wa'], lp['lru_ba'],
                               lp['lru_wx'], lp['lru_bx'], lp['lru_lam'], h_lru)
    y_b, fin_rwkv = mixer_rwkv7(zb, lp['rwkv_mu'], lp['rwkv_w0'], lp['rwkv_w2'], lp['rwkv_a0'], lp['rwkv_a2'],
                                lp['rwkv_g2'], lp['rwkv_kk'], lp['rwkv_ka'], lp['rwkv_rk'],
                                lp['rwkv_lnw'], lp['rwkv_lnb'], s_rwkv)
    if rows is not None:
        uc = to_col_major(uc, rows)
    y_c, fin_s5 = mixer_s5(uc, lp['s5_lam_re'], lp['s5_lam_im'], lp['s5_log_step'], lp['s5_b_re'],
                           lp['s5_b_im'], lp['s5_c_re'], lp['s5_c_im'], lp['s5_d'], lp['s5_w_glu'],
                           lp['s5_b_glu'], h_s5)
    if rows is not None:
        y_c = to_raster(y_c, rows)
    ys = jnp.stack([y_a, y_b, y_c], axis=2)
    return ys, zg, (fin_lru, fin_rwkv, fin_s5)


def merge_project(ys, zg, w_branch, w_out):
    b, l = ys.shape[:2]
    g = jax.nn.sigmoid(zg).reshape(b, l, N_BRANCH, D_MODEL)
    proj = jnp.einsum('blkw,kwd->blkd', ys, w_branch)
    return jnp.sum(g * proj, axis=2) @ w_out


def swiglu(xn, w_in, w_out):
    gate, up = jnp.split(xn @ w_in, 2, axis=-1)
    return (jax.nn.silu(gate) * up) @ w_out


def setup_inputs(seed: int = 0) -> dict:
    key = jax.random.key(seed)
    keys = iter(jax.random.split(key, 64))
    f32 = jnp.float32

    def nrm(shape, scale):
        return jax.random.normal(next(keys), shape, f32) * scale

    def gain(shape):
        return 1.0 + nrm(shape, 0.02)

    s = jax.random.uniform(next(keys), (DEPTH, 2, LRU_WIDTH), f32, 0.9, 0.999) ** (1.0 / LRU_C)
    lru_lam = jnp.log(s) - jnp.log1p(-s)
    rwkv_w0 = jnp.linspace(-6.0, -1.0, RWKV_WIDTH, dtype=f32) + nrm((DEPTH, 2, RWKV_WIDTH), 0.1)
    s5_shape = (DEPTH, 2, S5_GROUPS, S5_STATE)
    s5_lam_re = -0.5 + nrm(s5_shape, 0.01)
    s5_lam_im = math.pi * jnp.arange(S5_STATE, dtype=f32) + nrm(s5_shape, 0.01)
    s5_log_step = jax.random.uniform(next(keys), (DEPTH, 2, S5_GROUPS), f32, math.log(1e-3), math.log(1e-1))
    blk = (DEPTH, 2, LRU_HEADS, LRU_HEAD_DIM, LRU_HEAD_DIM)
    return {
        'x': nrm((BATCH, SEQ, D_MODEL), 1.0),
        'c': nrm((BATCH, D_MODEL), 1.0),
        'ctx': nrm((BATCH, CTX_LEN, D_MODEL), 1.0),
        'c_ctx': nrm((D_MODEL,), 1.0),
        'w_mod': nrm((DEPTH, D_MODEL, 6 * D_MODEL), 0.5 * D_MODEL ** -0.5),
        'b_mod': nrm((DEPTH, 6 * D_MODEL), 0.02),
        'norm1': gain((DEPTH, D_MODEL)),
        'norm2': gain((DEPTH, D_MODEL)),
        'norm_f': gain((D_MODEL,)),
        'w_in': nrm((DEPTH, D_MODEL, N_IN), D_MODEL ** -0.5),
        'lru_conv_w': nrm((DEPTH, CONV_W, LRU_WIDTH), CONV_W ** -0.5),
        'lru_conv_b': nrm((DEPTH, LRU_WIDTH), 0.02),
        'lru_wa': nrm(blk, LRU_HEAD_DIM ** -0.5),
        'lru_ba': nrm((DEPTH, 2, LRU_WIDTH), 0.02),
        'lru_wx': nrm(blk, LRU_HEAD_DIM ** -0.5),
        'lru_bx': nrm((DEPTH, 2, LRU_WIDTH), 0.02),
        'lru_lam': lru_lam,
        'rwkv_mu': jax.random.uniform(next(keys), (DEPTH, RWKV_IN), f32),
        'rwkv_w0': rwkv_w0,
        'rwkv_w2': nrm((DEPTH, 2, DECAY_LORA, RWKV_WIDTH), 0.1),
        'rwkv_a0': nrm((DEPTH, 2, RWKV_WIDTH), 0.1),
        'rwkv_a2': nrm((DEPTH, 2, ICLR_LORA, RWKV_WIDTH), 0.1),
        'rwkv_g2': nrm((DEPTH, GATE_LORA, RWKV_WIDTH), GATE_LORA ** -0.5),
        'rwkv_kk': 0.85 + nrm((DEPTH, RWKV_WIDTH), 0.02),
        'rwkv_ka': gain((DEPTH, RWKV_WIDTH)),
        'rwkv_rk': nrm((DEPTH, RWKV_HEADS, RWKV_HEAD_DIM), 0.1),
        'rwkv_lnw': gain((DEPTH, RWKV_WIDTH)),
        'rwkv_lnb': nrm((DEPTH, RWKV_WIDTH), 0.02),
        's5_lam_re': s5_lam_re,
        's5_lam_im': s5_lam_im,
        's5_log_step': s5_log_step,
        's5_b_re': nrm((DEPTH, 2, S5_GROUPS, S5_STATE, S5_GROUP), (2 * S5_GROUP) ** -0.5),
        's5_b_im': nrm((DEPTH, 2, S5_GROUPS, S5_STATE, S5_GROUP), (2 * S5_GROUP) ** -0.5),
        's5_c_re': nrm((DEPTH, 2, S5_GROUPS, S5_GROUP, S5_STATE), (2 * S5_STATE) ** -0.5),
        's5_c_im': nrm((DEPTH, 2, S5_GROUPS, S5_GROUP, S5_STATE), (2 * S5_STATE) ** -0.5),
        's5_d': nrm((DEPTH, S5_WIDTH), 1.0),
        's5_w_glu': nrm((DEPTH, S5_WIDTH, S5_WIDTH), S5_WIDTH ** -0.5),
        's5_b_glu': nrm((DEPTH, S5_WIDTH), 0.02),
        'w_branch': nrm((DEPTH, N_BRANCH, BRANCH_WIDTH, D_MODEL), BRANCH_WIDTH ** -0.5),
        'w_out': nrm((DEPTH, D_MODEL, D_MODEL), D_MODEL ** -0.5),
        'w_ffn_in': nrm((DEPTH, D_MODEL, 2 * FFN_HIDDEN), D_MODEL ** -0.5),
        'w_ffn_out': nrm((DEPTH, FFN_HIDDEN, D_MODEL), FFN_HIDDEN ** -0.5),
    }


def reference(x, c, ctx, c_ctx, w_mod, b_mod, norm1, norm2, norm_f, w_in,
              lru_conv_w, lru_conv_b, lru_wa, lru_ba, lru_wx, lru_bx, lru_lam,
              rwkv_mu, rwkv_w0, rwkv_w2, rwkv_a0, rwkv_a2, rwkv_g2, rwkv_kk, rwkv_ka, rwkv_rk,
              rwkv_lnw, rwkv_lnb,
              s5_lam_re, s5_lam_im, s5_log_step, s5_b_re, s5_b_im, s5_c_re, s5_c_im, s5_d,
              s5_w_glu, s5_b_glu,
              w_branch, w_out, w_ffn_in, w_ffn_out):
    f32 = jnp.float32
    bsz, n_tok, _ = x.shape
    rows = n_tok // GRID_W
    h_lat, h_ctx = x, ctx
    cond_lat = jax.nn.silu(c)
    cond_ctx = jax.nn.silu(c_ctx)
    zero_lru = jnp.zeros((2, bsz, LRU_WIDTH), f32)
    zero_rwkv = jnp.zeros((2, bsz, RWKV_HEADS, RWKV_HEAD_DIM, RWKV_HEAD_DIM), f32)
    zero_s5 = jnp.zeros((2, 2, bsz, S5_GROUPS, S5_STATE), f32)
    for i in range(DEPTH):
        last = i == DEPTH - 1
        m_lat = (cond_lat @ w_mod[i] + b_mod[i])[:, None, :]
        m_ctx = cond_ctx @ w_mod[i] + b_mod[i]
        sh1l, sc1l, gt1l, sh2l, sc2l, gt2l = jnp.split(m_lat, 6, axis=-1)
        sh1c, sc1c, gt1c, sh2c, sc2c, gt2c = jnp.split(m_ctx, 6, axis=-1)
        lp = dict(lru_conv_w=lru_conv_w[i], lru_conv_b=lru_conv_b[i], lru_wa=lru_wa[i], lru_ba=lru_ba[i],
                  lru_wx=lru_wx[i], lru_bx=lru_bx[i], lru_lam=lru_lam[i],
                  rwkv_mu=rwkv_mu[i], rwkv_w0=rwkv_w0[i], rwkv_w2=rwkv_w2[i], rwkv_a0=rwkv_a0[i],
                  rwkv_a2=rwkv_a2[i], rwkv_g2=rwkv_g2[i], rwkv_kk=rwkv_kk[i], rwkv_ka=rwkv_ka[i],
                  rwkv_rk=rwkv_rk[i], rwkv_lnw=rwkv_lnw[i], rwkv_lnb=rwkv_lnb[i],
                  s5_lam_re=s5_lam_re[i], s5_lam_im=s5_lam_im[i], s5_log_step=s5_log_step[i],
                  s5_b_re=s5_b_re[i], s5_b_im=s5_b_im[i], s5_c_re=s5_c_re[i], s5_c_im=s5_c_im[i],
                  s5_d=s5_d[i], s5_w_glu=s5_w_glu[i], s5_b_glu=s5_b_glu[i])
        z_ctx = modulate(rms_norm(h_ctx, norm1[i]), sh1c, sc1c) @ w_in[i]
        z_lat = modulate(rms_norm(h_lat, norm1[i]), sh1l, sc1l) @ w_in[i]
        ys_c, zg_c, (st_lru, st_rwkv, st_s5) = token_mixers(z_ctx, lp, zero_lru, zero_rwkv, zero_s5, None)
        ys_l, zg_l, _ = token_mixers(z_lat, lp, st_lru, st_rwkv, st_s5, rows)
        h_lat = h_lat + (gt1l * merge_project(ys_l, zg_l, w_branch[i], w_out[i])).astype(h_lat.dtype)
        h_lat = h_lat + (gt2l * swiglu(modulate(rms_norm(h_lat, norm2[i]), sh2l, sc2l),
                                       w_ffn_in[i], w_ffn_out[i])).astype(h_lat.dtype)
        if not last:
            h_ctx = h_ctx + (gt1c * merge_project(ys_c, zg_c, w_branch[i], w_out[i])).astype(h_ctx.dtype)
            h_ctx = h_ctx + (gt2c * swiglu(modulate(rms_norm(h_ctx, norm2[i]), sh2c, sc2c),
                                           w_ffn_in[i], w_ffn_out[i])).astype(h_ctx.dtype)
    return rms_norm(h_lat, norm_f)
```

```python
import math
import numpy as np
from contextlib import ExitStack
import concourse.bass as bass
import concourse.mybir as mybir
from concourse.bass_utils import run_bass_kernel_spmd

F32 = mybir.dt.float32
BF16 = mybir.dt.bfloat16
AF = mybir.ActivationFunctionType
ALU = mybir.AluOpType
AX = mybir.AxisListType

ENGS = ("tensor", "vector", "scalar", "gpsimd", "sync")
NDMASEM = 12


class Cell:
    __slots__ = ("w", "rs")

    def __init__(self):
        self.w = None
        self.rs = {}


class Tile:
    def __init__(self, ap, cells=None):
        self.ap = ap
        self.cells = cells if cells is not None else [Cell()]

    def __getitem__(self, key):
        return self.ap[key]

    def view(self, ap, cells=None):
        return Tile(ap, self.cells if cells is None else cells)


class Prog:
    def __init__(self):
        self.nc = bass.Bass("TRN2", target_bir_lowering=False)
        self.stack = ExitStack()
        self.q = {e: [] for e in ENGS}
        self.count = {e: 0 for e in ENGS}
        self.waited = {e: {} for e in ENGS}
        self.esem = {}
        for e in ENGS:
            self.esem[e] = self.stack.enter_context(self.nc.semaphore("s_" + e))
        self.dsem = {}
        self.dcount = {}
        self.drr = {}
        for e in ("sync", "gpsimd", "scalar"):
            self.dsem[e] = [self.stack.enter_context(self.nc.semaphore("d_%s_%d" % (e, i))) for i in range(NDMASEM)]
            self.dcount[e] = [0] * NDMASEM
            self.drr[e] = 0
        self.nt = 0
        self.ninst = 0

    def dram(self, name, shape, dt, kind):
        return self.nc.dram_tensor(name, list(shape), dt, kind=kind).ap()

    def sb(self, shape, dt=F32, name=None):
        self.nt += 1
        t = self.stack.enter_context(self.nc.sbuf_tensor(name or ("t%d" % self.nt), list(shape), dt))
        return Tile(t[:] if False else t)

    def ps(self, shape=(128, 512), dt=F32, name=None):
        self.nt += 1
        t = self.stack.enter_context(self.nc.psum_tensor(name or ("p%d" % self.nt), list(shape), dt))
        return Tile(t)

    def _need(self, eng, waits, ev):
        sem, val, src = ev
        if src == eng == "tensor":
            return
        key = id(sem)
        if self.waited[eng].get(key, 0) >= val:
            return
        cur = waits.get(key)
        if cur is None or cur[1] < val:
            waits[key] = (sem, val)

    def _deps(self, eng, reads, writes):
        waits = {}
        for t in reads:
            for c in t.cells:
                if c.w is not None:
                    self._need(eng, waits, c.w)
        for t in writes:
            for c in t.cells:
                if c.w is not None:
                    self._need(eng, waits, c.w)
                for r in c.rs.values():
                    self._need(eng, waits, r)
        for key, (sem, val) in waits.items():
            self.waited[eng][key] = val
        return list(waits.values())

    def _mark(self, ev, reads, writes):
        wset = set()
        for t in writes:
            for c in t.cells:
                c.w = ev
                c.rs = {}
                wset.add(id(c))
        for t in reads:
            for c in t.cells:
                if id(c) in wset:
                    continue
                c.rs[id(ev[0])] = ev

    def op(self, eng, fn, reads=(), writes=()):
        waits = self._deps(eng, reads, writes)
        self.count[eng] += 1
        ev = (self.esem[eng], self.count[eng], eng)
        self._mark(ev, reads, writes)
        self.q[eng].append((waits, fn, (self.esem[eng], 1)))
        self.ninst += 1

    def dma(self, eng, out, in_, reads=(), writes=(), **kw):
        waits = self._deps(eng, reads, writes)
        i = self.drr[eng]
        self.drr[eng] = (i + 1) % NDMASEM
        sem = self.dsem[eng][i]
        prev = self.dcount[eng][i]
        if prev > 0 and self.waited[eng].get(id(sem), 0) < prev:
            waits.append((sem, prev))
            self.waited[eng][id(sem)] = prev
        self.dcount[eng][i] = prev + 16
        ev = (sem, prev + 16, "dma")
        self._mark(ev, reads, writes)
        self.q[eng].append((waits, (lambda e, o=out, i_=in_, k=kw: e.dma_start(out=o, in_=i_, **k)), (sem, 16)))
        self.ninst += 1

    def barrier(self):
        evs = [(self.esem[e], self.count[e]) for e in ENGS if self.count[e] > 0]
        for e in self.dsem:
            for sm, c in zip(self.dsem[e], self.dcount[e]):
                if c > 0:
                    evs.append((sm, c))
        for eng in ENGS:
            waits = []
            for sm, val in evs:
                if sm is self.esem[eng]:
                    continue
                if self.waited[eng].get(id(sm), 0) < val:
                    waits.append((sm, val))
                    self.waited[eng][id(sm)] = val
            if waits:
                self.q[eng].append((waits, None, None))

    def finish(self):
        fin = []
        for e in self.dsem:
            for s, c in zip(self.dsem[e], self.dcount[e]):
                if c > 0:
                    fin.append((s, c))
        nc = self.nc
        q = self.q
        with nc.Block() as block:
            def emit(engname, eng):
                for waits, fn, inc in q[engname]:
                    for sem, val in waits:
                        eng.wait_ge(sem, val)
                    if fn is None:
                        continue
                    ins = fn(eng)
                    ins.then_inc(inc[0], inc[1])
                if engname == "sync":
                    for s, c in fin:
                        eng.wait_ge(s, c)

            @block.tensor
            def _(e):
                emit("tensor", e)

            @block.vector
            def _(e):
                emit("vector", e)

            @block.scalar
            def _(e):
                emit("scalar", e)

            @block.gpsimd
            def _(e):
                emit("gpsimd", e)

            @block.sync
            def _(e):
                emit("sync", e)
        self.stack.close()
        return nc


def rev_ap(ap):
    aps = [list(x) for x in ap.ap]
    step, cnt = aps[-1]
    off = ap.offset + step * (cnt - 1)
    aps[-1] = [-step, cnt]
    return bass.AP(ap.tensor, off, aps)


NTOK = 2080
CH = [(0, 32)] + [(32 + 512 * i, 512) for i in range(4)]
D = 1024
KT = 8
N_IN = 6528
EPS = 1e-6


def mod_vectors(P, wmod_d, bmod_sb, cond, ncolblk, wbuf, psum, out_tile):
    G = 4
    for g0 in range(0, ncolblk, G):
        wb = wbuf[(g0 // G) % len(wbuf)]
        P.dma("sync", wb[:, :, 0:G * 128],
              wmod_d[:, g0 * 128:(g0 + G) * 128].rearrange("(kt p) c -> p kt c", p=128), writes=[wb])
        for j in range(g0, g0 + G):
            for kt in range(KT):
                P.op("tensor", (lambda e, j=j, kt=kt, wb=wb, g0=g0: e.matmul(
                    psum[:, 2 * j:2 * j + 2], lhsT=wb[:, kt, (j - g0) * 128:(j - g0 + 1) * 128],
                    rhs=cond[:, kt, :], start=(kt == 0), stop=(kt == KT - 1))),
                    reads=[wb, cond], writes=[psum])
    for which in range(2):
        P.op("vector", (lambda e, which=which: e.tensor_tensor(
            out_tile[:, 0:ncolblk, which], psum[:, which:2 * ncolblk:2], bmod_sb[:, 0:ncolblk], ALU.add)),
            reads=[psum, bmod_sb], writes=[out_tile])


def rms_modulate(P, x, ones, psb, sq, rstd, gsc, shf, epst):
    for ci, (o, n) in enumerate(CH):
        which = 1 if ci == 0 else 0
        ps = psb[ci % len(psb)]
        for kt in range(KT):
            s = sq[kt % len(sq)]
            P.op("scalar", (lambda e, s=s, kt=kt, o=o, n=n: e.activation(out=s[:, 0:n], in_=x[:, kt, o:o + n], func=AF.Square)),
                 reads=[x], writes=[s])
            P.op("tensor", (lambda e, s=s, kt=kt, n=n, ps=ps: e.matmul(ps[:, 0:n], lhsT=ones[:, :], rhs=s[:, 0:n],
                                                                    start=(kt == 0), stop=(kt == KT - 1))),
                 reads=[s, ones], writes=[ps])
        r = rstd[ci % len(rstd)]
        P.op("scalar", (lambda e, r=r, ps=ps, n=n: e.activation(out=r[:, 0:n], in_=ps[:, 0:n], func=AF.Sqrt, bias=epst[:, 0:1], scale=1.0 / D)),
             reads=[ps, epst], writes=[r])
        P.op("vector", (lambda e, r=r, n=n: e.reciprocal(r[:, 0:n], r[:, 0:n])),
             reads=[r], writes=[r])
        for kt in range(KT):
            eng = "vector" if kt % 2 == 0 else "gpsimd"
            P.op(eng, (lambda e, kt=kt, o=o, n=n, r=r: e.tensor_tensor(x[:, kt, o:o + n], x[:, kt, o:o + n], r[:, 0:n], ALU.mult)),
                 reads=[x, r], writes=[x])
            P.op(eng, (lambda e, kt=kt, o=o, n=n, which=which: e.tensor_scalar(
                x[:, kt, o:o + n], x[:, kt, o:o + n], gsc[:, kt, which:which + 1], shf[:, kt, which:which + 1], ALU.mult, ALU.add)),
                reads=[x, gsc, shf], writes=[x])


def build_A():
    P = Prog()
    xT = P.dram("xT", [D, NTOK], F32, "ExternalInput")
    cnd = P.dram("cnd", [128, KT, 2], F32, "ExternalInput")
    wmod = P.dram("wmod", [D, 2048], F32, "ExternalInput")
    bmod = P.dram("bmod", [128, 16], F32, "ExternalInput")
    nrm = P.dram("nrm", [128, KT], F32, "ExternalInput")
    win = P.dram("win", [D, N_IN], F32, "ExternalInput")
    zT = P.dram("zT", [N_IN, NTOK], F32, "ExternalOutput")

    x = P.sb([128, KT, NTOK])
    cond = P.sb([128, KT, 2])
    bmod_sb = P.sb([128, 16])
    nrm_sb = P.sb([128, KT])
    ones = P.sb([128, 128])
    modv = P.sb([128, 16, 2])
    gsc = P.sb([128, KT, 2])
    wbuf = [P.sb([128, KT, 512]) for _ in range(2)]
    sq = [P.sb([128, 512]) for _ in range(2)]
    rstd = [P.sb([128, 512]) for _ in range(2)]
    zo = [P.sb([128, NTOK]) for _ in range(2)]
    psb = [P.ps() for _ in range(6)]
    psm = P.ps()

    P.dma("gpsimd", x[:, :, :], xT.rearrange("(kt p) t -> p kt t", p=128), writes=[x])
    P.dma("sync", cond[:], cnd, writes=[cond])
    P.dma("sync", bmod_sb[:], bmod, writes=[bmod_sb])
    P.dma("sync", nrm_sb[:], nrm, writes=[nrm_sb])
    P.op("vector", lambda e: e.memset(ones[:], 1.0), writes=[ones])
    P.op("scalar", lambda e: e.activation(out=cond[:], in_=cond[:], func=AF.Silu), reads=[cond], writes=[cond])
    mod_vectors(P, wmod, bmod_sb, cond, 16, wbuf, psm, modv)
    for which in range(2):
        P.op("vector", (lambda e, which=which: e.scalar_tensor_tensor(
            gsc[:, :, which], modv[:, 8:16, which], 1.0, nrm_sb[:, :], ALU.add, ALU.mult)),
            reads=[modv, nrm_sb], writes=[gsc])
    shf = modv.view(modv.ap)
    epst = P.sb([128, 1])
    P.op('vector', lambda e: e.memset(epst[:], EPS), writes=[epst])
    rms_modulate(P, x, ones, psb[0:2], sq, rstd, gsc, shf, epst)

    CB = 384
    nblk = N_IN // CB
    pi = 0
    for b in range(nblk):
        wb = wbuf[b % 2]
        P.dma("sync", wb[:, :, 0:CB], win[:, b * CB:(b + 1) * CB].rearrange("(kt p) c -> p kt c", p=128), writes=[wb])
        for sblk in range(CB // 128):
            col0 = b * CB + sblk * 128
            z = zo[(b * 3 + sblk) % 2]
            for ci, (o, n) in enumerate(CH):
                ps = psb[pi % len(psb)]
                pi += 1
                for kt in range(KT):
                    P.op("tensor", (lambda e, ps=ps, wb=wb, kt=kt, sblk=sblk, o=o, n=n: e.matmul(
                        ps[:, 0:n], lhsT=wb[:, kt, sblk * 128:(sblk + 1) * 128], rhs=x[:, kt, o:o + n],
                        start=(kt == 0), stop=(kt == KT - 1))), reads=[wb, x], writes=[ps])
                if ci % 2 == 0:
                    P.op("scalar", (lambda e, z=z, ps=ps, o=o, n=n: e.activation(out=z[:, o:o + n], in_=ps[:, 0:n], func=AF.Copy)),
                         reads=[ps], writes=[z])
                else:
                    P.op("vector", (lambda e, z=z, ps=ps, o=o, n=n: e.tensor_copy(z[:, o:o + n], ps[:, 0:n])),
                         reads=[ps], writes=[z])
            P.dma("gpsimd", zT[col0:col0 + 128, :], z[:, :], reads=[z])
    return P.finish()


T_ALL = 16640
SEQS = [(0, 256), (256, 16640)]


def blocks(tb):
    out = []
    for (s0, s1) in SEQS:
        t = s0
        while t < s1:
            n = min(tb, s1 - t)
            out.append((t, n, s0, s1))
            t += n
    return out


def build_B1():
    P = Prog()
    TB = 2048
    xs = P.dram("xs", [2, 64, T_ALL], F32, "ExternalInput")
    cw5 = P.dram("cw5", [128, 5], F32, "ExternalInput")
    vecs = P.dram("vecs", [128, 4], F32, "ExternalInput")
    wg = P.dram("wg", [128, 2, 64], F32, "ExternalInput")
    hs = P.dram("hs", [2, 64, T_ALL], F32, "ExternalOutput")

    cw = P.sb([128, 5]); vc = P.sb([128, 4]); wgs = P.sb([128, 2, 64])
    cl = P.sb([128, 2])
    xp = [P.sb([128, TB + 4]) for _ in range(2)]
    xc = P.sb([128, TB]); gr = P.sb([128, TB]); gi = P.sb([128, TB]); aa = P.sb([128, TB]); uu = P.sb([128, TB])
    hb = [P.sb([128, TB]) for _ in range(2)]
    h0 = P.sb([128, 1])
    psa = [P.ps() for _ in range(2)]
    psx = [P.ps() for _ in range(2)]

    P.dma("sync", cw[:], cw5, writes=[cw])
    P.dma("sync", vc[:], vecs, writes=[vc])
    P.dma("sync", wgs[:], wg, writes=[wgs])
    P.op("vector", lambda e: e.memset(h0[:], 0.0), writes=[h0])
    P.op("scalar", lambda e: e.activation(out=cl[:, 0:1], in_=vc[:, 3:4], func=AF.Exp, scale=-1.0), reads=[vc], writes=[cl])
    P.op("vector", lambda e: e.tensor_scalar(cl[:, 0:1], cl[:, 0:1], 1.0, None, ALU.add), reads=[cl], writes=[cl])
    P.op("scalar", lambda e: e.activation(out=cl[:, 0:1], in_=cl[:, 0:1], func=AF.Ln), reads=[cl], writes=[cl])
    P.op("vector", lambda e: e.tensor_scalar(cl[:, 1:2], cl[:, 0:1], -16.0, None, ALU.mult), reads=[cl], writes=[cl])
    P.op("vector", lambda e: e.tensor_scalar(cl[:, 0:1], cl[:, 0:1], -8.0, None, ALU.mult), reads=[cl], writes=[cl])

    prev_h = None
    for bi, (t0, n, s0, s1) in enumerate(blocks(TB)):
        x = xp[bi % 2]
        h = hb[bi % 2]
        lo = max(t0 - 2, s0); hi = min(t0 + n + 2, s1)
        P.op("gpsimd", lambda e, x=x: e.memset(x[:, 0:2], 0.0), writes=[x])
        P.op("gpsimd", lambda e, x=x, n=n: e.memset(x[:, n + 2:n + 4], 0.0), writes=[x])
        for s in range(2):
            P.dma("sync", x[64 * s:64 * s + 64, 2 + (lo - t0):2 + (hi - t0)], xs[s, :, lo:hi], writes=[x])
        P.op("vector", lambda e, x=x, n=n: e.tensor_scalar(xc[:, 0:n], x[:, 0:n], cw[:, 0:1], vc[:, 0:1], ALU.mult, ALU.add),
             reads=[x, cw, vc], writes=[xc])
        for j in range(1, 5):
            P.op("vector", lambda e, x=x, n=n, j=j: e.scalar_tensor_tensor(xc[:, 0:n], x[:, j:j + n], cw[:, j:j + 1], xc[:, 0:n], ALU.mult, ALU.add),
                 reads=[x, cw, xc], writes=[xc])
        for ci, c0 in enumerate(range(0, n, 512)):
            m = min(512, n - c0)
            pa = psa[ci % 2]; px = psx[ci % 2]
            for s in range(2):
                sl = slice(64 * s, 64 * s + 64)
                P.op("tensor", lambda e, pa=pa, sl=sl, c0=c0, m=m: e.matmul(pa[sl, 0:m], lhsT=wgs[sl, 0, :], rhs=xc[sl, c0:c0 + m], start=True, stop=True),
                     reads=[wgs, xc], writes=[pa])
                P.op("tensor", lambda e, px=px, sl=sl, c0=c0, m=m: e.matmul(px[sl, 0:m], lhsT=wgs[sl, 1, :], rhs=xc[sl, c0:c0 + m], start=True, stop=True),
                     reads=[wgs, xc], writes=[px])
            P.op("scalar", lambda e, pa=pa, c0=c0, m=m: e.activation(out=gr[:, c0:c0 + m], in_=pa[:, 0:m], func=AF.Sigmoid, bias=vc[:, 1:2]),
                 reads=[pa, vc], writes=[gr])
            P.op("scalar", lambda e, px=px, c0=c0, m=m: e.activation(out=gi[:, c0:c0 + m], in_=px[:, 0:m], func=AF.Sigmoid, bias=vc[:, 2:3]),
                 reads=[px, vc], writes=[gi])
        P.op("scalar", lambda e, n=n: e.activation(out=aa[:, 0:n], in_=gr[:, 0:n], func=AF.Exp, scale=cl[:, 0:1]), reads=[gr, cl], writes=[aa])
        P.op("scalar", lambda e, n=n: e.activation(out=gr[:, 0:n], in_=gr[:, 0:n], func=AF.Exp, scale=cl[:, 1:2]), reads=[gr, cl], writes=[gr])
        P.op("gpsimd", lambda e, n=n: e.tensor_tensor(uu[:, 0:n], gi[:, 0:n], xc[:, 0:n], ALU.mult), reads=[gi, xc], writes=[uu])
        P.op("vector", lambda e, n=n: e.tensor_scalar(gr[:, 0:n], gr[:, 0:n], -1.0, 1.0, ALU.mult, ALU.add), reads=[gr], writes=[gr])
        P.op("scalar", lambda e, n=n: e.activation(out=gr[:, 0:n], in_=gr[:, 0:n], func=AF.Sqrt), reads=[gr], writes=[gr])
        P.op("vector", lambda e, n=n: e.tensor_tensor(uu[:, 0:n], uu[:, 0:n], gr[:, 0:n], ALU.mult), reads=[uu, gr], writes=[uu])
        init = h0 if prev_h is None else prev_h[0]
        init_ap = h0[:, 0:1] if prev_h is None else prev_h[0][:, prev_h[1] - 1:prev_h[1]]
        P.op("vector", lambda e, h=h, n=n, init_ap=init_ap: e.tensor_tensor_scan(h[:, 0:n], aa[:, 0:n], uu[:, 0:n], init_ap, ALU.mult, ALU.add),
             reads=[aa, uu, init], writes=[h])
        prev_h = (h, n)
        for s in range(2):
            P.dma("gpsimd", hs[s, :, t0:t0 + n], h[64 * s:64 * s + 64, 0:n], reads=[h])
    return P.finish()


C0 = 0.6065306597126334
CL = 64
GN_EPS = 64e-5


def build_B2(nblk=99, phase2=True, nchunk=99, ndbl=5, stage=9, sub=3):
    P = Prog()
    TB = 1024
    zs = P.dram("zs", [2, 448, T_ALL], F32, "ExternalInput")
    mu5 = P.dram("mu5", [128, 5], F32, "ExternalInput")
    mug = P.dram("mug", [128, 1], F32, "ExternalInput")
    w2a2 = P.dram("w2a2", [128, 2, 64], F32, "ExternalInput")
    vecs = P.dram("vecs", [128, 5], F32, "ExternalInput")
    g2h = P.dram("g2h", [128, 64], F32, "ExternalInput")
    lnwb = P.dram("lnwb", [64, 2], F32, "ExternalInput")
    mask_d = P.dram("mask", [128, 320], F32, "ExternalInput")
    ident_d = P.dram("ident", [128, 128], F32, "ExternalInput")
    cmask_d = P.dram("cmask", [128, TB], F32, "ExternalInput")
    ident2_d = P.dram("ident2", [128, 64], F32, "ExternalInput")
    ys_d = P.dram("ys_scr", [2, 64, T_ALL], F32, "Internal")
    bon_d = P.dram("bon_scr", [2, 64, T_ALL], F32, "Internal")
    gg_d = P.dram("gg_scr", [64, T_ALL], F32, "Internal")
    yb_d = P.dram("yb", [64, T_ALL], F32, "ExternalOutput")

    mu = P.sb([128, 5]); hmu = P.sb([128, 5]); omm = P.sb([128, 5])
    mg = P.sb([128, 1]); hmg = P.sb([128, 1]); omg = P.sb([128, 1])
    wl = P.sb([128, 2, 64]); vc = P.sb([128, 5]); omka = P.sb([128, 1]); g2 = P.sb([128, 64]); lnw = P.sb([64, 2])
    mask = P.sb([128, 320]); ident = P.sb([128, 128]); cmask = P.sb([128, TB]); ident2 = P.sb([128, 64])
    ones = P.sb([128, 64]); rkm = P.sb([128, 64]); e12 = P.sb([128, 1]); egn = P.sb([128, 1]); o64 = P.sb([128, 64])
    IN = P.sb([128, 5, TB + 2]); GD = P.sb([128, TB + 2])
    SH = P.sb([128, 5, TB]); TW = P.sb([128, TB]); SG = P.sb([128, TB]); IC = P.sb([128, TB])
    KKr = P.sb([128, TB]); TMP = P.sb([128, TB]); RN = P.sb([128, TB]); KKN = P.sb([128, TB]); KD = P.sb([128, TB])
    CS = P.sb([128, TB]); EW = P.sb([128, TB]); EWI = P.sb([128, TB]); EWM = P.sb([128, TB])
    OPS = P.sb([128, (TB // CL) * 4 * CL])
    SGD = P.sb([128, TB]); G = P.sb([64, TB]); BV = P.sb([128, TB]); YO = [P.sb([128, TB]) for _ in range(2)]
    Mall = [P.sb([128, 320]) for _ in range(2)]
    TM = [P.sb([128, 192]) for _ in range(2)]
    PP = [P.sb([128, 128]) for _ in range(2)]
    TT_ = [P.sb([128, 64]) for _ in range(2)]
    XnT = P.sb([128, 64]); SAnT = P.sb([128, 64])
    S0T = [P.sb([128, 64]) for _ in range(2)]
    bank = [P.ps() for _ in range(8)]

    PM = bank[0]
    PD = bank[1]; PT = bank[2]
    PX = PS = bank[3]
    PST = bank[4]
    PY = [bank[5], bank[5]]
    PW = bank[6]; PA = bank[7]
    PTR = PW

    for (t, d) in ((mu, mu5), (mg, mug), (wl, w2a2), (vc, vecs), (g2, g2h), (lnw, lnwb), (mask, mask_d), (ident, ident_d), (cmask, cmask_d), (ident2, ident2_d)):
        P.dma("sync", t[:], d, writes=[t])
    V = "vector"
    P.op(V, lambda e: e.tensor_scalar(hmu[:], mu[:], 0.5, None, ALU.mult), reads=[mu], writes=[hmu])
    P.op(V, lambda e: e.tensor_scalar(omm[:], mu[:], -1.0, 1.0, ALU.mult, ALU.add), reads=[mu], writes=[omm])
    P.op(V, lambda e: e.tensor_scalar(hmg[:], mg[:], 0.5, None, ALU.mult), reads=[mg], writes=[hmg])
    P.op(V, lambda e: e.tensor_scalar(omg[:], mg[:], -1.0, 1.0, ALU.mult, ALU.add), reads=[mg], writes=[omg])
    P.op(V, lambda e: e.tensor_scalar(omka[:], vc[:, 3:4], -1.0, 1.0, ALU.mult, ALU.add), reads=[vc], writes=[omka])
    P.op(V, lambda e: e.memset(ones[:], 1.0), writes=[ones])
    P.op(V, lambda e: e.memset(o64[:], 1.0 / 64), writes=[o64])
    P.op(V, lambda e: e.memset(e12[:], 1e-12), writes=[e12])
    P.op(V, lambda e: e.memset(egn[:], GN_EPS), writes=[egn])
    P.op(V, lambda e: e.tensor_scalar(rkm[:], ones[:], vc[:, 4:5], None, ALU.mult), reads=[ones, vc], writes=[rkm])
    P.op(V, lambda e: e.memset(S0T[0][:], 0.0), writes=[S0T[0]])

    sidx = 0
    cidx = 0
    for bi, (t0, n, s0, s1) in enumerate(blocks(TB)[:nblk]):
        nch = n // CL
        lo = max(t0 - 1, s0); hi = min(t0 + n + 1, s1)
        P.op("gpsimd", lambda e: e.memset(IN[:, :, 0:1], 0.0), writes=[IN])
        P.op("gpsimd", lambda e, n=n: e.memset(IN[:, :, n + 1:n + 2], 0.0), writes=[IN])
        P.op("gpsimd", lambda e: e.memset(GD[:, 0:1], 0.0), writes=[GD])
        P.op("gpsimd", lambda e, n=n: e.memset(GD[:, n + 1:n + 2], 0.0), writes=[GD])
        for s in range(2):
            P.dma("sync", IN[64 * s:64 * s + 64, :, 1 + (lo - t0):1 + (hi - t0)],
                  zs[s, 0:320, lo:hi].rearrange("(a p) t -> p a t", p=64), writes=[IN])
        P.dma("sync", GD[:, 1 + (lo - t0):1 + (hi - t0)], zs[0, 320:448, lo:hi], writes=[GD])
        for a in range(5):
            eng = "gpsimd" if a % 2 else "vector"
            P.op(eng, lambda e, a=a, n=n: e.tensor_tensor(SH[:, a, 0:n], IN[:, a, 0:n], IN[:, a, 2:n + 2], ALU.add), reads=[IN], writes=[SH])
            P.op(eng, lambda e, a=a, n=n: e.tensor_scalar(SH[:, a, 0:n], SH[:, a, 0:n], hmu[:, a:a + 1], None, ALU.mult), reads=[SH, hmu], writes=[SH])
            P.op("vector", lambda e, a=a, n=n: e.scalar_tensor_tensor(SH[:, a, 0:n], IN[:, a, 1:n + 1], omm[:, a:a + 1], SH[:, a, 0:n], ALU.mult, ALU.add),
                 reads=[IN, omm, SH], writes=[SH])
        P.op("gpsimd", lambda e, n=n: e.tensor_tensor(SGD[:, 0:n], GD[:, 0:n], GD[:, 2:n + 2], ALU.add), reads=[GD], writes=[SGD])
        P.op("gpsimd", lambda e, n=n: e.tensor_scalar(SGD[:, 0:n], SGD[:, 0:n], hmg[:, 0:1], None, ALU.mult), reads=[SGD, hmg], writes=[SGD])
        P.op("vector", lambda e, n=n: e.scalar_tensor_tensor(SGD[:, 0:n], GD[:, 1:n + 1], omg[:, 0:1], SGD[:, 0:n], ALU.mult, ALU.add),
             reads=[GD, omg, SGD], writes=[SGD])
        P.op("scalar", lambda e, n=n: e.activation(out=SGD[:, 0:n], in_=SGD[:, 0:n], func=AF.Sigmoid), reads=[SGD], writes=[SGD])
        P.op("scalar", lambda e, n=n: e.activation(out=TW[:, 0:n], in_=SH[:, 3, 0:n], func=AF.Tanh), reads=[SH], writes=[TW])
        P.op("vector", lambda e, n=n: e.tensor_scalar(KKr[:, 0:n], SH[:, 1, 0:n], vc[:, 2:3], None, ALU.mult), reads=[SH, vc], writes=[KKr])
        P.op("gpsimd", lambda e, n=n: e.tensor_tensor(TMP[:, 0:n], KKr[:, 0:n], KKr[:, 0:n], ALU.mult), reads=[KKr], writes=[TMP])
        for c0 in range(0, n, 512):
            m = min(512, n - c0)
            for s in range(2):
                sl = slice(64 * s, 64 * s + 64)
                P.op("tensor", lambda e, sl=sl, c0=c0, m=m: e.matmul(PW[sl, 0:m], lhsT=wl[sl, 0, :], rhs=TW[sl, c0:c0 + m], start=True, stop=True),
                     reads=[wl, TW], writes=[PW])
            P.op("scalar", lambda e, c0=c0, m=m: e.activation(out=SG[:, c0:c0 + m], in_=PW[:, 0:m], func=AF.Sigmoid, bias=vc[:, 0:1]),
                 reads=[PW, vc], writes=[SG])
            for s in range(2):
                sl = slice(64 * s, 64 * s + 64)
                P.op("tensor", lambda e, sl=sl, c0=c0, m=m: e.matmul(PA[sl, 0:m], lhsT=wl[sl, 1, :], rhs=SH[sl, 4, c0:c0 + m], start=True, stop=True),
                     reads=[wl, SH], writes=[PA])
            P.op("scalar", lambda e, c0=c0, m=m: e.activation(out=IC[:, c0:c0 + m], in_=PA[:, 0:m], func=AF.Sigmoid, bias=vc[:, 1:2]),
                 reads=[PA, vc], writes=[IC])
            for s in range(2):
                sl = slice(64 * s, 64 * s + 64)
                P.op("tensor", lambda e, sl=sl, c0=c0, m=m: e.matmul(PW[sl, 0:m], lhsT=ones[sl, :], rhs=TMP[sl, c0:c0 + m], start=True, stop=True),
                     reads=[ones, TMP], writes=[PW])
            P.op("scalar", lambda e, c0=c0, m=m: e.activation(out=RN[:, c0:c0 + m], in_=PW[:, 0:m], func=AF.Sqrt, bias=e12[:, 0:1]),
                 reads=[PW, e12], writes=[RN])
            P.op("tensor", lambda e, c0=c0, m=m: e.matmul(PA[0:64, 0:m], lhsT=g2[:, :], rhs=SGD[:, c0:c0 + m], start=True, stop=True),
                 reads=[g2, SGD], writes=[PA])
            P.op("scalar", lambda e, c0=c0, m=m: e.activation(out=G[:, c0:c0 + m], in_=PA[0:64, 0:m], func=AF.Copy), reads=[PA], writes=[G])
        P.dma("gpsimd", gg_d[:, t0:t0 + n], G[:, 0:n], reads=[G])
        P.op("vector", lambda e, n=n: e.reciprocal(RN[:, 0:n], RN[:, 0:n]), reads=[RN], writes=[RN])
        P.op("vector", lambda e, n=n: e.tensor_tensor(KKN[:, 0:n], KKr[:, 0:n], RN[:, 0:n], ALU.mult), reads=[KKr, RN], writes=[KKN])
        P.op("vector", lambda e, n=n: e.tensor_scalar(KD[:, 0:n], IC[:, 0:n], vc[:, 3:4], omka[:, 0:1], ALU.mult, ALU.add), reads=[IC, vc, omka], writes=[KD])
        P.op("gpsimd", lambda e, n=n: e.tensor_tensor(KD[:, 0:n], KD[:, 0:n], SH[:, 1, 0:n], ALU.mult), reads=[KD, SH], writes=[KD])
        P.op("gpsimd", lambda e, n=n: e.tensor_tensor(TMP[:, 0:n], SH[:, 0, 0:n], KD[:, 0:n], ALU.mult), reads=[SH, KD], writes=[TMP])
        for c0 in range(0, n, 512):
            m = min(512, n - c0)
            for s in range(2):
                sl = slice(64 * s, 64 * s + 64)
                P.op("tensor", lambda e, sl=sl, c0=c0, m=m: e.matmul(PW[sl, 0:m], lhsT=rkm[sl, :], rhs=TMP[sl, c0:c0 + m], start=True, stop=True),
                     reads=[rkm, TMP], writes=[PW])
            P.op("vector", lambda e, c0=c0, m=m: e.tensor_tensor(BV[:, c0:c0 + m], PW[:, 0:m], SH[:, 2, c0:c0 + m], ALU.mult), reads=[PW, SH], writes=[BV])
        for s in range(2):
            P.dma("gpsimd", bon_d[s, :, t0:t0 + n], BV[64 * s:64 * s + 64, 0:n], reads=[BV])
        P.op("vector", lambda e, n=n: e.tensor_tensor_scan(CS[:, 0:n], cmask[:, 0:n], SG[:, 0:n], 0.0, ALU.mult, ALU.add), reads=[cmask, SG], writes=[CS])
        P.op("scalar", lambda e, n=n: e.activation(out=EW[:, 0:n], in_=CS[:, 0:n], func=AF.Exp, scale=-C0), reads=[CS], writes=[EW])
        P.op("scalar", lambda e, n=n: e.activation(out=EWI[:, 0:n], in_=CS[:, 0:n], func=AF.Exp, scale=C0), reads=[CS], writes=[EWI])
        P.op("vector", lambda e, n=n: e.tensor_tensor(EWM[:, 0:n], CS[:, 0:n], SG[:, 0:n], ALU.subtract), reads=[CS, SG], writes=[EWM])
        P.op("scalar", lambda e, n=n: e.activation(out=EWM[:, 0:n], in_=EWM[:, 0:n], func=AF.Exp, scale=-C0), reads=[EWM], writes=[EWM])
        ch3 = lambda t, n=n: t[:, 0:n].rearrange("p (c t) -> p c t", t=CL)
        P.op("vector", lambda e, nch=nch, ch3=ch3: e.tensor_tensor(OPS[:, 0:nch * 256].rearrange("p (c a t) -> p c a t", a=4, t=CL)[:, :, 0, :], ch3(KKN), ch3(EWM), ALU.mult), reads=[KKN, EWM], writes=[OPS])
        P.op("gpsimd", lambda e, nch=nch, n=n, ch3=ch3: e.tensor_tensor(OPS[:, 0:nch * 256].rearrange("p (c a t) -> p c a t", a=4, t=CL)[:, :, 1, :], SH[:, 0, 0:n].rearrange("p (c t) -> p c t", t=CL), ch3(EW), ALU.mult),
             reads=[SH, EW], writes=[OPS])
        P.op("vector", lambda e, nch=nch, ch3=ch3: e.tensor_tensor(OPS[:, 0:nch * 256].rearrange("p (c a t) -> p c a t", a=4, t=CL)[:, :, 2, :], ch3(KD), ch3(EWI), ALU.mult), reads=[KD, EWI], writes=[OPS])
        P.op("gpsimd", lambda e, n=n: e.tensor_tensor(TMP[:, 0:n], KKN[:, 0:n], IC[:, 0:n], ALU.mult), reads=[KKN, IC], writes=[TMP])
        P.op("vector", lambda e, nch=nch, ch3=ch3: e.tensor_tensor(OPS[:, 0:nch * 256].rearrange("p (c a t) -> p c a t", a=4, t=CL)[:, :, 3, :], ch3(TMP), ch3(EWI), ALU.mult), reads=[TMP, EWI], writes=[OPS])
        Y = YO[bi % 2]
        for c in range(min(nch, nchunk)):
            M = Mall[cidx % 2]; tm = TM[cidx % 2]
            cidx += 1
            for s in range(2):
                sl = slice(64 * s, 64 * s + 64)
                if sub & 1:
                    P.op("tensor", lambda e, sl=sl, c=c: e.matmul(PM[sl, 0:128], lhsT=OPS[sl, c * 256 + 128:c * 256 + 192], rhs=OPS[sl, c * 256:c * 256 + 128], start=True, stop=True),
                         reads=[OPS], writes=[PM])
                if sub & 1:
                    P.op("tensor", lambda e, sl=sl, c=c: e.matmul(PM[sl, 128:256], lhsT=OPS[sl, c * 256 + 192:c * 256 + 256], rhs=OPS[sl, c * 256:c * 256 + 128], start=True, stop=True),
                         reads=[OPS], writes=[PM])
                if sub & 1:
                    P.op("tensor", lambda e, sl=sl, c=c: e.matmul(PM[sl, 256:320], lhsT=OPS[sl, c * 256 + 0:c * 256 + 64], rhs=OPS[sl, c * 256 + 192:c * 256 + 256], start=True, stop=True),
                         reads=[OPS], writes=[PM])
                if sub & 2:
                    P.op("tensor", lambda e, sl=sl, c=c: e.matmul(PTR[sl, 320:384], lhsT=OPS[sl, c * 256 + 128:c * 256 + 192], rhs=ident[sl, sl], start=True, stop=True),
                         reads=[OPS, ident], writes=[PTR])
                if sub & 2:
                    P.op("tensor", lambda e, sl=sl, c=c: e.matmul(PTR[sl, 384:448], lhsT=OPS[sl, c * 256 + 192:c * 256 + 256], rhs=ident[sl, sl], start=True, stop=True),
                         reads=[OPS, ident], writes=[PTR])
                if sub & 2:
                    P.op("tensor", lambda e, sl=sl, c=c: e.matmul(PTR[sl, 448:512], lhsT=SH[sl, 2, c * CL:(c + 1) * CL], rhs=ident[sl, sl], start=True, stop=True),
                         reads=[SH, ident], writes=[PTR])
            if sub & 1:
                P.op("vector", lambda e, M=M: e.tensor_tensor(M[:, :], PM[:, 0:320], mask[:, :], ALU.mult), reads=[PM, mask], writes=[M])
            if sub & 2:
                P.op("scalar", lambda e, tm=tm: e.activation(out=tm[:, :], in_=PTR[:, 320:512], func=AF.Copy), reads=[PTR], writes=[tm])
            if stage < 2:
                continue
            pp = PP[0]; tt = TT_[0]
            P.op("gpsimd", lambda e, M=M, pp=pp: e.tensor_copy(pp[:, 0:64], M[:, 128:192]), reads=[M], writes=[pp])
            P.op("gpsimd", lambda e, M=M, pp=pp: e.tensor_copy(pp[:, 64:128], M[:, 256:320]), reads=[M], writes=[pp])
            P.op("vector", lambda e, M=M, tt=tt: e.tensor_tensor(tt[:, :], M[:, 128:192], ident2[:, :], ALU.add), reads=[M, ident2], writes=[tt])
            for it in range(ndbl):
                pn = PP[(it + 1) % 2]; tn = TT_[(it + 1) % 2]
                for s in range(2):
                    sl = slice(64 * s, 64 * s + 64)
                    P.op("tensor", lambda e, sl=sl, pp=pp: e.matmul(PD[sl, 0:64], lhsT=pp[sl, 64:128], rhs=pp[sl, 0:64], start=True, stop=True),
                         reads=[pp], writes=[PD])
                    P.op("tensor", lambda e, sl=sl, pp=pp: e.matmul(PD[sl, 64:128], lhsT=pp[sl, 0:64], rhs=pp[sl, 64:128], start=True, stop=True),
                         reads=[pp], writes=[PD])
                P.op("scalar", lambda e, pn=pn: e.activation(out=pn[:, :], in_=PD[:, 0:128], func=AF.Copy), reads=[PD], writes=[pn])
                for s in range(2):
                    sl = slice(64 * s, 64 * s + 64)
                    P.op("tensor", lambda e, sl=sl, pn=pn, tt=tt: e.matmul(PT[sl, 128:192], lhsT=pn[sl, 64:128], rhs=tt[sl, :], start=True, stop=True),
                         reads=[pn, tt], writes=[PT])
                P.op("vector", lambda e, tn=tn, tt=tt: e.tensor_tensor(tn[:, :], PT[:, 128:192], tt[:, :], ALU.add), reads=[PT, tt], writes=[tn])
                pp = pn; tt = tn
            if stage < 3:
                continue
            so = S0T[sidx % 2]; sn = S0T[(sidx + 1) % 2]
            sidx += 1
            for s in range(2):
                sl = slice(64 * s, 64 * s + 64)
                P.op("tensor", lambda e, sl=sl, c=c, so=so: e.matmul(PX[sl, 0:64], lhsT=OPS[sl, c * 256 + 0:c * 256 + 64], rhs=so[sl, :], start=True, stop=False),
                     reads=[OPS, so], writes=[PX])
                P.op("tensor", lambda e, sl=sl, M=M, tm=tm: e.matmul(PX[sl, 0:64], lhsT=M[sl, 0:64], rhs=tm[sl, 128:192], start=False, stop=True),
                     reads=[M, tm], writes=[PX])
            P.op("scalar", lambda e: e.activation(out=XnT[:, :], in_=PX[:, 0:64], func=AF.Copy, scale=-1.0), reads=[PX], writes=[XnT])
            for s in range(2):
                sl = slice(64 * s, 64 * s + 64)
                P.op("tensor", lambda e, sl=sl, tt=tt: e.matmul(PS[sl, 64:128], lhsT=tt[sl, :], rhs=XnT[sl, :], start=True, stop=True),
                     reads=[tt, XnT], writes=[PS])
            P.op("vector", lambda e: e.tensor_copy(SAnT[:, :], PS[:, 64:128]), reads=[PS], writes=[SAnT])
            if stage < 4:
                continue
            py = PY[(c // 8) % 2]
            cc = (c % 8) * CL
            for s in range(2):
                sl = slice(64 * s, 64 * s + 64)
                P.op("tensor", lambda e, sl=sl, c=c, so=so, py=py, cc=cc: e.matmul(py[sl, cc:cc + CL], lhsT=so[sl, :], rhs=OPS[sl, c * 256 + 64:c * 256 + 128], start=True, stop=False),
                     reads=[so, OPS], writes=[py])
                P.op("tensor", lambda e, sl=sl, M=M, tm=tm, py=py, cc=cc: e.matmul(py[sl, cc:cc + CL], lhsT=tm[sl, 128:192], rhs=M[sl, 64:128], start=False, stop=False),
                     reads=[M, tm], writes=[py])
                P.op("tensor", lambda e, sl=sl, M=M, py=py, cc=cc: e.matmul(py[sl, cc:cc + CL], lhsT=SAnT[sl, :], rhs=M[sl, 192:256], start=False, stop=True),
                     reads=[M, SAnT], writes=[py])
                P.op("tensor", lambda e, sl=sl, so=so: e.matmul(PST[sl, 0:64], lhsT=ident[sl, sl], rhs=so[sl, :], start=True, stop=False),
                     reads=[ident, so], writes=[PST])
                P.op("tensor", lambda e, sl=sl, tm=tm: e.matmul(PST[sl, 0:64], lhsT=tm[sl, 0:64], rhs=tm[sl, 128:192], start=False, stop=False),
                     reads=[tm], writes=[PST])
                P.op("tensor", lambda e, sl=sl, tm=tm: e.matmul(PST[sl, 0:64], lhsT=tm[sl, 64:128], rhs=SAnT[sl, :], start=False, stop=True),
                     reads=[tm, SAnT], writes=[PST])
            P.op("vector", lambda e, sn=sn, c=c: e.tensor_scalar(sn[:, :], PST[:, 0:64], EW[:, (c + 1) * CL - 1:(c + 1) * CL], None, ALU.mult),
                 reads=[PST, EW], writes=[sn])
            if c % 8 == 7 or c == nch - 1:
                cb = (c // 8) * 8 * CL
                w = (c % 8 + 1) * CL
                P.op("scalar", lambda e, py=py, cb=cb, w=w, Y=Y: e.activation(out=Y[:, cb:cb + w], in_=py[:, 0:w], func=AF.Copy), reads=[py], writes=[Y])
        for s in range(2):
            P.dma("gpsimd", ys_d[s, :, t0:t0 + n], Y[64 * s:64 * s + 64, 0:n], reads=[Y])

    P.barrier()
    TB2 = 1024
    for bi, (t0, n, s0, s1) in enumerate(blocks(TB2)[:nblk] if phase2 else []):
        m0 = s0 + s1 - t0 - n
        A0 = KKr; A1 = TMP; B0 = RN; B1 = KKN; GG = KD; YC = CS; SQ = EW; RS = EWI; OUT = YO[bi % 2]
        P.dma("sync", A0[0:64, 0:n], ys_d[0, :, t0:t0 + n], writes=[A0])
        P.dma("sync", A1[0:64, 0:n], ys_d[1, :, m0:m0 + n], writes=[A1])
        P.dma("sync", B0[0:64, 0:n], bon_d[0, :, t0:t0 + n], writes=[B0])
        P.dma("sync", B1[0:64, 0:n], bon_d[1, :, m0:m0 + n], writes=[B1])
        P.dma("sync", GG[0:64, 0:n], gg_d[:, t0:t0 + n], writes=[GG])
        P.op("vector", lambda e, n=n: e.tensor_tensor(A0[0:64, 0:n], A0[0:64, 0:n], rev_ap(A1[0:64, 0:n]), ALU.add), reads=[A0, A1], writes=[A0])
        P.op("gpsimd", lambda e, n=n: e.tensor_tensor(B0[0:64, 0:n], B0[0:64, 0:n], rev_ap(B1[0:64, 0:n]), ALU.add), reads=[B0, B1], writes=[B0])
        for c0 in range(0, n, 512):
            m = min(512, n - c0)
            P.op("tensor", lambda e, c0=c0, m=m: e.matmul(PW[0:64, 0:m], lhsT=o64[0:64, :], rhs=A0[0:64, c0:c0 + m], start=True, stop=True), reads=[o64, A0], writes=[PW])
            P.op("vector", lambda e, c0=c0, m=m: e.tensor_tensor(YC[0:64, c0:c0 + m], A0[0:64, c0:c0 + m], PW[0:64, 0:m], ALU.subtract), reads=[A0, PW], writes=[YC])
            P.op("scalar", lambda e, c0=c0, m=m: e.activation(out=SQ[0:64, c0:c0 + m], in_=YC[0:64, c0:c0 + m], func=AF.Square), reads=[YC], writes=[SQ])
            P.op("tensor", lambda e, c0=c0, m=m: e.matmul(PA[0:64, 0:m], lhsT=o64[0:64, :], rhs=SQ[0:64, c0:c0 + m], start=True, stop=True), reads=[o64, SQ], writes=[PA])
            P.op("scalar", lambda e, c0=c0, m=m: e.activation(out=RS[0:64, c0:c0 + m], in_=PA[0:64, 0:m], func=AF.Sqrt, bias=egn[0:64, 0:1]), reads=[PA, egn], writes=[RS])
        P.op("vector", lambda e, n=n: e.reciprocal(RS[0:64, 0:n], RS[0:64, 0:n]), reads=[RS], writes=[RS])
        P.op("vector", lambda e, n=n: e.tensor_tensor(YC[0:64, 0:n], YC[0:64, 0:n], RS[0:64, 0:n], ALU.mult), reads=[YC, RS], writes=[YC])
        P.op("vector", lambda e, n=n: e.tensor_scalar(YC[0:64, 0:n], YC[0:64, 0:n], lnw[:, 0:1], lnw[:, 1:2], ALU.mult, ALU.add), reads=[YC, lnw], writes=[YC])
        P.op("gpsimd", lambda e, n=n: e.tensor_tensor(YC[0:64, 0:n], YC[0:64, 0:n], B0[0:64, 0:n], ALU.add), reads=[YC, B0], writes=[YC])
        P.op("vector", lambda e, n=n, OUT=OUT: e.tensor_tensor(OUT[0:64, 0:n], YC[0:64, 0:n], GG[0:64, 0:n], ALU.mult), reads=[YC, GG], writes=[OUT])
        P.dma("gpsimd", yb_d[:, t0:t0 + n], OUT[0:64, 0:n], reads=[OUT])
    return P.finish()


NL = 20
LAGS = list(range(9)) + [8 * 2 ** i for i in range(1, 12)]
NCHK = T_ALL // 8
NBLKS = [(0, 32)] + [(32 + 256 * i, 256) for i in range(8)]


def build_B3(upto=9):
    P = Prog()
    us = P.dram("us", [2, 64, T_ALL], F32, "ExternalInput")
    lamp_d = P.dram("lamp", [128, 8, 3], F32, "ExternalInput")
    bx_d = P.dram("bx", [128, 2, 8, 128], F32, "ExternalInput")
    cx_d = P.dram("cx", [128, 2, 8, 128], F32, "ExternalInput")
    dsk_d = P.dram("dsk", [128, 1], F32, "ExternalInput")
    lagt_d = P.dram("lagt", [128, 8, NL], F32, "ExternalInput")
    sgn_d = P.dram("sgn", [128, 2], F32, "ExternalInput")
    ident_d = P.dram("ident", [128, 128], F32, "ExternalInput")
    jmat_d = P.dram("jmat", [128, 128], F32, "ExternalInput")
    ys = P.dram("ys", [2, 64, T_ALL], F32, "ExternalOutput")

    lamp = P.sb([128, 8, 3]); bx = P.sb([128, 2, 8, 128]); cx = P.sb([128, 2, 8, 128]); dsk = P.sb([128, 1])
    lagt = P.sb([128, 8, NL]); sgn = P.sb([128, 2]); ident = P.sb([128, 128]); jmat = P.sb([128, 128])
    for t, d in ((lamp, lamp_d), (bx, bx_d), (cx, cx_d), (dsk, dsk_d), (lagt, lagt_d), (sgn, sgn_d), (ident, ident_d), (jmat, jmat_d)):
        P.dma("sync", t[:], d, writes=[t])
    V = "vector"
    step = P.sb([128, 8]); xr = P.sb([128, 8]); ang = P.sb([128, 8])
    XR3 = P.sb([128, 8, NL]); AN3 = P.sb([128, 8, NL]); T3 = P.sb([128, 8, NL]); MAG = P.sb([128, 8, NL])
    S3 = P.sb([128, 8, NL]); C3 = P.sb([128, 8, NL]); LR = P.sb([128, 8, NL]); LI = P.sb([128, 8, NL])
    LIA = P.sb([128, 8, NL]); LRB = P.sb([128, 8, NL]); LIN = P.sb([128, 8, NL])
    mpi = P.sb([128, 1])
    P.op(V, lambda e: e.memset(mpi[:], -math.pi), writes=[mpi])
    P.op("scalar", lambda e: e.activation(out=step[:], in_=lamp[:, :, 2], func=AF.Exp), reads=[lamp], writes=[step])
    P.op(V, lambda e: e.tensor_tensor(xr[:], lamp[:, :, 0], step[:], ALU.mult), reads=[lamp, step], writes=[xr])
    P.op(V, lambda e: e.tensor_tensor(ang[:], lamp[:, :, 1], step[:], ALU.mult), reads=[lamp, step], writes=[ang])
    for u in range(8):
        P.op(V, lambda e, u=u: e.tensor_scalar(XR3[:, u, :], lagt[:, u, :], xr[:, u:u + 1], None, ALU.mult), reads=[lagt, xr], writes=[XR3])
        P.op("gpsimd", lambda e, u=u: e.tensor_scalar(AN3[:, u, :], lagt[:, u, :], ang[:, u:u + 1], None, ALU.mult), reads=[lagt, ang], writes=[AN3])
    P.op("scalar", lambda e: e.activation(out=MAG[:], in_=XR3[:], func=AF.Exp), reads=[XR3], writes=[MAG])

    I32 = mybir.dt.int32
    KI3 = P.sb([128, 8 * NL], I32); KF3 = P.sb([128, 8 * NL]); GG3 = P.sb([128, 8 * NL])

    def sin_of(DST_T, dst, SRC_T, src, shift, TMP_T, tmp, n):
        TWO_PI = 2 * math.pi
        P.op(V, lambda e: e.tensor_scalar(tmp, src, 1.0 / TWO_PI, shift / TWO_PI, ALU.mult, ALU.add), reads=[SRC_T], writes=[TMP_T])
        P.op(V, lambda e: e.tensor_copy(KI3[:, 0:n], tmp), reads=[TMP_T], writes=[KI3])
        P.op(V, lambda e: e.tensor_copy(KF3[:, 0:n], KI3[:, 0:n]), reads=[KI3], writes=[KF3])
        P.op(V, lambda e: e.scalar_tensor_tensor(tmp, KF3[:, 0:n], -TWO_PI, src, ALU.mult, ALU.add), reads=[KF3, SRC_T], writes=[TMP_T])
        P.op(V, lambda e: e.tensor_scalar(tmp, tmp, shift, None, ALU.add), reads=[TMP_T], writes=[TMP_T])
        P.op(V, lambda e: e.tensor_scalar(GG3[:, 0:n], tmp, math.pi, TWO_PI, ALU.is_gt, ALU.mult), reads=[TMP_T], writes=[GG3])
        P.op(V, lambda e: e.tensor_tensor(tmp, tmp, GG3[:, 0:n], ALU.subtract), reads=[TMP_T, GG3], writes=[TMP_T])
        P.op(V, lambda e: e.tensor_scalar(GG3[:, 0:n], tmp, -math.pi, TWO_PI, ALU.is_lt, ALU.mult), reads=[TMP_T], writes=[GG3])
        P.op(V, lambda e: e.tensor_tensor(tmp, tmp, GG3[:, 0:n], ALU.add), reads=[TMP_T, GG3], writes=[TMP_T])
        P.op(V, lambda e: e.tensor_scalar(tmp, tmp, math.pi, -math.pi, ALU.min, ALU.max), reads=[TMP_T], writes=[TMP_T])
        P.op("scalar", lambda e: e.activation(out=dst, in_=tmp, func=AF.Sin), reads=[TMP_T], writes=[DST_T])
    sin_of(S3, S3[:, :, :].rearrange("p a b -> p (a b)"), AN3, AN3[:, :, :].rearrange("p a b -> p (a b)"), 0.0, T3, T3[:, :, :].rearrange("p a b -> p (a b)"), 8 * NL)
    sin_of(C3, C3[:, :, :].rearrange("p a b -> p (a b)"), AN3, AN3[:, :, :].rearrange("p a b -> p (a b)"), math.pi / 2, T3, T3[:, :, :].rearrange("p a b -> p (a b)"), 8 * NL)
    P.op(V, lambda e: e.tensor_tensor(LR[:], MAG[:], C3[:], ALU.mult), reads=[MAG, C3], writes=[LR])
    P.op(V, lambda e: e.tensor_tensor(LI[:], MAG[:], S3[:], ALU.mult), reads=[MAG, S3], writes=[LI])
    P.op(V, lambda e: e.tensor_scalar(LIA[:], LI[:], sgn[:, 0:1], None, ALU.mult), reads=[LI, sgn], writes=[LIA])
    P.op(V, lambda e: e.tensor_scalar(LRB[:], LR[:], sgn[:, 1:2], None, ALU.mult), reads=[LR, sgn], writes=[LRB])
    P.op(V, lambda e: e.tensor_scalar(LIN[:], LI[:], -1.0, None, ALU.mult), reads=[LI], writes=[LIN])
    em1 = P.sb([128, 8]); t8 = P.sb([128, 8]); sh = P.sb([128, 8]); nr = P.sb([128, 8]); den = P.sb([128, 8])
    fr = P.sb([128, 8]); fi = P.sb([128, 8]); fiA = P.sb([128, 8]); fiB = P.sb([128, 8]); t8b = P.sb([128, 8])
    P.op(V, lambda e: e.tensor_scalar(em1[:], xr[:], 0.25, 1.0, ALU.mult, ALU.add), reads=[xr], writes=[em1])
    P.op(V, lambda e: e.tensor_tensor(em1[:], em1[:], xr[:], ALU.mult), reads=[em1, xr], writes=[em1])
    P.op(V, lambda e: e.tensor_scalar(em1[:], em1[:], 1.0 / 3, 1.0, ALU.mult, ALU.add), reads=[em1], writes=[em1])
    P.op(V, lambda e: e.tensor_tensor(em1[:], em1[:], xr[:], ALU.mult), reads=[em1, xr], writes=[em1])
    P.op(V, lambda e: e.tensor_scalar(em1[:], em1[:], 0.5, 1.0, ALU.mult, ALU.add), reads=[em1], writes=[em1])
    P.op(V, lambda e: e.tensor_tensor(em1[:], em1[:], xr[:], ALU.mult), reads=[em1, xr], writes=[em1])
    P.op(V, lambda e: e.tensor_scalar(t8[:], ang[:], 0.5, None, ALU.mult), reads=[ang], writes=[t8])
    sin_of(sh, sh[:, :], t8, t8[:, :], 0.0, t8b, t8b[:, :], 8)
    P.op(V, lambda e: e.tensor_tensor(sh[:], sh[:], sh[:], ALU.mult), reads=[sh], writes=[sh])
    P.op(V, lambda e: e.tensor_tensor(nr[:], em1[:], C3[:, :, 1], ALU.mult), reads=[em1, C3], writes=[nr])
    P.op(V, lambda e: e.scalar_tensor_tensor(nr[:], sh[:], -2.0, nr[:], ALU.mult, ALU.add), reads=[sh, nr], writes=[nr])
    P.op(V, lambda e: e.tensor_tensor(den[:], lamp[:, :, 0], lamp[:, :, 0], ALU.mult), reads=[lamp], writes=[den])
    P.op(V, lambda e: e.tensor_tensor(t8[:], lamp[:, :, 1], lamp[:, :, 1], ALU.mult), reads=[lamp], writes=[t8])
    P.op(V, lambda e: e.tensor_tensor(den[:], den[:], t8[:], ALU.add), reads=[den, t8], writes=[den])
    P.op(V, lambda e: e.reciprocal(den[:], den[:]), reads=[den], writes=[den])
    P.op(V, lambda e: e.tensor_tensor(fr[:], nr[:], lamp[:, :, 0], ALU.mult), reads=[nr, lamp], writes=[fr])
    P.op(V, lambda e: e.tensor_tensor(t8[:], LI[:, :, 1], lamp[:, :, 1], ALU.mult), reads=[LI, lamp], writes=[t8])
    P.op(V, lambda e: e.tensor_tensor(fr[:], fr[:], t8[:], ALU.add), reads=[fr, t8], writes=[fr])
    P.op(V, lambda e: e.tensor_tensor(fr[:], fr[:], den[:], ALU.mult), reads=[fr, den], writes=[fr])
    P.op(V, lambda e: e.tensor_tensor(fi[:], LI[:, :, 1], lamp[:, :, 0], ALU.mult), reads=[LI, lamp], writes=[fi])
    P.op(V, lambda e: e.tensor_tensor(t8[:], nr[:], lamp[:, :, 1], ALU.mult), reads=[nr, lamp], writes=[t8])
    P.op(V, lambda e: e.tensor_tensor(fi[:], fi[:], t8[:], ALU.subtract), reads=[fi, t8], writes=[fi])
    P.op(V, lambda e: e.tensor_tensor(fi[:], fi[:], den[:], ALU.mult), reads=[fi, den], writes=[fi])
    P.op(V, lambda e: e.tensor_scalar(fiA[:], fi[:], sgn[:, 0:1], None, ALU.mult), reads=[fi, sgn], writes=[fiA])
    P.op(V, lambda e: e.tensor_scalar(fiB[:], fi[:], sgn[:, 1:2], None, ALU.mult), reads=[fi, sgn], writes=[fiB])

    BT = P.sb([128, 8, 128]); BTs = P.sb([128, 8, 128]); CC = P.sb([128, 8, 128])
    MATS = P.sb([128, 64, 128])
    KL = P.sb([128, 8, 128])
    LLB = [P.sb([128, 128]) for _ in range(2)]
    ps = [P.ps() for _ in range(8)]
    pk = 0
    for u in range(8):
        eng = V if u % 2 == 0 else "gpsimd"
        P.op(eng, lambda e, u=u: e.tensor_scalar(BT[:, u, :], bx[:, 0, u, :], fr[:, u:u + 1], None, ALU.mult), reads=[bx, fr], writes=[BT])
        P.op(V, lambda e, u=u: e.scalar_tensor_tensor(BT[:, u, :], bx[:, 1, u, :], fiA[:, u:u + 1], BT[:, u, :], ALU.mult, ALU.add), reads=[bx, fiA, BT], writes=[BT])
        P.op(eng, lambda e, u=u: e.tensor_scalar(BTs[:, u, :], bx[:, 1, u, :], fr[:, u:u + 1], None, ALU.mult), reads=[bx, fr], writes=[BTs])
        P.op(V, lambda e, u=u: e.scalar_tensor_tensor(BTs[:, u, :], bx[:, 0, u, :], fiB[:, u:u + 1], BTs[:, u, :], ALU.mult, ALU.add), reads=[bx, fiB, BTs], writes=[BTs])
        P.op(eng, lambda e, u=u: e.tensor_scalar(CC[:, u, :], cx[:, 0, u, :], sgn[:, 1:2], None, ALU.mult), reads=[cx, sgn], writes=[CC])
    pkl = [ps[0], ps[1]]
    for L in range(8):
        pK = pkl[L % 2]
        for u in range(8):
            llb = LLB[(L * 8 + u) % 2]
            eng = V if u % 2 == 0 else "gpsimd"
            P.op(eng, lambda e, u=u, L=L, llb=llb: e.tensor_scalar(llb[:], BT[:, u, :], LR[:, u, L:L + 1], None, ALU.mult), reads=[BT, LR], writes=[llb])
            P.op(V, lambda e, u=u, L=L, llb=llb: e.scalar_tensor_tensor(llb[:], BTs[:, u, :], LIA[:, u, L:L + 1], llb[:], ALU.mult, ALU.add), reads=[BTs, LIA, llb], writes=[llb])
            pt = ps[2 + (L * 8 + u) % 2]
            P.op("tensor", lambda e, llb=llb, pt=pt: e.matmul(pt[:, 0:128], lhsT=llb[:], rhs=ident[:], start=True, stop=True), reads=[llb, ident], writes=[pt])
            P.op("scalar", lambda e, u=u, L=L, pt=pt: e.activation(out=MATS[:, u * 8 + L, :], in_=pt[:, 0:128], func=AF.Copy), reads=[pt], writes=[MATS])
            P.op("tensor", lambda e, llb=llb, u=u, pK=pK: e.matmul(pK[:, 0:128], lhsT=llb[:], rhs=CC[:, u, :], start=(u == 0), stop=(u == 7)), reads=[llb, CC], writes=[pK])
        P.op(V, lambda e, L=L, pK=pK: e.tensor_copy(KL[:, L, :], pK[:, 0:128]), reads=[pK], writes=[KL])
    P.op(V, lambda e: e.scalar_tensor_tensor(KL[:, 0, :], ident[:], dsk[:, 0:1], KL[:, 0, :], ALU.mult, ALU.add), reads=[ident, dsk, KL], writes=[KL])

    EALL = P.sb([128, 8, NCHK + 1])
    UB = [P.sb([128, 2048]) for _ in range(2)]
    UD = [P.sb([128, 2048]) for _ in range(2)]
    P.op("gpsimd", lambda e: e.memset(EALL[:, :, 0:1], 0.0), writes=[EALL])
    pi = 0
    for bi, (n0, nb) in enumerate(NBLKS):
        ub = UB[bi % 2]
        for s in range(2):
            P.dma("sync", ub[64 * s:64 * s + 64, 0:nb * 8], us[s, :, n0 * 8:(n0 + nb) * 8], writes=[ub])
        ud = UD[bi % 2]
        P.op("gpsimd", lambda e, ub=ub, ud=ud, nb=nb: e.tensor_copy(ud[:, 0:8 * nb].rearrange("p (i n) -> p i n", i=8), ub[:, 0:8 * nb].rearrange("p (n i) -> p i n", i=8)),
             reads=[ub], writes=[ud])
        for u in range(8):
            pg = ps[4 + pi % 4]
            pi += 1
            for i in range(8):
                P.op("tensor", lambda e, u=u, i=i, ud=ud, nb=nb, pg=pg: e.matmul(pg[:, 0:nb], lhsT=MATS[:, u * 8 + (7 - i), :], rhs=ud[:, i * nb:(i + 1) * nb],
                                                                                start=(i == 0), stop=(i == 7)), reads=[MATS, ud], writes=[pg])
            if u % 2 == 0:
                P.op("scalar", lambda e, u=u, n0=n0, nb=nb, pg=pg: e.activation(out=EALL[:, u, 1 + n0:1 + n0 + nb], in_=pg[:, 0:nb], func=AF.Copy), reads=[pg], writes=[EALL])
            else:
                P.op(V, lambda e, u=u, n0=n0, nb=nb, pg=pg: e.tensor_copy(EALL[:, u, 1 + n0:1 + n0 + nb], pg[:, 0:nb]), reads=[pg], writes=[EALL])
    if upto < 2:
        return P.finish()
    W = [P.sb([128, NCHK]) for _ in range(2)]
    MT = [P.sb([128, 128]) for _ in range(2)]
    mi = 0
    for u in range(8):
        cur = None
        for it in range(12):
            shf = 2 ** it
            mt = MT[mi % 2]
            mi += 1
            P.op("gpsimd", lambda e, mt=mt, u=u, it=it: e.tensor_scalar(mt[:], ident[:], LR[:, u, 8 + it:9 + it], None, ALU.mult), reads=[ident, LR], writes=[mt])
            P.op(V, lambda e, mt=mt, u=u, it=it: e.scalar_tensor_tensor(mt[:], jmat[:], LI[:, u, 8 + it:9 + it], mt[:], ALU.mult, ALU.add), reads=[jmat, LI, mt], writes=[mt])
            dst = W[it % 2]
            src_ap = (lambda a, b, u=u: EALL[:, u, 1 + a:1 + b]) if cur is None else (lambda a, b, cur=cur: cur[:, a:b])
            src_t = EALL if cur is None else cur
            P.op("scalar", lambda e, dst=dst, src_ap=src_ap, shf=shf: e.activation(out=dst[:, 0:shf], in_=src_ap(0, shf), func=AF.Copy), reads=[src_t], writes=[dst])
            for c0 in range(shf, NCHK, 512):
                w = min(512, NCHK - c0)
                pd = ps[4 + pi % 4]
                pi += 1
                P.op("tensor", lambda e, mt=mt, src_ap=src_ap, c0=c0, w=w, shf=shf, pd=pd: e.matmul(pd[:, 0:w], lhsT=mt[:], rhs=src_ap(c0 - shf, c0 - shf + w), start=True, stop=True),
                     reads=[mt, src_t], writes=[pd])
                P.op(V, lambda e, dst=dst, src_ap=src_ap, c0=c0, w=w, pd=pd: e.tensor_tensor(dst[:, c0:c0 + w], pd[:, 0:w], src_ap(c0, c0 + w), ALU.add),
                     reads=[pd, src_t], writes=[dst])
            cur = dst
        P.op("scalar", lambda e, u=u, cur=cur: e.activation(out=EALL[:, u, 1:1 + NCHK], in_=cur[:, :], func=AF.Copy), reads=[cur], writes=[EALL])
    for u in range(8):
        for L in range(1, 9):
            eng = V if L % 2 == 0 else "gpsimd"
            P.op(eng, lambda e, u=u, L=L: e.tensor_scalar(MATS[:, u * 8 + L - 1, :], cx[:, 0, u, :], LRB[:, u, L:L + 1], None, ALU.mult), reads=[cx, LRB], writes=[MATS])
            P.op(V, lambda e, u=u, L=L: e.scalar_tensor_tensor(MATS[:, u * 8 + L - 1, :], cx[:, 1, u, :], LIN[:, u, L:L + 1], MATS[:, u * 8 + L - 1, :], ALU.mult, ALU.add),
                 reads=[cx, LIN, MATS], writes=[MATS])
    YB = W
    for bi, (n0, nb) in enumerate(NBLKS):
        ub = UB[bi % 2]; yb = YB[bi % 2]
        for s in range(2):
            P.dma("sync", ub[64 * s:64 * s + 64, 0:nb * 8], us[s, :, n0 * 8:(n0 + nb) * 8], writes=[ub])
        ud = UD[bi % 2]; yd = UB[(bi + 1) % 2]
        P.op("gpsimd", lambda e, ub=ub, ud=ud, nb=nb: e.tensor_copy(ud[:, 0:8 * nb].rearrange("p (i n) -> p i n", i=8), ub[:, 0:8 * nb].rearrange("p (n i) -> p i n", i=8)),
             reads=[ub], writes=[ud])
        for j in range(8):
            py = ps[pi % 4]
            pi += 1
            k = 0
            nmm = (j + 1) + 8
            for i in range(j + 1):
                P.op("tensor", lambda e, i=i, j=j, ud=ud, nb=nb, py=py, k=k, nmm=nmm: e.matmul(py[:, 0:nb], lhsT=KL[:, j - i, :], rhs=ud[:, i * nb:(i + 1) * nb],
                                                                                           start=(k == 0), stop=(k == nmm - 1)), reads=[KL, ud], writes=[py])
                k += 1
            for u in range(8):
                P.op("tensor", lambda e, u=u, j=j, n0=n0, nb=nb, py=py, k=k, nmm=nmm: e.matmul(py[:, 0:nb], lhsT=MATS[:, u * 8 + j, :], rhs=EALL[:, u, n0:n0 + nb],
                                                                                            start=(k == 0), stop=(k == nmm - 1)), reads=[MATS, EALL], writes=[py])
                k += 1
            if j % 2 == 0:
                P.op("scalar", lambda e, j=j, nb=nb, py=py, yd=yd: e.activation(out=yd[:, j * nb:(j + 1) * nb], in_=py[:, 0:nb], func=AF.Copy), reads=[py], writes=[yd])
            else:
                P.op(V, lambda e, j=j, nb=nb, py=py, yd=yd: e.tensor_copy(yd[:, j * nb:(j + 1) * nb], py[:, 0:nb]), reads=[py], writes=[yd])
        P.op("gpsimd", lambda e, yb=yb, yd=yd, nb=nb: e.tensor_copy(yb[:, 0:8 * nb].rearrange("p (n i) -> p i n", i=8), yd[:, 0:8 * nb].rearrange("p (i n) -> p i n", i=8)),
             reads=[yd], writes=[yb])
        for s in range(2):
            P.dma("gpsimd", ys[s, :, n0 * 8:(n0 + nb) * 8], yb[64 * s:64 * s + 64, 0:nb * 8], reads=[yb])
    return P.finish()


CW = 416
CHC = [(CW * i, CW) for i in range(5)]
NCTX = 32
FH = 2816
NFB = FH // 128
GELU_C = 2.0 * math.sqrt(2.0 / math.pi)


def build_C():
    P = Prog()
    hT = P.dram("hT", [D, NTOK], F32, "ExternalInput")
    zT = P.dram("zT", [N_IN, NTOK], F32, "ExternalInput")
    hlf = P.dram("hlf", [512, NTOK], F32, "ExternalInput"); hlb = P.dram("hlb", [512, NTOK], F32, "ExternalInput")
    ybT = P.dram("ybT", [512, NTOK], F32, "ExternalInput")
    ysf = P.dram("ysf", [512, NTOK], F32, "ExternalInput"); ysb = P.dram("ysb", [512, NTOK], F32, "ExternalInput")
    cnd = P.dram("cnd", [128, KT, 2], F32, "ExternalInput")
    wmod = P.dram("wmod", [D, 4096], F32, "ExternalInput")
    bmod = P.dram("bmod", [128, 32], F32, "ExternalInput")
    nrm2 = P.dram("nrm2", [128, KT], F32, "ExternalInput"); nrmf = P.dram("nrmf", [128, KT], F32, "ExternalInput")
    wglu = P.dram("wglu", [512, 512], F32, "ExternalInput"); bglu = P.dram("bglu", [128, 4], F32, "ExternalInput")
    wbr = P.dram("wbr", [3, 512, D], F32, "ExternalInput")
    wout = P.dram("wout", [D, D], F32, "ExternalInput")
    wfi = P.dram("wfi", [D, 2 * FH], F32, "ExternalInput")
    wfo = P.dram("wfo", [FH, D], F32, "ExternalInput")
    hN = P.dram("hN", [D, NTOK], F32, "ExternalOutput")
    hF = P.dram("hF", [D, NTOK], F32, "ExternalOutput")

    V = "vector"
    cond = P.sb([128, KT, 2]); bmod_sb = P.sb([128, 32]); n2 = P.sb([128, KT]); nf = P.sb([128, KT]); bg = P.sb([128, 4])
    ones = P.sb([128, 128]); epst = P.sb([128, 1]); modv = P.sb([128, 32, 2]); gsc = P.sb([128, KT, 2])
    WB = [P.sb([128, 4096]) for _ in range(3)]
    wbi = [0]

    def wbuf():
        w = WB[wbi[0] % 3]
        wbi[0] += 1
        return w
    H = P.sb([128, KT, CW]); XM = P.sb([128, KT, CW]); ACC = P.sb([128, KT, CW]); HO = P.sb([128, KT, CW])
    YK = [P.sb([128, 4, CW]) for _ in range(3)]
    T1 = P.sb([128, 4, CW]); T2 = P.sb([128, 4, CW])
    ZG = [P.sb([128, CW]) for _ in range(2)]
    TMP = [P.sb([128, CW]) for _ in range(2)]
    SQ = [P.sb([128, CW]) for _ in range(2)]
    RSTD = P.sb([128, CW])
    AH = P.sb([128, NFB, CW])
    pp = [P.ps() for _ in range(7)]
    psm = P.ps()
    pc = [0]

    def psum():
        p = pp[pc[0] % 7]
        pc[0] += 1
        return p

    for t, d in ((cond, cnd), (bmod_sb, bmod), (n2, nrm2), (nf, nrmf), (bg, bglu)):
        P.dma("sync", t[:], d, writes=[t])
    P.op(V, lambda e: e.memset(ones[:], 1.0), writes=[ones])
    P.op(V, lambda e: e.memset(epst[:], EPS), writes=[epst])
    P.op("scalar", lambda e: e.activation(out=cond[:], in_=cond[:], func=AF.Silu), reads=[cond], writes=[cond])
    WM = [Tile(WB[0].ap[:, 0:4096].rearrange("p (k c) -> p k c", k=KT), WB[0].cells), Tile(WB[1].ap[:, 0:4096].rearrange("p (k c) -> p k c", k=KT), WB[1].cells)]
    mod_vectors(P, wmod, bmod_sb, cond, 32, WM, psm, modv)
    for which in range(2):
        P.op(V, (lambda e, which=which: e.scalar_tensor_tensor(gsc[:, :, which], modv[:, 16:24, which], 1.0, n2[:, :], ALU.add, ALU.mult)),
             reads=[modv, n2], writes=[gsc])

    def gelu_tanh(dst, src, tmp, eng2="gpsimd"):
        fl = lambda t: t[:, :, :].rearrange("p a b -> p (a b)")
        P.op("scalar", lambda e: e.activation(out=fl(tmp), in_=fl(src), func=AF.Square), reads=[src], writes=[tmp])
        P.op(V, lambda e: e.tensor_scalar(fl(tmp), fl(tmp), 0.044715, 1.0, ALU.mult, ALU.add), reads=[tmp], writes=[tmp])
        P.op(eng2, lambda e: e.tensor_tensor(fl(tmp), fl(tmp), fl(src), ALU.mult), reads=[tmp, src], writes=[tmp])
        P.op("scalar", lambda e: e.activation(out=fl(tmp), in_=fl(tmp), func=AF.Sigmoid, scale=GELU_C), reads=[tmp], writes=[tmp])
        P.op(V, lambda e: e.tensor_tensor(fl(dst), fl(tmp), fl(src), ALU.mult), reads=[tmp, src], writes=[dst])

    def rstd_of(X, which_none=None):
        ps = psum()
        for kt in range(KT):
            s = SQ[kt % 2]
            P.op("scalar", lambda e, s=s, kt=kt: e.activation(out=s[:, :], in_=X[:, kt, :], func=AF.Square), reads=[X], writes=[s])
            P.op("tensor", lambda e, s=s, kt=kt, ps=ps: e.matmul(ps[:, 0:CW], lhsT=ones[:, :], rhs=s[:, :], start=(kt == 0), stop=(kt == KT - 1)),
                 reads=[s, ones], writes=[ps])
        P.op("scalar", lambda e, ps=ps: e.activation(out=RSTD[:, :], in_=ps[:, 0:CW], func=AF.Sqrt, bias=epst[:, 0:1], scale=1.0 / D), reads=[ps, epst], writes=[RSTD])
        P.op(V, lambda e: e.reciprocal(RSTD[:, :], RSTD[:, :]), reads=[RSTD], writes=[RSTD])

    for ci, (o, n) in enumerate(CHC):
        rngs = [(0, NCTX, 1), (NCTX, n, 0)] if ci == 0 else [(0, n, 0)]
        tk = slice(o, o + n)
        P.dma("gpsimd", H[:, :, :], hT[:, tk].rearrange("(kt p) t -> p kt t", p=128), writes=[H])
        P.dma("gpsimd", T1[:, :, :], zT[512:1024, tk].rearrange("(kt p) t -> p kt t", p=128), writes=[T1])
        P.dma("gpsimd", YK[0][:, :, :], hlf[:, tk].rearrange("(kt p) t -> p kt t", p=128), writes=[YK[0]])
        P.dma("gpsimd", YK[1][:, :, :], hlb[:, tk].rearrange("(kt p) t -> p kt t", p=128), writes=[YK[1]])
        gelu_tanh(T1, T1, T2)
        P.op("gpsimd", lambda e: e.tensor_tensor(YK[0][:, :, :], YK[0][:, :, :], YK[1][:, :, :], ALU.add), reads=[YK[0], YK[1]], writes=[YK[0]])
        P.op(V, lambda e: e.tensor_tensor(YK[0][:, :, :], YK[0][:, :, :], T1[:, :, :], ALU.mult), reads=[YK[0], T1], writes=[YK[0]])
        P.dma("gpsimd", YK[1][:, :, :], ybT[:, tk].rearrange("(kt p) t -> p kt t", p=128), writes=[YK[1]])
        P.dma("gpsimd", T1[:, :, :], ysf[:, tk].rearrange("(kt p) t -> p kt t", p=128), writes=[T1])
        P.dma("gpsimd", YK[2][:, :, :], ysb[:, tk].rearrange("(kt p) t -> p kt t", p=128), writes=[YK[2]])
        P.op("gpsimd", lambda e: e.tensor_tensor(T1[:, :, :], T1[:, :, :], YK[2][:, :, :], ALU.add), reads=[T1, YK[2]], writes=[T1])
        gelu_tanh(T1, T1, T2)
        wg = wbuf()
        P.dma("sync", wg[:, 0:2048].rearrange("p (k c) -> p k c", k=4), wglu.rearrange("(kt p) c -> p kt c", p=128), writes=[wg])
        for jb in range(4):
            ps = psum()
            for kt in range(4):
                P.op("tensor", lambda e, ps=ps, wg=wg, kt=kt, jb=jb: e.matmul(ps[:, 0:CW], lhsT=wg[:, kt * 512 + jb * 128:kt * 512 + jb * 128 + 128], rhs=T1[:, kt, :],
                                                                             start=(kt == 0), stop=(kt == 3)), reads=[wg, T1], writes=[ps])
            P.op("scalar", lambda e, ps=ps, jb=jb: e.activation(out=T2[:, jb, :], in_=ps[:, 0:CW], func=AF.Sigmoid, bias=bg[:, jb:jb + 1]), reads=[ps, bg], writes=[T2])
        P.op(V, lambda e: e.tensor_tensor(YK[2][:, :, :], T1[:, :, :], T2[:, :, :], ALU.mult), reads=[T1, T2], writes=[YK[2]])
        for k in range(3):
            wb = wbuf()
            P.dma("sync", wb[:, 0:4096].rearrange("p (k c) -> p k c", k=4), wbr[k].rearrange("(kt p) c -> p kt c", p=128), writes=[wb])
            for db in range(8):
                zg = ZG[(k * 8 + db) % 2]
                r0 = 3456 + k * 1024 + db * 128
                P.dma("gpsimd", zg[:, :], zT[r0:r0 + 128, tk], writes=[zg])
                P.op("scalar", lambda e, zg=zg: e.activation(out=zg[:, :], in_=zg[:, :], func=AF.Sigmoid), reads=[zg], writes=[zg])
                ps = psum()
                for kt in range(4):
                    P.op("tensor", lambda e, ps=ps, wb=wb, kt=kt, db=db, k=k: e.matmul(ps[:, 0:CW], lhsT=wb[:, kt * 1024 + db * 128:kt * 1024 + db * 128 + 128], rhs=YK[k][:, kt, :],
                                                                                    start=(kt == 0), stop=(kt == 3)), reads=[wb, YK[k]], writes=[ps])
                if k == 0:
                    P.op(V, lambda e, ps=ps, zg=zg, db=db: e.tensor_tensor(ACC[:, db, :], ps[:, 0:CW], zg[:, :], ALU.mult), reads=[ps, zg], writes=[ACC])
                else:
                    tm = TMP[db % 2]
                    P.op(V, lambda e, ps=ps, zg=zg, tm=tm: e.tensor_tensor(tm[:, :], ps[:, 0:CW], zg[:, :], ALU.mult), reads=[ps, zg], writes=[tm])
                    P.op("gpsimd", lambda e, tm=tm, db=db: e.tensor_tensor(ACC[:, db, :], ACC[:, db, :], tm[:, :], ALU.add), reads=[ACC, tm], writes=[ACC])
        for half in range(2):
            wo = wbuf()
            P.dma("sync", wo[:, 0:4096].rearrange("p (k c) -> p k c", k=8), wout[:, half * 512:(half + 1) * 512].rearrange("(kt p) c -> p kt c", p=128), writes=[wo])
            for dq in range(4):
                db = half * 4 + dq
                ps = psum()
                for kt in range(KT):
                    P.op("tensor", lambda e, ps=ps, wo=wo, kt=kt, dq=dq: e.matmul(ps[:, 0:CW], lhsT=wo[:, kt * 512 + dq * 128:kt * 512 + dq * 128 + 128], rhs=ACC[:, kt, :],
                                                                               start=(kt == 0), stop=(kt == KT - 1)), reads=[wo, ACC], writes=[ps])
                for (a, b, which) in rngs:
                    P.op(V, lambda e, ps=ps, db=db, a=a, b=b, which=which: e.scalar_tensor_tensor(H[:, db, a:b], ps[:, a:b], modv[:, db, which:which + 1], H[:, db, a:b], ALU.mult, ALU.add),
                         reads=[ps, modv, H], writes=[H])
        rstd_of(H)
        for kt in range(KT):
            eng = V if kt % 2 == 0 else "gpsimd"
            P.op(eng, lambda e, kt=kt: e.tensor_tensor(XM[:, kt, :], H[:, kt, :], RSTD[:, :], ALU.mult), reads=[H, RSTD], writes=[XM])
            for (a, b, which) in rngs:
                P.op(eng, lambda e, kt=kt, a=a, b=b, which=which: e.tensor_scalar(XM[:, kt, a:b], XM[:, kt, a:b], gsc[:, kt, which:which + 1], modv[:, 8 + kt, which:which + 1], ALU.mult, ALU.add),
                     reads=[XM, gsc, modv], writes=[XM])
        for fq in range(0, NFB, 2):
            wf = wbuf()
            for gu in range(2):
                P.dma("sync", wf[:, 0:4096].rearrange("p (k g c) -> p k g c", k=8, g=2)[:, :, gu, :],
                      wfi[:, gu * FH + fq * 128:gu * FH + fq * 128 + 256].rearrange("(kt p) c -> p kt c", p=128), writes=[wf])
            for f2 in range(2):
                fb = fq + f2
                pg = psum(); pu = psum()
                for kt in range(KT):
                    P.op("tensor", lambda e, pg=pg, wf=wf, kt=kt, f2=f2: e.matmul(pg[:, 0:CW], lhsT=wf[:, kt * 512 + f2 * 128:kt * 512 + f2 * 128 + 128], rhs=XM[:, kt, :],
                                                                               start=(kt == 0), stop=(kt == KT - 1)), reads=[wf, XM], writes=[pg])
                for kt in range(KT):
                    P.op("tensor", lambda e, pu=pu, wf=wf, kt=kt, f2=f2: e.matmul(pu[:, 0:CW], lhsT=wf[:, kt * 512 + 256 + f2 * 128:kt * 512 + 256 + f2 * 128 + 128], rhs=XM[:, kt, :],
                                                                               start=(kt == 0), stop=(kt == KT - 1)), reads=[wf, XM], writes=[pu])
                tm = TMP[fb % 2]
                P.op("scalar", lambda e, pg=pg, tm=tm: e.activation(out=tm[:, :], in_=pg[:, 0:CW], func=AF.Silu), reads=[pg], writes=[tm])
                P.op(V, lambda e, pu=pu, tm=tm, fb=fb: e.tensor_tensor(AH[:, fb, :], pu[:, 0:CW], tm[:, :], ALU.mult), reads=[pu, tm], writes=[AH])
        for db in range(8):
            wo = wbuf()
            P.dma("sync", wo[:, 0:NFB * 128].rearrange("p (f c) -> p f c", f=NFB), wfo[:, db * 128:(db + 1) * 128].rearrange("(f p) c -> p f c", p=128), writes=[wo])
            ps = psum()
            for fb in range(NFB):
                P.op("tensor", lambda e, ps=ps, wo=wo, fb=fb: e.matmul(ps[:, 0:CW], lhsT=wo[:, fb * 128:(fb + 1) * 128], rhs=AH[:, fb, :],
                                                                    start=(fb == 0), stop=(fb == NFB - 1)), reads=[wo, AH], writes=[ps])
            for (a, b, which) in rngs:
                P.op(V, lambda e, ps=ps, db=db, a=a, b=b, which=which: e.scalar_tensor_tensor(HO[:, db, a:b], ps[:, a:b], modv[:, 24 + db, which:which + 1], H[:, db, a:b], ALU.mult, ALU.add),
                     reads=[ps, modv, H], writes=[HO])
        P.dma("gpsimd", hN[:, tk].rearrange("(kt p) t -> p kt t", p=128), HO[:, :, :], reads=[HO])
        rstd_of(HO)
        for kt in range(KT):
            eng = V if kt % 2 == 0 else "gpsimd"
            P.op(eng, lambda e, kt=kt: e.tensor_tensor(XM[:, kt, :], HO[:, kt, :], RSTD[:, :], ALU.mult), reads=[HO, RSTD], writes=[XM])
            P.op(eng, lambda e, kt=kt: e.tensor_scalar(XM[:, kt, :], XM[:, kt, :], nf[:, kt:kt + 1], None, ALU.mult), reads=[XM, nf], writes=[XM])
        P.dma("gpsimd", hF[:, tk].rearrange("(kt p) t -> p kt t", p=128), XM[:, :, :], reads=[XM])
    return P.finish()


_PROGS = {}


def _prog(name):
    if name not in _PROGS:
        _PROGS[name] = {"A": build_A, "B1": build_B1, "B2": build_B2, "B3": build_B3, "C": build_C}[name]()
    return _PROGS[name]


def _run(name, maps):
    maps = [{k: np.ascontiguousarray(v, dtype=np.float32) for k, v in m.items()} for m in maps]
    res = run_bass_kernel_spmd(_prog(name), maps, core_ids=list(range(8)))
    return res.results


def _tok_shard(ctxT, latT, k):
    return np.concatenate([ctxT[:, 32 * k:32 * k + 32], latT[:, 2048 * k:2048 * k + 2048]], axis=1)


def _streams(zc, zl):
    return np.concatenate([zc, zl], 0).T, np.concatenate([zc[::-1], zl[::-1]], 0).T


def _col_major(t):
    return t.reshape(256, 64, -1).transpose(1, 0, 2).reshape(16384, -1)


def _raster_cm(t):
    c = t.shape[0]
    return t.reshape(c, 64, 256).transpose(0, 2, 1).reshape(c, 16384)


def _cnd(c, c_ctx):
    return np.stack([c.reshape(-1), c_ctx.reshape(-1)], axis=1).reshape(8, 128, 2).transpose(1, 0, 2)


def _pk(v):
    return v.reshape(8, 128).T


def kernel(x, c, ctx, c_ctx, w_mod, b_mod, norm1, norm2, norm_f, w_in,
           lru_conv_w, lru_conv_b, lru_wa, lru_ba, lru_wx, lru_bx, lru_lam,
           rwkv_mu, rwkv_w0, rwkv_w2, rwkv_a0, rwkv_a2, rwkv_g2, rwkv_kk, rwkv_ka, rwkv_rk,
           rwkv_lnw, rwkv_lnb,
           s5_lam_re, s5_lam_im, s5_log_step, s5_b_re, s5_b_im, s5_c_re, s5_c_im, s5_d,
           s5_w_glu, s5_b_glu,
           w_branch, w_out, w_ffn_in, w_ffn_out, _nlayers=4, _debug=None):
    f32 = np.float32
    A_ = lambda a: np.asarray(a, dtype=f32)
    x, c, ctx, c_ctx = A_(x), A_(c), A_(ctx), A_(c_ctx)
    latT = np.ascontiguousarray(x[0].T)
    ctxT = np.ascontiguousarray(ctx[0].T)
    cnd = _cnd(c, c_ctx)
    i64 = np.arange(64)
    su = (i64[None, :] > i64[:, None]).astype(f32); up = (i64[None, :] >= i64[:, None]).astype(f32)
    m1 = np.concatenate([su, up, -su, up, -su.T], 1)
    cm = np.ones((128, 1024), f32); cm[:, ::64] = 0
    cB2 = {"mask": np.concatenate([m1, m1], 0), "ident": np.eye(128, dtype=f32),
           "ident2": np.concatenate([np.eye(64, dtype=f32)] * 2, 0), "cmask": cm}
    jm = np.zeros((128, 128), f32)
    for k in range(64):
        jm[k, k + 64] = 1.0; jm[k + 64, k] = -1.0
    cB3 = {"lagt": np.broadcast_to(np.array(LAGS, f32)[None, None, :], (128, 8, NL)).copy(),
           "sgn": np.concatenate([np.tile([[-1.0, 1.0]], (64, 1)), np.tile([[1.0, -1.0]], (64, 1))], 0).astype(f32),
           "ident": np.eye(128, dtype=f32), "jmat": jm}
    hF = None
    for li in range(_nlayers):
        wm, bm = A_(w_mod[li]), A_(b_mod[li])
        mapsA = [{"xT": _tok_shard(ctxT, latT, k), "cnd": cnd, "wmod": wm[:, 0:2048], "bmod": bm[0:2048].reshape(-1, 128).T,
                  "nrm": _pk(A_(norm1[li])), "win": A_(w_in[li])} for k in range(8)]
        zTs = [r["zT"] for r in _run("A", mapsA)]
        zc = np.concatenate([z[:, :32] for z in zTs], axis=1).T
        zl = np.concatenate([z[:, 32:] for z in zTs], axis=1).T
        xa0, xa1 = _streams(zc[:, 0:512], zl[:, 0:512])
        mapsB1 = []
        for h in range(8):
            sl = slice(64 * h, 64 * h + 64)
            w = A_(lru_conv_w[li])[:, sl]
            z0 = np.zeros(64, f32)
            cw5 = np.concatenate([np.stack([w[0], w[1], w[2], w[3], z0], 1), np.stack([z0, w[3], w[2], w[1], w[0]], 1)], 0)
            vec = lambda a: np.concatenate([A_(a[li])[0][sl], A_(a[li])[1][sl]])
            cb = np.concatenate([A_(lru_conv_b[li])[sl]] * 2)
            vecs = np.stack([cb, vec(lru_ba), vec(lru_bx), vec(lru_lam)], 1)
            wg = np.stack([np.concatenate([A_(lru_wa[li])[0][h], A_(lru_wa[li])[1][h]], 0),
                           np.concatenate([A_(lru_wx[li])[0][h], A_(lru_wx[li])[1][h]], 0)], 1)
            mapsB1.append({"xs": np.stack([xa0[sl], xa1[sl]]), "cw5": cw5, "vecs": vecs, "wg": wg})
        hs = np.stack([r["hs"] for r in _run("B1", mapsB1)])
        hlf = hs[:, 0].reshape(512, -1)
        hb = hs[:, 1].reshape(512, -1)
        hlb = np.concatenate([hb[:, :256][:, ::-1], hb[:, 256:][:, ::-1]], axis=1)
        o = 1024
        mu = A_(rwkv_mu[li])
        mapsB2 = []
        for h in range(8):
            hsl = slice(64 * h, 64 * h + 64)
            st = []
            for s in range(2):
                cols = np.r_[o + 64 * h:o + 64 * h + 64, o + 512 + 64 * h:o + 512 + 64 * h + 64, o + 1024 + 64 * h:o + 1024 + 64 * h + 64,
                             o + 1536 + 64 * s:o + 1536 + 64 * s + 64, o + 1664 + 64 * s:o + 1664 + 64 * s + 64, o + 1792:o + 1920]
                st.append(_streams(zc[:, cols], zl[:, cols])[s])
            mu5 = np.concatenate([np.stack([mu[64 * h:64 * h + 64], mu[512 + 64 * h:512 + 64 * h + 64], mu[1024 + 64 * h:1024 + 64 * h + 64],
                                            mu[1536 + 64 * s:1536 + 64 * s + 64], mu[1664 + 64 * s:1664 + 64 * s + 64]], 1) for s in range(2)], 0)
            w2a2 = np.concatenate([np.stack([A_(rwkv_w2[li])[s][:, hsl], A_(rwkv_a2[li])[s][:, hsl]], 1) for s in range(2)], 0)
            vecs = np.concatenate([np.stack([A_(rwkv_w0[li])[s][hsl], A_(rwkv_a0[li])[s][hsl], A_(rwkv_kk[li])[hsl], A_(rwkv_ka[li])[hsl],
                                             A_(rwkv_rk[li])[h]], 1) for s in range(2)], 0)
            d = {"zs": np.stack(st), "mu5": mu5, "mug": mu[1792:1920].reshape(128, 1), "w2a2": w2a2, "vecs": vecs,
                 "g2h": A_(rwkv_g2[li])[:, hsl], "lnwb": np.stack([A_(rwkv_lnw[li])[hsl], A_(rwkv_lnb[li])[hsl]], 1)}
            d.update(cB2)
            mapsB2.append(d)
        ybT = np.stack([r["yb"] for r in _run("B2", mapsB2)]).reshape(512, -1)
        us0, us1 = _streams(zc[:, 2944:3456], _col_major(zl[:, 2944:3456]))
        mapsB3 = []
        for h in range(8):
            lamp = np.zeros((128, 8, 3), f32); bx = np.zeros((128, 2, 8, 128), f32); cx = np.zeros((128, 2, 8, 128), f32)
            for s in range(2):
                for gl in range(4):
                    g = 4 * h + gl; u = s * 4 + gl; cs = slice(64 * s + 16 * gl, 64 * s + 16 * gl + 16)
                    for half in range(2):
                        r_ = slice(64 * half, 64 * half + 64)
                        lamp[r_, u, 0] = A_(s5_lam_re[li])[s][g]; lamp[r_, u, 1] = A_(s5_lam_im[li])[s][g]; lamp[r_, u, 2] = A_(s5_log_step[li])[s][g]
                    bre, bim = A_(s5_b_re[li])[s][g], A_(s5_b_im[li])[s][g]
                    cre, cim = A_(s5_c_re[li])[s][g].T, A_(s5_c_im[li])[s][g].T
                    bx[0:64, 0, u, cs] = bre; bx[64:128, 0, u, cs] = bim; bx[0:64, 1, u, cs] = bim; bx[64:128, 1, u, cs] = bre
                    cx[0:64, 0, u, cs] = cre; cx[64:128, 0, u, cs] = cim; cx[0:64, 1, u, cs] = cim; cx[64:128, 1, u, cs] = cre
            dsk = np.zeros((128, 1), f32); dsk[0:64, 0] = A_(s5_d[li])[64 * h:64 * h + 64]
            d = {"us": np.stack([us0[64 * h:64 * h + 64], us1[64 * h:64 * h + 64]]), "lamp": lamp, "bx": bx, "cx": cx, "dsk": dsk}
            d.update(cB3)
            mapsB3.append(d)
        ysr = np.stack([r["ys"] for r in _run("B3", mapsB3)])
        yf = ysr[:, 0].reshape(512, -1); yb_ = ysr[:, 1].reshape(512, -1)
        ysf = np.concatenate([yf[:, :256], _raster_cm(yf[:, 256:])], axis=1)
        ysb = np.concatenate([yb_[:, :256][:, ::-1], _raster_cm(yb_[:, 256:][:, ::-1])], axis=1)
        shard = lambda t: [np.concatenate([t[:, 32 * k:32 * k + 32], t[:, 256 + 2048 * k:256 + 2048 * k + 2048]], axis=1) for k in range(8)]
        s_hlf, s_hlb, s_yb, s_ysf, s_ysb = shard(hlf), shard(hlb), shard(ybT), shard(ysf), shard(ysb)
        mapsC = [{"hT": _tok_shard(ctxT, latT, k), "zT": zTs[k], "hlf": s_hlf[k], "hlb": s_hlb[k], "ybT": s_yb[k], "ysf": s_ysf[k], "ysb": s_ysb[k],
                  "cnd": cnd, "wmod": wm[:, 2048:6144], "bmod": bm[2048:6144].reshape(-1, 128).T, "nrm2": _pk(A_(norm2[li])), "nrmf": _pk(A_(norm_f)),
                  "wglu": A_(s5_w_glu[li]), "bglu": A_(s5_b_glu[li]).reshape(4, 128).T, "wbr": A_(w_branch[li]), "wout": A_(w_out[li]),
                  "wfi": A_(w_ffn_in[li]), "wfo": A_(w_ffn_out[li])} for k in range(8)]
        resC = _run("C", mapsC)
        ctxT = np.concatenate([r["hN"][:, :32] for r in resC], axis=1)
        latT = np.concatenate([r["hN"][:, 32:] for r in resC], axis=1)
        hF = np.concatenate([r["hF"][:, 32:] for r in resC], axis=1)
        if _debug is not None:
            _debug.append({"latT": latT, "ctxT": ctxT, "hlf": hlf, "hlb": hlb, "ybT": ybT, "ysf": ysf, "ysb": ysb})
    return np.ascontiguousarray(hF.T).reshape(1, 16384, 1024).astype(np.float32)
```

```python
import math
import numpy as np
from contextlib import ExitStack
import concourse.bass as bass
import concourse.mybir as mybir
from concourse.bass_utils import run_bass_kernel_spmd

F32 = mybir.dt.float32
BF16 = mybir.dt.bfloat16
AF = mybir.ActivationFunctionType
ALU = mybir.AluOpType
AX = mybir.AxisListType

ENGS = ("tensor", "vector", "scalar", "gpsimd", "sync")
NDMASEM = 12


class Cell:
    __slots__ = ("w", "rs")

    def __init__(self):
        self.w = None
        self.rs = {}


class Tile:
    def __init__(self, ap, cells=None):
        self.ap = ap
        self.cells = cells if cells is not None else [Cell()]

    def __getitem__(self, key):
        return self.ap[key]

    def view(self, ap, cells=None):
        return Tile(ap, self.cells if cells is None else cells)


class Prog:
    def __init__(self):
        self.nc = bass.Bass("TRN2", target_bir_lowering=False)
        self.stack = ExitStack()
        self.q = {e: [] for e in ENGS}
        self.count = {e: 0 for e in ENGS}
        self.waited = {e: {} for e in ENGS}
        self.esem = {}
        for e in ENGS:
            self.esem[e] = self.stack.enter_context(self.nc.semaphore("s_" + e))
        self.dsem = {}
        self.dcount = {}
        self.drr = {}
        for e in ("sync", "gpsimd", "scalar"):
            self.dsem[e] = [self.stack.enter_context(self.nc.semaphore("d_%s_%d" % (e, i))) for i in range(NDMASEM)]
            self.dcount[e] = [0] * NDMASEM
            self.drr[e] = 0
        self.nt = 0
        self.ninst = 0

    def dram(self, name, shape, dt, kind):
        return self.nc.dram_tensor(name, list(shape), dt, kind=kind).ap()

    def sb(self, shape, dt=F32, name=None):
        self.nt += 1
        t = self.stack.enter_context(self.nc.sbuf_tensor(name or ("t%d" % self.nt), list(shape), dt))
        return Tile(t[:] if False else t)

    def ps(self, shape=(128, 512), dt=F32, name=None):
        self.nt += 1
        t = self.stack.enter_context(self.nc.psum_tensor(name or ("p%d" % self.nt), list(shape), dt))
        return Tile(t)

    def _need(self, eng, waits, ev):
        sem, val, src = ev
        if src == eng == "tensor":
            return
        key = id(sem)
        if self.waited[eng].get(key, 0) >= val:
            return
        cur = waits.get(key)
        if cur is None or cur[1] < val:
            waits[key] = (sem, val)

    def _deps(self, eng, reads, writes):
        waits = {}
        for t in reads:
            for c in t.cells:
                if c.w is not None:
                    self._need(eng, waits, c.w)
        for t in writes:
            for c in t.cells:
                if c.w is not None:
                    self._need(eng, waits, c.w)
                for r in c.rs.values():
                    self._need(eng, waits, r)
        for key, (sem, val) in waits.items():
            self.waited[eng][key] = val
        return list(waits.values())

    def _mark(self, ev, reads, writes):
        wset = set()
        for t in writes:
            for c in t.cells:
                c.w = ev
                c.rs = {}
                wset.add(id(c))
        for t in reads:
            for c in t.cells:
                if id(c) in wset:
                    continue
                c.rs[id(ev[0])] = ev

    def op(self, eng, fn, reads=(), writes=()):
        waits = self._deps(eng, reads, writes)
        self.count[eng] += 1
        ev = (self.esem[eng], self.count[eng], eng)
        self._mark(ev, reads, writes)
        self.q[eng].append((waits, fn, (self.esem[eng], 1)))
        self.ninst += 1

    def dma(self, eng, out, in_, reads=(), writes=(), **kw):
        waits = self._deps(eng, reads, writes)
        i = self.drr[eng]
        self.drr[eng] = (i + 1) % NDMASEM
        sem = self.dsem[eng][i]
        prev = self.dcount[eng][i]
        if prev > 0 and self.waited[eng].get(id(sem), 0) < prev:
            waits.append((sem, prev))
            self.waited[eng][id(sem)] = prev
        self.dcount[eng][i] = prev + 16
        ev = (sem, prev + 16, "dma")
        self._mark(ev, reads, writes)
        self.q[eng].append((waits, (lambda e, o=out, i_=in_, k=kw: e.dma_start(out=o, in_=i_, **k)), (sem, 16)))
        self.ninst += 1

    def barrier(self):
        evs = [(self.esem[e], self.count[e]) for e in ENGS if self.count[e] > 0]
        for e in self.dsem:
            for sm, c in zip(self.dsem[e], self.dcount[e]):
                if c > 0:
                    evs.append((sm, c))
        for eng in ENGS:
            waits = []
            for sm, val in evs:
                if sm is self.esem[eng]:
                    continue
                if self.waited[eng].get(id(sm), 0) < val:
                    waits.append((sm, val))
                    self.waited[eng][id(sm)] = val
            if waits:
                self.q[eng].append((waits, None, None))

    def finish(self):
        fin = []
        for e in self.dsem:
            for s, c in zip(self.dsem[e], self.dcount[e]):
                if c > 0:
                    fin.append((s, c))
        nc = self.nc
        q = self.q
        with nc.Block() as block:
            def emit(engname, eng):
                for waits, fn, inc in q[engname]:
                    for sem, val in waits:
                        eng.wait_ge(sem, val)
                    if fn is None:
                        continue
                    ins = fn(eng)
                    ins.then_inc(inc[0], inc[1])
                if engname == "sync":
                    for s, c in fin:
                        eng.wait_ge(s, c)

            @block.tensor
            def _(e):
                emit("tensor", e)

            @block.vector
            def _(e):
                emit("vector", e)

            @block.scalar
            def _(e):
                emit("scalar", e)

            @block.gpsimd
            def _(e):
                emit("gpsimd", e)

            @block.sync
            def _(e):
                emit("sync", e)
        self.stack.close()
        return nc


def rev_ap(ap):
    aps = [list(x) for x in ap.ap]
    step, cnt = aps[-1]
    off = ap.offset + step * (cnt - 1)
    aps[-1] = [-step, cnt]
    return bass.AP(ap.tensor, off, aps)


NTOK = 2080
CH = [(0, 32)] + [(32 + 512 * i, 512) for i in range(4)]
D = 1024
KT = 8
N_IN = 6528
EPS = 1e-6


def mod_vectors(P, wmod_d, bmod_sb, cond, ncolblk, wbuf, psum, out_tile):
    G = 4
    for g0 in range(0, ncolblk, G):
        wb = wbuf[(g0 // G) % len(wbuf)]
        P.dma("sync", wb[:, :, 0:G * 128],
              wmod_d[:, g0 * 128:(g0 + G) * 128].rearrange("(kt p) c -> p kt c", p=128), writes=[wb])
        for j in range(g0, g0 + G):
            for kt in range(KT):
                P.op("tensor", (lambda e, j=j, kt=kt, wb=wb, g0=g0: e.matmul(
                    psum[:, 2 * j:2 * j + 2], lhsT=wb[:, kt, (j - g0) * 128:(j - g0 + 1) * 128],
                    rhs=cond[:, kt, :], start=(kt == 0), stop=(kt == KT - 1))),
                    reads=[wb, cond], writes=[psum])
    for which in range(2):
        P.op("vector", (lambda e, which=which: e.tensor_tensor(
            out_tile[:, 0:ncolblk, which], psum[:, which:2 * ncolblk:2], bmod_sb[:, 0:ncolblk], ALU.add)),
            reads=[psum, bmod_sb], writes=[out_tile])


def rms_modulate(P, x, ones, psb, sq, rstd, gsc, shf, epst, xout=None):
    for ci, (o, n) in enumerate(CH):
        which = 1 if ci == 0 else 0
        ps = psb[ci % len(psb)]
        for kt in range(KT):
            s = sq[kt % len(sq)]
            P.op("scalar", (lambda e, s=s, kt=kt, o=o, n=n: e.activation(out=s[:, 0:n], in_=x[:, kt, o:o + n], func=AF.Square)),
                 reads=[x], writes=[s])
            P.op("tensor", (lambda e, s=s, kt=kt, n=n, ps=ps: e.matmul(ps[:, 0:n], lhsT=ones[:, :], rhs=s[:, 0:n],
                                                                    start=(kt == 0), stop=(kt == KT - 1))),
                 reads=[s, ones], writes=[ps])
        r = rstd[ci % len(rstd)]
        P.op("scalar", (lambda e, r=r, ps=ps, n=n: e.activation(out=r[:, 0:n], in_=ps[:, 0:n], func=AF.Sqrt, bias=epst[:, 0:1], scale=1.0 / D)),
             reads=[ps, epst], writes=[r])
        P.op("vector", (lambda e, r=r, n=n: e.reciprocal(r[:, 0:n], r[:, 0:n])),
             reads=[r], writes=[r])
        for kt in range(KT):
            eng = "vector" if kt % 2 == 0 else "gpsimd"
            P.op(eng, (lambda e, kt=kt, o=o, n=n, r=r: e.tensor_tensor(x[:, kt, o:o + n], x[:, kt, o:o + n], r[:, 0:n], ALU.mult)),
                 reads=[x, r], writes=[x])
            xo = x if xout is None else xout
            P.op(eng, (lambda e, kt=kt, o=o, n=n, which=which, xo=xo: e.tensor_scalar(
                xo[:, kt, o:o + n], x[:, kt, o:o + n], gsc[:, kt, which:which + 1], shf[:, kt, which:which + 1], ALU.mult, ALU.add)),
                reads=[x, gsc, shf], writes=[xo])


def build_A():
    P = Prog()
    xT = P.dram("xT", [D, NTOK], F32, "ExternalInput")
    cnd = P.dram("cnd", [128, KT, 2], F32, "ExternalInput")
    wmod = P.dram("wmod", [D, 2048], F32, "ExternalInput")
    bmod = P.dram("bmod", [128, 16], F32, "ExternalInput")
    nrm = P.dram("nrm", [128, KT], F32, "ExternalInput")
    win = P.dram("win", [D, N_IN], F32, "ExternalInput")
    zT = P.dram("zT", [N_IN, NTOK], F32, "ExternalOutput")

    x = P.sb([128, KT, NTOK])
    cond = P.sb([128, KT, 2])
    bmod_sb = P.sb([128, 16])
    nrm_sb = P.sb([128, KT])
    ones = P.sb([128, 128])
    modv = P.sb([128, 16, 2])
    gsc = P.sb([128, KT, 2])
    wbuf = [P.sb([128, KT, 512]) for _ in range(2)]
    sq = [P.sb([128, 512]) for _ in range(2)]
    rstd = [P.sb([128, 512]) for _ in range(2)]
    zo = [P.sb([128, NTOK]) for _ in range(2)]
    psb = [P.ps() for _ in range(6)]
    psm = P.ps()

    P.dma("gpsimd", x[:, :, :], xT.rearrange("(kt p) t -> p kt t", p=128), writes=[x])
    P.dma("sync", cond[:], cnd, writes=[cond])
    P.dma("sync", bmod_sb[:], bmod, writes=[bmod_sb])
    P.dma("sync", nrm_sb[:], nrm, writes=[nrm_sb])
    P.op("vector", lambda e: e.memset(ones[:], 1.0), writes=[ones])
    P.op("scalar", lambda e: e.activation(out=cond[:], in_=cond[:], func=AF.Silu), reads=[cond], writes=[cond])
    mod_vectors(P, wmod, bmod_sb, cond, 16, wbuf, psm, modv)
    for which in range(2):
        P.op("vector", (lambda e, which=which: e.scalar_tensor_tensor(
            gsc[:, :, which], modv[:, 8:16, which], 1.0, nrm_sb[:, :], ALU.add, ALU.mult)),
            reads=[modv, nrm_sb], writes=[gsc])
    shf = modv.view(modv.ap)
    epst = P.sb([128, 1])
    P.op('vector', lambda e: e.memset(epst[:], EPS), writes=[epst])
    xb = P.sb([128, KT, NTOK], BF16)
    wbf = [P.sb([128, KT, 384], BF16) for _ in range(2)]
    rms_modulate(P, x, ones, psb[0:2], sq, rstd, gsc, shf, epst, xout=xb)

    CB = 384
    nblk = N_IN // CB
    pi = 0
    for b in range(nblk):
        wb = wbuf[b % 2]
        P.dma("sync", wb[:, :, 0:CB], win[:, b * CB:(b + 1) * CB].rearrange("(kt p) c -> p kt c", p=128), writes=[wb])
        wq = wbf[b % 2]
        P.op("gpsimd", (lambda e, wq=wq, wb=wb: e.tensor_copy(wq[:, :, :], wb[:, :, 0:CB])), reads=[wb], writes=[wq])
        for sblk in range(CB // 128):
            col0 = b * CB + sblk * 128
            z = zo[(b * 3 + sblk) % 2]
            for ci, (o, n) in enumerate(CH):
                ps = psb[pi % len(psb)]
                pi += 1
                for kt in range(KT):
                    P.op("tensor", (lambda e, ps=ps, wq=wq, kt=kt, sblk=sblk, o=o, n=n: e.matmul(
                        ps[:, 0:n], lhsT=wq[:, kt, sblk * 128:(sblk + 1) * 128], rhs=xb[:, kt, o:o + n],
                        start=(kt == 0), stop=(kt == KT - 1))), reads=[wq, xb], writes=[ps])
                if ci % 2 == 0:
                    P.op("scalar", (lambda e, z=z, ps=ps, o=o, n=n: e.activation(out=z[:, o:o + n], in_=ps[:, 0:n], func=AF.Copy)),
                         reads=[ps], writes=[z])
                else:
                    P.op("vector", (lambda e, z=z, ps=ps, o=o, n=n: e.tensor_copy(z[:, o:o + n], ps[:, 0:n])),
                         reads=[ps], writes=[z])
            P.dma("gpsimd", zT[col0:col0 + 128, :], z[:, :], reads=[z])
    return P.finish()


T_ALL = 16640
SEQS = [(0, 256), (256, 16640)]


def blocks(tb):
    out = []
    for (s0, s1) in SEQS:
        t = s0
        while t < s1:
            n = min(tb, s1 - t)
            out.append((t, n, s0, s1))
            t += n
    return out


def build_B1():
    P = Prog()
    TB = 2048
    xs = P.dram("xs", [2, 64, T_ALL], F32, "ExternalInput")
    cw5 = P.dram("cw5", [128, 5], F32, "ExternalInput")
    vecs = P.dram("vecs", [128, 4], F32, "ExternalInput")
    wg = P.dram("wg", [128, 2, 64], F32, "ExternalInput")
    hs = P.dram("hs", [2, 64, T_ALL], F32, "ExternalOutput")

    cw = P.sb([128, 5]); vc = P.sb([128, 4]); wgs = P.sb([128, 2, 64])
    cl = P.sb([128, 2])
    xp = [P.sb([128, TB + 4]) for _ in range(2)]
    xc = P.sb([128, TB]); gr = P.sb([128, TB]); gi = P.sb([128, TB]); aa = P.sb([128, TB]); uu = P.sb([128, TB])
    hb = [P.sb([128, TB]) for _ in range(2)]
    h0 = P.sb([128, 1])
    psa = [P.ps() for _ in range(2)]
    psx = [P.ps() for _ in range(2)]

    P.dma("sync", cw[:], cw5, writes=[cw])
    P.dma("sync", vc[:], vecs, writes=[vc])
    P.dma("sync", wgs[:], wg, writes=[wgs])
    P.op("vector", lambda e: e.memset(h0[:], 0.0), writes=[h0])
    P.op("scalar", lambda e: e.activation(out=cl[:, 0:1], in_=vc[:, 3:4], func=AF.Exp, scale=-1.0), reads=[vc], writes=[cl])
    P.op("vector", lambda e: e.tensor_scalar(cl[:, 0:1], cl[:, 0:1], 1.0, None, ALU.add), reads=[cl], writes=[cl])
    P.op("scalar", lambda e: e.activation(out=cl[:, 0:1], in_=cl[:, 0:1], func=AF.Ln), reads=[cl], writes=[cl])
    P.op("vector", lambda e: e.tensor_scalar(cl[:, 1:2], cl[:, 0:1], -16.0, None, ALU.mult), reads=[cl], writes=[cl])
    P.op("vector", lambda e: e.tensor_scalar(cl[:, 0:1], cl[:, 0:1], -8.0, None, ALU.mult), reads=[cl], writes=[cl])

    prev_h = None
    for bi, (t0, n, s0, s1) in enumerate(blocks(TB)):
        x = xp[bi % 2]
        h = hb[bi % 2]
        lo = max(t0 - 2, s0); hi = min(t0 + n + 2, s1)
        P.op("gpsimd", lambda e, x=x: e.memset(x[:, 0:2], 0.0), writes=[x])
        P.op("gpsimd", lambda e, x=x, n=n: e.memset(x[:, n + 2:n + 4], 0.0), writes=[x])
        for s in range(2):
            P.dma("sync", x[64 * s:64 * s + 64, 2 + (lo - t0):2 + (hi - t0)], xs[s, :, lo:hi], writes=[x])
        P.op("vector", lambda e, x=x, n=n: e.tensor_scalar(xc[:, 0:n], x[:, 0:n], cw[:, 0:1], vc[:, 0:1], ALU.mult, ALU.add),
             reads=[x, cw, vc], writes=[xc])
        for j in range(1, 5):
            P.op("vector", lambda e, x=x, n=n, j=j: e.scalar_tensor_tensor(xc[:, 0:n], x[:, j:j + n], cw[:, j:j + 1], xc[:, 0:n], ALU.mult, ALU.add),
                 reads=[x, cw, xc], writes=[xc])
        for ci, c0 in enumerate(range(0, n, 512)):
            m = min(512, n - c0)
            pa = psa[ci % 2]; px = psx[ci % 2]
            for s in range(2):
                sl = slice(64 * s, 64 * s + 64)
                P.op("tensor", lambda e, pa=pa, sl=sl, c0=c0, m=m: e.matmul(pa[sl, 0:m], lhsT=wgs[sl, 0, :], rhs=xc[sl, c0:c0 + m], start=True, stop=True),
                     reads=[wgs, xc], writes=[pa])
                P.op("tensor", lambda e, px=px, sl=sl, c0=c0, m=m: e.matmul(px[sl, 0:m], lhsT=wgs[sl, 1, :], rhs=xc[sl, c0:c0 + m], start=True, stop=True),
                     reads=[wgs, xc], writes=[px])
            P.op("scalar", lambda e, pa=pa, c0=c0, m=m: e.activation(out=gr[:, c0:c0 + m], in_=pa[:, 0:m], func=AF.Sigmoid, bias=vc[:, 1:2]),
                 reads=[pa, vc], writes=[gr])
            P.op("scalar", lambda e, px=px, c0=c0, m=m: e.activation(out=gi[:, c0:c0 + m], in_=px[:, 0:m], func=AF.Sigmoid, bias=vc[:, 2:3]),
                 reads=[px, vc], writes=[gi])
        P.op("scalar", lambda e, n=n: e.activation(out=aa[:, 0:n], in_=gr[:, 0:n], func=AF.Exp, scale=cl[:, 0:1]), reads=[gr, cl], writes=[aa])
        P.op("scalar", lambda e, n=n: e.activation(out=gr[:, 0:n], in_=gr[:, 0:n], func=AF.Exp, scale=cl[:, 1:2]), reads=[gr, cl], writes=[gr])
        P.op("gpsimd", lambda e, n=n: e.tensor_tensor(uu[:, 0:n], gi[:, 0:n], xc[:, 0:n], ALU.mult), reads=[gi, xc], writes=[uu])
        P.op("vector", lambda e, n=n: e.tensor_scalar(gr[:, 0:n], gr[:, 0:n], -1.0, 1.0, ALU.mult, ALU.add), reads=[gr], writes=[gr])
        P.op("scalar", lambda e, n=n: e.activation(out=gr[:, 0:n], in_=gr[:, 0:n], func=AF.Sqrt), reads=[gr], writes=[gr])
        P.op("vector", lambda e, n=n: e.tensor_tensor(uu[:, 0:n], uu[:, 0:n], gr[:, 0:n], ALU.mult), reads=[uu, gr], writes=[uu])
        init = h0 if prev_h is None else prev_h[0]
        init_ap = h0[:, 0:1] if prev_h is None else prev_h[0][:, prev_h[1] - 1:prev_h[1]]
        P.op("vector", lambda e, h=h, n=n, init_ap=init_ap: e.tensor_tensor_scan(h[:, 0:n], aa[:, 0:n], uu[:, 0:n], init_ap, ALU.mult, ALU.add),
             reads=[aa, uu, init], writes=[h])
        prev_h = (h, n)
        for s in range(2):
            P.dma("gpsimd", hs[s, :, t0:t0 + n], h[64 * s:64 * s + 64, 0:n], reads=[h])
    return P.finish()


C0 = 0.6065306597126334
CL = 64
GN_EPS = 64e-5


def build_B2(nblk=99, phase2=True, nchunk=99, ndbl=5, stage=9, sub=3):
    P = Prog()
    TB = 1024
    zs = P.dram("zs", [2, 448, T_ALL], F32, "ExternalInput")
    mu5 = P.dram("mu5", [128, 5], F32, "ExternalInput")
    mug = P.dram("mug", [128, 1], F32, "ExternalInput")
    w2a2 = P.dram("w2a2", [128, 2, 64], F32, "ExternalInput")
    vecs = P.dram("vecs", [128, 5], F32, "ExternalInput")
    g2h = P.dram("g2h", [128, 64], F32, "ExternalInput")
    lnwb = P.dram("lnwb", [64, 2], F32, "ExternalInput")
    mask_d = P.dram("mask", [128, 320], F32, "ExternalInput")
    ident_d = P.dram("ident", [128, 128], F32, "ExternalInput")
    cmask_d = P.dram("cmask", [128, TB], F32, "ExternalInput")
    ident2_d = P.dram("ident2", [128, 2, 64], F32, "ExternalInput")
    ys_d = P.dram("ys_scr", [2, 64, T_ALL], F32, "Internal")
    bon_d = P.dram("bon_scr", [2, 64, T_ALL], F32, "Internal")
    gg_d = P.dram("gg_scr", [64, T_ALL], F32, "Internal")
    yb_d = P.dram("yb", [64, T_ALL], F32, "ExternalOutput")

    mu = P.sb([128, 5]); hmu = P.sb([128, 5]); omm = P.sb([128, 5])
    mg = P.sb([128, 1]); hmg = P.sb([128, 1]); omg = P.sb([128, 1])
    wl = P.sb([128, 2, 64]); vc = P.sb([128, 5]); omka = P.sb([128, 1]); g2 = P.sb([128, 64]); lnw = P.sb([64, 2])
    mask = P.sb([128, 320]); ident = P.sb([128, 128]); cmask = P.sb([128, TB]); ident2 = P.sb([128, 2, 64])
    ones = P.sb([128, 64]); rkm = P.sb([128, 64]); e12 = P.sb([128, 1]); egn = P.sb([128, 1]); o64 = P.sb([128, 64])
    IN = P.sb([128, 5, TB + 2]); GD = P.sb([128, TB + 2])
    SH = P.sb([128, 5, TB]); TW = P.sb([128, TB]); SG = P.sb([128, TB]); IC = P.sb([128, TB])
    KKr = P.sb([128, TB]); TMP = P.sb([128, TB]); RN = P.sb([128, TB]); KKN = P.sb([128, TB]); KD = P.sb([128, TB])
    CS = P.sb([128, TB]); EW = P.sb([128, TB]); EWI = P.sb([128, TB]); EWM = P.sb([128, TB])
    OPS = P.sb([128, (TB // CL) * 4 * CL])
    SGD = P.sb([128, TB]); G = P.sb([64, TB]); BV = P.sb([128, TB]); YO = [P.sb([128, TB]) for _ in range(2)]
    Mall = [P.sb([128, 2, 320]) for _ in range(2)]
    TM = [P.sb([128, 2, 192]) for _ in range(2)]
    PP = [P.sb([128, 2, 128]) for _ in range(2)]
    TT_ = [P.sb([128, 2, 64]) for _ in range(2)]
    XnT = P.sb([128, 64]); SAnT = P.sb([128, 64])
    S0T = [P.sb([128, 64]) for _ in range(2)]
    bank = [P.ps() for _ in range(8)]

    PMg = [bank[0], bank[1]]
    PD = bank[2]; PT = bank[3]
    PX = PS = bank[4]
    PST = bank[5]
    PY = [bank[6], bank[6]]
    PW = bank[7]; PA = PD
    PTR = PW

    for (t, d) in ((mu, mu5), (mg, mug), (wl, w2a2), (vc, vecs), (g2, g2h), (lnw, lnwb), (mask, mask_d), (ident, ident_d), (cmask, cmask_d), (ident2, ident2_d)):
        P.dma("sync", t[:], d, writes=[t])
    V = "vector"
    P.op(V, lambda e: e.tensor_scalar(hmu[:], mu[:], 0.5, None, ALU.mult), reads=[mu], writes=[hmu])
    P.op(V, lambda e: e.tensor_scalar(omm[:], mu[:], -1.0, 1.0, ALU.mult, ALU.add), reads=[mu], writes=[omm])
    P.op(V, lambda e: e.tensor_scalar(hmg[:], mg[:], 0.5, None, ALU.mult), reads=[mg], writes=[hmg])
    P.op(V, lambda e: e.tensor_scalar(omg[:], mg[:], -1.0, 1.0, ALU.mult, ALU.add), reads=[mg], writes=[omg])
    P.op(V, lambda e: e.tensor_scalar(omka[:], vc[:, 3:4], -1.0, 1.0, ALU.mult, ALU.add), reads=[vc], writes=[omka])
    P.op(V, lambda e: e.memset(ones[:], 1.0), writes=[ones])
    P.op(V, lambda e: e.memset(o64[:], 1.0 / 64), writes=[o64])
    P.op(V, lambda e: e.memset(e12[:], 1e-12), writes=[e12])
    P.op(V, lambda e: e.memset(egn[:], GN_EPS), writes=[egn])
    P.op(V, lambda e: e.tensor_scalar(rkm[:], ones[:], vc[:, 4:5], None, ALU.mult), reads=[ones, vc], writes=[rkm])
    P.op(V, lambda e: e.memset(S0T[0][:], 0.0), writes=[S0T[0]])

    sidx = 0
    cidx = 0
    for bi, (t0, n, s0, s1) in enumerate(blocks(TB)[:nblk]):
        nch = n // CL
        lo = max(t0 - 1, s0); hi = min(t0 + n + 1, s1)
        P.op("gpsimd", lambda e: e.memset(IN[:, :, 0:1], 0.0), writes=[IN])
        P.op("gpsimd", lambda e, n=n: e.memset(IN[:, :, n + 1:n + 2], 0.0), writes=[IN])
        P.op("gpsimd", lambda e: e.memset(GD[:, 0:1], 0.0), writes=[GD])
        P.op("gpsimd", lambda e, n=n: e.memset(GD[:, n + 1:n + 2], 0.0), writes=[GD])
        for s in range(2):
            P.dma("sync", IN[64 * s:64 * s + 64, :, 1 + (lo - t0):1 + (hi - t0)],
                  zs[s, 0:320, lo:hi].rearrange("(a p) t -> p a t", p=64), writes=[IN])
        P.dma("sync", GD[:, 1 + (lo - t0):1 + (hi - t0)], zs[0, 320:448, lo:hi], writes=[GD])
        for a in range(5):
            eng = "gpsimd" if a % 2 else "vector"
            P.op(eng, lambda e, a=a, n=n: e.tensor_tensor(SH[:, a, 0:n], IN[:, a, 0:n], IN[:, a, 2:n + 2], ALU.add), reads=[IN], writes=[SH])
            P.op(eng, lambda e, a=a, n=n: e.tensor_scalar(SH[:, a, 0:n], SH[:, a, 0:n], hmu[:, a:a + 1], None, ALU.mult), reads=[SH, hmu], writes=[SH])
            P.op("vector", lambda e, a=a, n=n: e.scalar_tensor_tensor(SH[:, a, 0:n], IN[:, a, 1:n + 1], omm[:, a:a + 1], SH[:, a, 0:n], ALU.mult, ALU.add),
                 reads=[IN, omm, SH], writes=[SH])
        P.op("gpsimd", lambda e, n=n: e.tensor_tensor(SGD[:, 0:n], GD[:, 0:n], GD[:, 2:n + 2], ALU.add), reads=[GD], writes=[SGD])
        P.op("gpsimd", lambda e, n=n: e.tensor_scalar(SGD[:, 0:n], SGD[:, 0:n], hmg[:, 0:1], None, ALU.mult), reads=[SGD, hmg], writes=[SGD])
        P.op("vector", lambda e, n=n: e.scalar_tensor_tensor(SGD[:, 0:n], GD[:, 1:n + 1], omg[:, 0:1], SGD[:, 0:n], ALU.mult, ALU.add),
             reads=[GD, omg, SGD], writes=[SGD])
        P.op("scalar", lambda e, n=n: e.activation(out=SGD[:, 0:n], in_=SGD[:, 0:n], func=AF.Sigmoid), reads=[SGD], writes=[SGD])
        P.op("scalar", lambda e, n=n: e.activation(out=TW[:, 0:n], in_=SH[:, 3, 0:n], func=AF.Tanh), reads=[SH], writes=[TW])
        P.op("vector", lambda e, n=n: e.tensor_scalar(KKr[:, 0:n], SH[:, 1, 0:n], vc[:, 2:3], None, ALU.mult), reads=[SH, vc], writes=[KKr])
        P.op("gpsimd", lambda e, n=n: e.tensor_tensor(TMP[:, 0:n], KKr[:, 0:n], KKr[:, 0:n], ALU.mult), reads=[KKr], writes=[TMP])
        for c0 in range(0, n, 512):
            m = min(512, n - c0)
            for s in range(2):
                sl = slice(64 * s, 64 * s + 64)
                P.op("tensor", lambda e, sl=sl, c0=c0, m=m: e.matmul(PW[sl, 0:m], lhsT=wl[sl, 0, :], rhs=TW[sl, c0:c0 + m], start=True, stop=True),
                     reads=[wl, TW], writes=[PW])
            P.op("scalar", lambda e, c0=c0, m=m: e.activation(out=SG[:, c0:c0 + m], in_=PW[:, 0:m], func=AF.Sigmoid, bias=vc[:, 0:1]),
                 reads=[PW, vc], writes=[SG])
            for s in range(2):
                sl = slice(64 * s, 64 * s + 64)
                P.op("tensor", lambda e, sl=sl, c0=c0, m=m: e.matmul(PA[sl, 0:m], lhsT=wl[sl, 1, :], rhs=SH[sl, 4, c0:c0 + m], start=True, stop=True),
                     reads=[wl, SH], writes=[PA])
            P.op("scalar", lambda e, c0=c0, m=m: e.activation(out=IC[:, c0:c0 + m], in_=PA[:, 0:m], func=AF.Sigmoid, bias=vc[:, 1:2]),
                 reads=[PA, vc], writes=[IC])
            for s in range(2):
                sl = slice(64 * s, 64 * s + 64)
                P.op("tensor", lambda e, sl=sl, c0=c0, m=m: e.matmul(PW[sl, 0:m], lhsT=ones[sl, :], rhs=TMP[sl, c0:c0 + m], start=True, stop=True),
                     reads=[ones, TMP], writes=[PW])
            P.op("scalar", lambda e, c0=c0, m=m: e.activation(out=RN[:, c0:c0 + m], in_=PW[:, 0:m], func=AF.Sqrt, bias=e12[:, 0:1]),
                 reads=[PW, e12], writes=[RN])
            P.op("tensor", lambda e, c0=c0, m=m: e.matmul(PA[0:64, 0:m], lhsT=g2[:, :], rhs=SGD[:, c0:c0 + m], start=True, stop=True),
                 reads=[g2, SGD], writes=[PA])
            P.op("scalar", lambda e, c0=c0, m=m: e.activation(out=G[:, c0:c0 + m], in_=PA[0:64, 0:m], func=AF.Copy), reads=[PA], writes=[G])
        P.dma("gpsimd", gg_d[:, t0:t0 + n], G[:, 0:n], reads=[G])
        P.op("vector", lambda e, n=n: e.reciprocal(RN[:, 0:n], RN[:, 0:n]), reads=[RN], writes=[RN])
        P.op("vector", lambda e, n=n: e.tensor_tensor(KKN[:, 0:n], KKr[:, 0:n], RN[:, 0:n], ALU.mult), reads=[KKr, RN], writes=[KKN])
        P.op("vector", lambda e, n=n: e.tensor_scalar(KD[:, 0:n], IC[:, 0:n], vc[:, 3:4], omka[:, 0:1], ALU.mult, ALU.add), reads=[IC, vc, omka], writes=[KD])
        P.op("gpsimd", lambda e, n=n: e.tensor_tensor(KD[:, 0:n], KD[:, 0:n], SH[:, 1, 0:n], ALU.mult), reads=[KD, SH], writes=[KD])
        P.op("gpsimd", lambda e, n=n: e.tensor_tensor(TMP[:, 0:n], SH[:, 0, 0:n], KD[:, 0:n], ALU.mult), reads=[SH, KD], writes=[TMP])
        for c0 in range(0, n, 512):
            m = min(512, n - c0)
            for s in range(2):
                sl = slice(64 * s, 64 * s + 64)
                P.op("tensor", lambda e, sl=sl, c0=c0, m=m: e.matmul(PW[sl, 0:m], lhsT=rkm[sl, :], rhs=TMP[sl, c0:c0 + m], start=True, stop=True),
                     reads=[rkm, TMP], writes=[PW])
            P.op("vector", lambda e, c0=c0, m=m: e.tensor_tensor(BV[:, c0:c0 + m], PW[:, 0:m], SH[:, 2, c0:c0 + m], ALU.mult), reads=[PW, SH], writes=[BV])
        for s in range(2):
            P.dma("gpsimd", bon_d[s, :, t0:t0 + n], BV[64 * s:64 * s + 64, 0:n], reads=[BV])
        P.op("vector", lambda e, n=n: e.tensor_tensor_scan(CS[:, 0:n], cmask[:, 0:n], SG[:, 0:n], 0.0, ALU.mult, ALU.add), reads=[cmask, SG], writes=[CS])
        P.op("scalar", lambda e, n=n: e.activation(out=EW[:, 0:n], in_=CS[:, 0:n], func=AF.Exp, scale=-C0), reads=[CS], writes=[EW])
        P.op("scalar", lambda e, n=n: e.activation(out=EWI[:, 0:n], in_=CS[:, 0:n], func=AF.Exp, scale=C0), reads=[CS], writes=[EWI])
        P.op("vector", lambda e, n=n: e.tensor_tensor(EWM[:, 0:n], CS[:, 0:n], SG[:, 0:n], ALU.subtract), reads=[CS, SG], writes=[EWM])
        P.op("scalar", lambda e, n=n: e.activation(out=EWM[:, 0:n], in_=EWM[:, 0:n], func=AF.Exp, scale=-C0), reads=[EWM], writes=[EWM])
        ch3 = lambda t, n=n: t[:, 0:n].rearrange("p (c t) -> p c t", t=CL)
        P.op("vector", lambda e, nch=nch, ch3=ch3: e.tensor_tensor(OPS[:, 0:nch * 256].rearrange("p (c a t) -> p c a t", a=4, t=CL)[:, :, 0, :], ch3(KKN), ch3(EWM), ALU.mult), reads=[KKN, EWM], writes=[OPS])
        P.op("gpsimd", lambda e, nch=nch, n=n, ch3=ch3: e.tensor_tensor(OPS[:, 0:nch * 256].rearrange("p (c a t) -> p c a t", a=4, t=CL)[:, :, 1, :], SH[:, 0, 0:n].rearrange("p (c t) -> p c t", t=CL), ch3(EW), ALU.mult),
             reads=[SH, EW], writes=[OPS])
        P.op("vector", lambda e, nch=nch, ch3=ch3: e.tensor_tensor(OPS[:, 0:nch * 256].rearrange("p (c a t) -> p c a t", a=4, t=CL)[:, :, 2, :], ch3(KD), ch3(EWI), ALU.mult), reads=[KD, EWI], writes=[OPS])
        P.op("gpsimd", lambda e, n=n: e.tensor_tensor(TMP[:, 0:n], KKN[:, 0:n], IC[:, 0:n], ALU.mult), reads=[KKN, IC], writes=[TMP])
        P.op("vector", lambda e, nch=nch, ch3=ch3: e.tensor_tensor(OPS[:, 0:nch * 256].rearrange("p (c a t) -> p c a t", a=4, t=CL)[:, :, 3, :], ch3(TMP), ch3(EWI), ALU.mult), reads=[TMP, EWI], writes=[OPS])
        Y = YO[bi % 2]
        for c2 in range(0, nch, 2):
            M = Mall[cidx % 2]; tm = TM[cidx % 2]
            cidx += 1
            for g in range(2):
                c = c2 + g
                pm = PMg[g]
                for s in range(2):
                    sl = slice(64 * s, 64 * s + 64)
                    P.op("tensor", lambda e, sl=sl, c=c, pm=pm: e.matmul(pm[sl, 0:128], lhsT=OPS[sl, c * 256 + 128:c * 256 + 192], rhs=OPS[sl, c * 256:c * 256 + 128], start=True, stop=True),
                         reads=[OPS], writes=[pm])
                    P.op("tensor", lambda e, sl=sl, c=c, pm=pm: e.matmul(pm[sl, 128:256], lhsT=OPS[sl, c * 256 + 192:c * 256 + 256], rhs=OPS[sl, c * 256:c * 256 + 128], start=True, stop=True),
                         reads=[OPS], writes=[pm])
                    P.op("tensor", lambda e, sl=sl, c=c, pm=pm: e.matmul(pm[sl, 256:320], lhsT=OPS[sl, c * 256 + 0:c * 256 + 64], rhs=OPS[sl, c * 256 + 192:c * 256 + 256], start=True, stop=True),
                         reads=[OPS], writes=[pm])
                    o_ = 192 * g
                    P.op("tensor", lambda e, sl=sl, c=c, o_=o_: e.matmul(PTR[sl, o_:o_ + 64], lhsT=OPS[sl, c * 256 + 128:c * 256 + 192], rhs=ident[sl, sl], start=True, stop=True),
                         reads=[OPS, ident], writes=[PTR])
                    P.op("tensor", lambda e, sl=sl, c=c, o_=o_: e.matmul(PTR[sl, o_ + 64:o_ + 128], lhsT=OPS[sl, c * 256 + 192:c * 256 + 256], rhs=ident[sl, sl], start=True, stop=True),
                         reads=[OPS, ident], writes=[PTR])
                    P.op("tensor", lambda e, sl=sl, c=c, o_=o_: e.matmul(PTR[sl, o_ + 128:o_ + 192], lhsT=SH[sl, 2, c * CL:(c + 1) * CL], rhs=ident[sl, sl], start=True, stop=True),
                         reads=[SH, ident], writes=[PTR])
                P.op("vector", lambda e, M=M, pm=pm, g=g: e.tensor_tensor(M[:, g, :], pm[:, 0:320], mask[:, :], ALU.mult), reads=[pm, mask], writes=[M])
            P.op("scalar", lambda e, tm=tm: e.activation(out=tm[:, :, :].rearrange("p g c -> p (g c)"), in_=PTR[:, 0:384], func=AF.Copy), reads=[PTR], writes=[tm])
            pp = PP[0]; tt = TT_[0]
            P.op("gpsimd", lambda e, M=M, pp=pp: e.tensor_copy(pp[:, :, 0:64], M[:, :, 128:192]), reads=[M], writes=[pp])
            P.op("gpsimd", lambda e, M=M, pp=pp: e.tensor_copy(pp[:, :, 64:128], M[:, :, 256:320]), reads=[M], writes=[pp])
            P.op("vector", lambda e, M=M, tt=tt: e.tensor_tensor(tt[:, :, :], M[:, :, 128:192], ident2[:, :, :], ALU.add), reads=[M, ident2], writes=[tt])
            for it in range(ndbl):
                pn = PP[(it + 1) % 2]; tn = TT_[(it + 1) % 2]
                for g in range(2):
                    for s in range(2):
                        sl = slice(64 * s, 64 * s + 64)
                        P.op("tensor", lambda e, sl=sl, pp=pp, g=g: e.matmul(PD[sl, g * 128:g * 128 + 64], lhsT=pp[sl, g, 64:128], rhs=pp[sl, g, 0:64], start=True, stop=True),
                             reads=[pp], writes=[PD])
                        P.op("tensor", lambda e, sl=sl, pp=pp, g=g: e.matmul(PD[sl, g * 128 + 64:g * 128 + 128], lhsT=pp[sl, g, 0:64], rhs=pp[sl, g, 64:128], start=True, stop=True),
                             reads=[pp], writes=[PD])
                P.op("scalar", lambda e, pn=pn: e.activation(out=pn[:, :, :].rearrange("p g c -> p (g c)"), in_=PD[:, 0:256], func=AF.Copy), reads=[PD], writes=[pn])
                for g in range(2):
                    for s in range(2):
                        sl = slice(64 * s, 64 * s + 64)
                        P.op("tensor", lambda e, sl=sl, pn=pn, tt=tt, g=g: e.matmul(PT[sl, g * 64:g * 64 + 64], lhsT=pn[sl, g, 64:128], rhs=tt[sl, g, :], start=True, stop=True),
                             reads=[pn, tt], writes=[PT])
                P.op("vector", lambda e, tn=tn, tt=tt: e.tensor_tensor(tn[:, :, :].rearrange("p g c -> p (g c)"), PT[:, 0:128], tt[:, :, :].rearrange("p g c -> p (g c)"), ALU.add),
                     reads=[PT, tt], writes=[tn])
                pp = pn; tt = tn
            for g in range(2):
                c = c2 + g
                so = S0T[sidx % 2]; sn = S0T[(sidx + 1) % 2]
                sidx += 1
                for s in range(2):
                    sl = slice(64 * s, 64 * s + 64)
                    P.op("tensor", lambda e, sl=sl, c=c, so=so: e.matmul(PX[sl, 0:64], lhsT=OPS[sl, c * 256 + 0:c * 256 + 64], rhs=so[sl, :], start=True, stop=False),
                         reads=[OPS, so], writes=[PX])
                    P.op("tensor", lambda e, sl=sl, M=M, tm=tm, g=g: e.matmul(PX[sl, 0:64], lhsT=M[sl, g, 0:64], rhs=tm[sl, g, 128:192], start=False, stop=True),
                         reads=[M, tm], writes=[PX])
                P.op("scalar", lambda e: e.activation(out=XnT[:, :], in_=PX[:, 0:64], func=AF.Copy, scale=-1.0), reads=[PX], writes=[XnT])
                for s in range(2):
                    sl = slice(64 * s, 64 * s + 64)
                    P.op("tensor", lambda e, sl=sl, tt=tt, g=g: e.matmul(PS[sl, 64:128], lhsT=tt[sl, g, :], rhs=XnT[sl, :], start=True, stop=True),
                         reads=[tt, XnT], writes=[PS])
                P.op("vector", lambda e: e.tensor_copy(SAnT[:, :], PS[:, 64:128]), reads=[PS], writes=[SAnT])
                py = PY[0]
                cc = (c % 8) * CL
                for s in range(2):
                    sl = slice(64 * s, 64 * s + 64)
                    P.op("tensor", lambda e, sl=sl, c=c, so=so, py=py, cc=cc: e.matmul(py[sl, cc:cc + CL], lhsT=so[sl, :], rhs=OPS[sl, c * 256 + 64:c * 256 + 128], start=True, stop=False),
                         reads=[so, OPS], writes=[py])
                    P.op("tensor", lambda e, sl=sl, M=M, tm=tm, py=py, cc=cc, g=g: e.matmul(py[sl, cc:cc + CL], lhsT=tm[sl, g, 128:192], rhs=M[sl, g, 64:128], start=False, stop=False),
                         reads=[M, tm], writes=[py])
                    P.op("tensor", lambda e, sl=sl, M=M, py=py, cc=cc, g=g: e.matmul(py[sl, cc:cc + CL], lhsT=SAnT[sl, :], rhs=M[sl, g, 192:256], start=False, stop=True),
                         reads=[M, SAnT], writes=[py])
                    P.op("tensor", lambda e, sl=sl, so=so: e.matmul(PST[sl, 0:64], lhsT=ident[sl, sl], rhs=so[sl, :], start=True, stop=False),
                         reads=[ident, so], writes=[PST])
                    P.op("tensor", lambda e, sl=sl, tm=tm, g=g: e.matmul(PST[sl, 0:64], lhsT=tm[sl, g, 0:64], rhs=tm[sl, g, 128:192], start=False, stop=False),
                         reads=[tm], writes=[PST])
                    P.op("tensor", lambda e, sl=sl, tm=tm, g=g: e.matmul(PST[sl, 0:64], lhsT=tm[sl, g, 64:128], rhs=SAnT[sl, :], start=False, stop=True),
                         reads=[tm, SAnT], writes=[PST])
                P.op("vector", lambda e, sn=sn, c=c: e.tensor_scalar(sn[:, :], PST[:, 0:64], EW[:, (c + 1) * CL - 1:(c + 1) * CL], None, ALU.mult),
                     reads=[PST, EW], writes=[sn])
                if c % 8 == 7 or c == nch - 1:
                    cb = (c // 8) * 8 * CL
                    w = (c % 8 + 1) * CL
                    P.op("scalar", lambda e, py=py, cb=cb, w=w, Y=Y: e.activation(out=Y[:, cb:cb + w], in_=py[:, 0:w], func=AF.Copy), reads=[py], writes=[Y])
        for s in range(2):
            P.dma("gpsimd", ys_d[s, :, t0:t0 + n], Y[64 * s:64 * s + 64, 0:n], reads=[Y])

    P.barrier()
    TB2 = 1024
    for bi, (t0, n, s0, s1) in enumerate(blocks(TB2)[:nblk] if phase2 else []):
        m0 = s0 + s1 - t0 - n
        A0 = KKr; A1 = TMP; B0 = RN; B1 = KKN; GG = KD; YC = CS; SQ = EW; RS = EWI; OUT = YO[bi % 2]
        P.dma("sync", A0[0:64, 0:n], ys_d[0, :, t0:t0 + n], writes=[A0])
        P.dma("sync", A1[0:64, 0:n], ys_d[1, :, m0:m0 + n], writes=[A1])
        P.dma("sync", B0[0:64, 0:n], bon_d[0, :, t0:t0 + n], writes=[B0])
        P.dma("sync", B1[0:64, 0:n], bon_d[1, :, m0:m0 + n], writes=[B1])
        P.dma("sync", GG[0:64, 0:n], gg_d[:, t0:t0 + n], writes=[GG])
        P.op("vector", lambda e, n=n: e.tensor_tensor(A0[0:64, 0:n], A0[0:64, 0:n], rev_ap(A1[0:64, 0:n]), ALU.add), reads=[A0, A1], writes=[A0])
        P.op("gpsimd", lambda e, n=n: e.tensor_tensor(B0[0:64, 0:n], B0[0:64, 0:n], rev_ap(B1[0:64, 0:n]), ALU.add), reads=[B0, B1], writes=[B0])
        for c0 in range(0, n, 512):
            m = min(512, n - c0)
            P.op("tensor", lambda e, c0=c0, m=m: e.matmul(PW[0:64, 0:m], lhsT=o64[0:64, :], rhs=A0[0:64, c0:c0 + m], start=True, stop=True), reads=[o64, A0], writes=[PW])
            P.op("vector", lambda e, c0=c0, m=m: e.tensor_tensor(YC[0:64, c0:c0 + m], A0[0:64, c0:c0 + m], PW[0:64, 0:m], ALU.subtract), reads=[A0, PW], writes=[YC])
            P.op("scalar", lambda e, c0=c0, m=m: e.activation(out=SQ[0:64, c0:c0 + m], in_=YC[0:64, c0:c0 + m], func=AF.Square), reads=[YC], writes=[SQ])
            P.op("tensor", lambda e, c0=c0, m=m: e.matmul(PA[0:64, 0:m], lhsT=o64[0:64, :], rhs=SQ[0:64, c0:c0 + m], start=True, stop=True), reads=[o64, SQ], writes=[PA])
            P.op("scalar", lambda e, c0=c0, m=m: e.activation(out=RS[0:64, c0:c0 + m], in_=PA[0:64, 0:m], func=AF.Sqrt, bias=egn[0:64, 0:1]), reads=[PA, egn], writes=[RS])
        P.op("vector", lambda e, n=n: e.reciprocal(RS[0:64, 0:n], RS[0:64, 0:n]), reads=[RS], writes=[RS])
        P.op("vector", lambda e, n=n: e.tensor_tensor(YC[0:64, 0:n], YC[0:64, 0:n], RS[0:64, 0:n], ALU.mult), reads=[YC, RS], writes=[YC])
        P.op("vector", lambda e, n=n: e.tensor_scalar(YC[0:64, 0:n], YC[0:64, 0:n], lnw[:, 0:1], lnw[:, 1:2], ALU.mult, ALU.add), reads=[YC, lnw], writes=[YC])
        P.op("gpsimd", lambda e, n=n: e.tensor_tensor(YC[0:64, 0:n], YC[0:64, 0:n], B0[0:64, 0:n], ALU.add), reads=[YC, B0], writes=[YC])
        P.op("vector", lambda e, n=n, OUT=OUT: e.tensor_tensor(OUT[0:64, 0:n], YC[0:64, 0:n], GG[0:64, 0:n], ALU.mult), reads=[YC, GG], writes=[OUT])
        P.dma("gpsimd", yb_d[:, t0:t0 + n], OUT[0:64, 0:n], reads=[OUT])
    return P.finish()


NL = 20
LAGS = list(range(9)) + [8 * 2 ** i for i in range(1, 12)]
NCHK = T_ALL // 8
NBLKS = [(0, 32)] + [(32 + 256 * i, 256) for i in range(8)]


def build_B3(upto=9):
    P = Prog()
    us = P.dram("us", [2, 64, T_ALL], F32, "ExternalInput")
    lamp_d = P.dram("lamp", [128, 8, 3], F32, "ExternalInput")
    bx_d = P.dram("bx", [128, 2, 8, 128], F32, "ExternalInput")
    cx_d = P.dram("cx", [128, 2, 8, 128], F32, "ExternalInput")
    dsk_d = P.dram("dsk", [128, 1], F32, "ExternalInput")
    lagt_d = P.dram("lagt", [128, 8, NL], F32, "ExternalInput")
    sgn_d = P.dram("sgn", [128, 2], F32, "ExternalInput")
    ident_d = P.dram("ident", [128, 128], F32, "ExternalInput")
    jmat_d = P.dram("jmat", [128, 128], F32, "ExternalInput")
    ys = P.dram("ys", [2, 64, T_ALL], F32, "ExternalOutput")

    lamp = P.sb([128, 8, 3]); bx = P.sb([128, 2, 8, 128]); cx = P.sb([128, 2, 8, 128]); dsk = P.sb([128, 1])
    lagt = P.sb([128, 8, NL]); sgn = P.sb([128, 2]); ident = P.sb([128, 128]); jmat = P.sb([128, 128])
    for t, d in ((lamp, lamp_d), (bx, bx_d), (cx, cx_d), (dsk, dsk_d), (lagt, lagt_d), (sgn, sgn_d), (ident, ident_d), (jmat, jmat_d)):
        P.dma("sync", t[:], d, writes=[t])
    V = "vector"
    step = P.sb([128, 8]); xr = P.sb([128, 8]); ang = P.sb([128, 8])
    XR3 = P.sb([128, 8, NL]); AN3 = P.sb([128, 8, NL]); T3 = P.sb([128, 8, NL]); MAG = P.sb([128, 8, NL])
    S3 = P.sb([128, 8, NL]); C3 = P.sb([128, 8, NL]); LR = P.sb([128, 8, NL]); LI = P.sb([128, 8, NL])
    LIA = P.sb([128, 8, NL]); LRB = P.sb([128, 8, NL]); LIN = P.sb([128, 8, NL])
    mpi = P.sb([128, 1])
    P.op(V, lambda e: e.memset(mpi[:], -math.pi), writes=[mpi])
    P.op("scalar", lambda e: e.activation(out=step[:], in_=lamp[:, :, 2], func=AF.Exp), reads=[lamp], writes=[step])
    P.op(V, lambda e: e.tensor_tensor(xr[:], lamp[:, :, 0], step[:], ALU.mult), reads=[lamp, step], writes=[xr])
    P.op(V, lambda e: e.tensor_tensor(ang[:], lamp[:, :, 1], step[:], ALU.mult), reads=[lamp, step], writes=[ang])
    for u in range(8):
        P.op(V, lambda e, u=u: e.tensor_scalar(XR3[:, u, :], lagt[:, u, :], xr[:, u:u + 1], None, ALU.mult), reads=[lagt, xr], writes=[XR3])
        P.op("gpsimd", lambda e, u=u: e.tensor_scalar(AN3[:, u, :], lagt[:, u, :], ang[:, u:u + 1], None, ALU.mult), reads=[lagt, ang], writes=[AN3])
    P.op("scalar", lambda e: e.activation(out=MAG[:], in_=XR3[:], func=AF.Exp), reads=[XR3], writes=[MAG])

    I32 = mybir.dt.int32
    KI3 = P.sb([128, 8 * NL], I32); KF3 = P.sb([128, 8 * NL]); GG3 = P.sb([128, 8 * NL])

    def sin_of(DST_T, dst, SRC_T, src, shift, TMP_T, tmp, n):
        TWO_PI = 2 * math.pi
        P.op(V, lambda e: e.tensor_scalar(tmp, src, 1.0 / TWO_PI, shift / TWO_PI, ALU.mult, ALU.add), reads=[SRC_T], writes=[TMP_T])
        P.op(V, lambda e: e.tensor_copy(KI3[:, 0:n], tmp), reads=[TMP_T], writes=[KI3])
        P.op(V, lambda e: e.tensor_copy(KF3[:, 0:n], KI3[:, 0:n]), reads=[KI3], writes=[KF3])
        P.op(V, lambda e: e.scalar_tensor_tensor(tmp, KF3[:, 0:n], -TWO_PI, src, ALU.mult, ALU.add), reads=[KF3, SRC_T], writes=[TMP_T])
        P.op(V, lambda e: e.tensor_scalar(tmp, tmp, shift, None, ALU.add), reads=[TMP_T], writes=[TMP_T])
        P.op(V, lambda e: e.tensor_scalar(GG3[:, 0:n], tmp, math.pi, TWO_PI, ALU.is_gt, ALU.mult), reads=[TMP_T], writes=[GG3])
        P.op(V, lambda e: e.tensor_tensor(tmp, tmp, GG3[:, 0:n], ALU.subtract), reads=[TMP_T, GG3], writes=[TMP_T])
        P.op(V, lambda e: e.tensor_scalar(GG3[:, 0:n], tmp, -math.pi, TWO_PI, ALU.is_lt, ALU.mult), reads=[TMP_T], writes=[GG3])
        P.op(V, lambda e: e.tensor_tensor(tmp, tmp, GG3[:, 0:n], ALU.add), reads=[TMP_T, GG3], writes=[TMP_T])
        P.op(V, lambda e: e.tensor_scalar(tmp, tmp, math.pi, -math.pi, ALU.min, ALU.max), reads=[TMP_T], writes=[TMP_T])
        P.op("scalar", lambda e: e.activation(out=dst, in_=tmp, func=AF.Sin), reads=[TMP_T], writes=[DST_T])
    sin_of(S3, S3[:, :, :].rearrange("p a b -> p (a b)"), AN3, AN3[:, :, :].rearrange("p a b -> p (a b)"), 0.0, T3, T3[:, :, :].rearrange("p a b -> p (a b)"), 8 * NL)
    sin_of(C3, C3[:, :, :].rearrange("p a b -> p (a b)"), AN3, AN3[:, :, :].rearrange("p a b -> p (a b)"), math.pi / 2, T3, T3[:, :, :].rearrange("p a b -> p (a b)"), 8 * NL)
    P.op(V, lambda e: e.tensor_tensor(LR[:], MAG[:], C3[:], ALU.mult), reads=[MAG, C3], writes=[LR])
    P.op(V, lambda e: e.tensor_tensor(LI[:], MAG[:], S3[:], ALU.mult), reads=[MAG, S3], writes=[LI])
    P.op(V, lambda e: e.tensor_scalar(LIA[:], LI[:], sgn[:, 0:1], None, ALU.mult), reads=[LI, sgn], writes=[LIA])
    P.op(V, lambda e: e.tensor_scalar(LRB[:], LR[:], sgn[:, 1:2], None, ALU.mult), reads=[LR, sgn], writes=[LRB])
    P.op(V, lambda e: e.tensor_scalar(LIN[:], LI[:], -1.0, None, ALU.mult), reads=[LI], writes=[LIN])
    em1 = P.sb([128, 8]); t8 = P.sb([128, 8]); sh = P.sb([128, 8]); nr = P.sb([128, 8]); den = P.sb([128, 8])
    fr = P.sb([128, 8]); fi = P.sb([128, 8]); fiA = P.sb([128, 8]); fiB = P.sb([128, 8]); t8b = P.sb([128, 8])
    P.op(V, lambda e: e.tensor_scalar(em1[:], xr[:], 0.25, 1.0, ALU.mult, ALU.add), reads=[xr], writes=[em1])
    P.op(V, lambda e: e.tensor_tensor(em1[:], em1[:], xr[:], ALU.mult), reads=[em1, xr], writes=[em1])
    P.op(V, lambda e: e.tensor_scalar(em1[:], em1[:], 1.0 / 3, 1.0, ALU.mult, ALU.add), reads=[em1], writes=[em1])
    P.op(V, lambda e: e.tensor_tensor(em1[:], em1[:], xr[:], ALU.mult), reads=[em1, xr], writes=[em1])
    P.op(V, lambda e: e.tensor_scalar(em1[:], em1[:], 0.5, 1.0, ALU.mult, ALU.add), reads=[em1], writes=[em1])
    P.op(V, lambda e: e.tensor_tensor(em1[:], em1[:], xr[:], ALU.mult), reads=[em1, xr], writes=[em1])
    P.op(V, lambda e: e.tensor_scalar(t8[:], ang[:], 0.5, None, ALU.mult), reads=[ang], writes=[t8])
    sin_of(sh, sh[:, :], t8, t8[:, :], 0.0, t8b, t8b[:, :], 8)
    P.op(V, lambda e: e.tensor_tensor(sh[:], sh[:], sh[:], ALU.mult), reads=[sh], writes=[sh])
    P.op(V, lambda e: e.tensor_tensor(nr[:], em1[:], C3[:, :, 1], ALU.mult), reads=[em1, C3], writes=[nr])
    P.op(V, lambda e: e.scalar_tensor_tensor(nr[:], sh[:], -2.0, nr[:], ALU.mult, ALU.add), reads=[sh, nr], writes=[nr])
    P.op(V, lambda e: e.tensor_tensor(den[:], lamp[:, :, 0], lamp[:, :, 0], ALU.mult), reads=[lamp], writes=[den])
    P.op(V, lambda e: e.tensor_tensor(t8[:], lamp[:, :, 1], lamp[:, :, 1], ALU.mult), reads=[lamp], writes=[t8])
    P.op(V, lambda e: e.tensor_tensor(den[:], den[:], t8[:], ALU.add), reads=[den, t8], writes=[den])
    P.op(V, lambda e: e.reciprocal(den[:], den[:]), reads=[den], writes=[den])
    P.op(V, lambda e: e.tensor_tensor(fr[:], nr[:], lamp[:, :, 0], ALU.mult), reads=[nr, lamp], writes=[fr])
    P.op(V, lambda e: e.tensor_tensor(t8[:], LI[:, :, 1], lamp[:, :, 1], ALU.mult), reads=[LI, lamp], writes=[t8])
    P.op(V, lambda e: e.tensor_tensor(fr[:], fr[:], t8[:], ALU.add), reads=[fr, t8], writes=[fr])
    P.op(V, lambda e: e.tensor_tensor(fr[:], fr[:], den[:], ALU.mult), reads=[fr, den], writes=[fr])
    P.op(V, lambda e: e.tensor_tensor(fi[:], LI[:, :, 1], lamp[:, :, 0], ALU.mult), reads=[LI, lamp], writes=[fi])
    P.op(V, lambda e: e.tensor_tensor(t8[:], nr[:], lamp[:, :, 1], ALU.mult), reads=[nr, lamp], writes=[t8])
    P.op(V, lambda e: e.tensor_tensor(fi[:], fi[:], t8[:], ALU.subtract), reads=[fi, t8], writes=[fi])
    P.op(V, lambda e: e.tensor_tensor(fi[:], fi[:], den[:], ALU.mult), reads=[fi, den], writes=[fi])
    P.op(V, lambda e: e.tensor_scalar(fiA[:], fi[:], sgn[:, 0:1], None, ALU.mult), reads=[fi, sgn], writes=[fiA])
    P.op(V, lambda e: e.tensor_scalar(fiB[:], fi[:], sgn[:, 1:2], None, ALU.mult), reads=[fi, sgn], writes=[fiB])

    BT = P.sb([128, 8, 128]); BTs = P.sb([128, 8, 128]); CC = P.sb([128, 8, 128])
    MATS = P.sb([128, 64, 128])
    KL = P.sb([128, 8, 128])
    LLB = [P.sb([128, 128]) for _ in range(2)]
    ps = [P.ps() for _ in range(8)]
    pk = 0
    for u in range(8):
        eng = V if u % 2 == 0 else "gpsimd"
        P.op(eng, lambda e, u=u: e.tensor_scalar(BT[:, u, :], bx[:, 0, u, :], fr[:, u:u + 1], None, ALU.mult), reads=[bx, fr], writes=[BT])
        P.op(V, lambda e, u=u: e.scalar_tensor_tensor(BT[:, u, :], bx[:, 1, u, :], fiA[:, u:u + 1], BT[:, u, :], ALU.mult, ALU.add), reads=[bx, fiA, BT], writes=[BT])
        P.op(eng, lambda e, u=u: e.tensor_scalar(BTs[:, u, :], bx[:, 1, u, :], fr[:, u:u + 1], None, ALU.mult), reads=[bx, fr], writes=[BTs])
        P.op(V, lambda e, u=u: e.scalar_tensor_tensor(BTs[:, u, :], bx[:, 0, u, :], fiB[:, u:u + 1], BTs[:, u, :], ALU.mult, ALU.add), reads=[bx, fiB, BTs], writes=[BTs])
        P.op(eng, lambda e, u=u: e.tensor_scalar(CC[:, u, :], cx[:, 0, u, :], sgn[:, 1:2], None, ALU.mult), reads=[cx, sgn], writes=[CC])
    pkl = [ps[0], ps[1]]
    for L in range(8):
        pK = pkl[L % 2]
        for u in range(8):
            llb = LLB[(L * 8 + u) % 2]
            eng = V if u % 2 == 0 else "gpsimd"
            P.op(eng, lambda e, u=u, L=L, llb=llb: e.tensor_scalar(llb[:], BT[:, u, :], LR[:, u, L:L + 1], None, ALU.mult), reads=[BT, LR], writes=[llb])
            P.op(V, lambda e, u=u, L=L, llb=llb: e.scalar_tensor_tensor(llb[:], BTs[:, u, :], LIA[:, u, L:L + 1], llb[:], ALU.mult, ALU.add), reads=[BTs, LIA, llb], writes=[llb])
            pt = ps[2 + (L * 8 + u) % 2]
            P.op("tensor", lambda e, llb=llb, pt=pt: e.matmul(pt[:, 0:128], lhsT=llb[:], rhs=ident[:], start=True, stop=True), reads=[llb, ident], writes=[pt])
            P.op("scalar", lambda e, u=u, L=L, pt=pt: e.activation(out=MATS[:, u * 8 + L, :], in_=pt[:, 0:128], func=AF.Copy), reads=[pt], writes=[MATS])
            P.op("tensor", lambda e, llb=llb, u=u, pK=pK: e.matmul(pK[:, 0:128], lhsT=llb[:], rhs=CC[:, u, :], start=(u == 0), stop=(u == 7)), reads=[llb, CC], writes=[pK])
        P.op(V, lambda e, L=L, pK=pK: e.tensor_copy(KL[:, L, :], pK[:, 0:128]), reads=[pK], writes=[KL])
    P.op(V, lambda e: e.scalar_tensor_tensor(KL[:, 0, :], ident[:], dsk[:, 0:1], KL[:, 0, :], ALU.mult, ALU.add), reads=[ident, dsk, KL], writes=[KL])

    EALL = P.sb([128, 8, NCHK + 1])
    UB = [P.sb([128, 2048]) for _ in range(2)]
    UD = [P.sb([128, 2048]) for _ in range(2)]
    P.op("gpsimd", lambda e: e.memset(EALL[:, :, 0:1], 0.0), writes=[EALL])
    pi = 0
    for bi, (n0, nb) in enumerate(NBLKS):
        ub = UB[bi % 2]
        for s in range(2):
            P.dma("sync", ub[64 * s:64 * s + 64, 0:nb * 8], us[s, :, n0 * 8:(n0 + nb) * 8], writes=[ub])
        ud = UD[bi % 2]
        P.op("gpsimd", lambda e, ub=ub, ud=ud, nb=nb: e.tensor_copy(ud[:, 0:8 * nb].rearrange("p (i n) -> p i n", i=8), ub[:, 0:8 * nb].rearrange("p (n i) -> p i n", i=8)),
             reads=[ub], writes=[ud])
        for u in range(8):
            pg = ps[4 + pi % 4]
            pi += 1
            for i in range(8):
                P.op("tensor", lambda e, u=u, i=i, ud=ud, nb=nb, pg=pg: e.matmul(pg[:, 0:nb], lhsT=MATS[:, u * 8 + (7 - i), :], rhs=ud[:, i * nb:(i + 1) * nb],
                                                                                start=(i == 0), stop=(i == 7)), reads=[MATS, ud], writes=[pg])
            if u % 2 == 0:
                P.op("scalar", lambda e, u=u, n0=n0, nb=nb, pg=pg: e.activation(out=EALL[:, u, 1 + n0:1 + n0 + nb], in_=pg[:, 0:nb], func=AF.Copy), reads=[pg], writes=[EALL])
            else:
                P.op(V, lambda e, u=u, n0=n0, nb=nb, pg=pg: e.tensor_copy(EALL[:, u, 1 + n0:1 + n0 + nb], pg[:, 0:nb]), reads=[pg], writes=[EALL])
    if upto < 2:
        return P.finish()
    W = [P.sb([128, NCHK]) for _ in range(2)]
    MT = [P.sb([128, 128]) for _ in range(2)]
    mi = 0
    for u in range(8):
        cur = None
        for it in range(12):
            shf = 2 ** it
            mt = MT[mi % 2]
            mi += 1
            P.op("gpsimd", lambda e, mt=mt, u=u, it=it: e.tensor_scalar(mt[:], ident[:], LR[:, u, 8 + it:9 + it], None, ALU.mult), reads=[ident, LR], writes=[mt])
            P.op(V, lambda e, mt=mt, u=u, it=it: e.scalar_tensor_tensor(mt[:], jmat[:], LI[:, u, 8 + it:9 + it], mt[:], ALU.mult, ALU.add), reads=[jmat, LI, mt], writes=[mt])
            dst = W[it % 2]
            src_ap = (lambda a, b, u=u: EALL[:, u, 1 + a:1 + b]) if cur is None else (lambda a, b, cur=cur: cur[:, a:b])
            src_t = EALL if cur is None else cur
            P.op("scalar", lambda e, dst=dst, src_ap=src_ap, shf=shf: e.activation(out=dst[:, 0:shf], in_=src_ap(0, shf), func=AF.Copy), reads=[src_t], writes=[dst])
            for c0 in range(shf, NCHK, 512):
                w = min(512, NCHK - c0)
                pd = ps[4 + pi % 4]
                pi += 1
                P.op("tensor", lambda e, mt=mt, src_ap=src_ap, c0=c0, w=w, shf=shf, pd=pd: e.matmul(pd[:, 0:w], lhsT=mt[:], rhs=src_ap(c0 - shf, c0 - shf + w), start=True, stop=True),
                     reads=[mt, src_t], writes=[pd])
                P.op(V, lambda e, dst=dst, src_ap=src_ap, c0=c0, w=w, pd=pd: e.tensor_tensor(dst[:, c0:c0 + w], pd[:, 0:w], src_ap(c0, c0 + w), ALU.add),
                     reads=[pd, src_t], writes=[dst])
            cur = dst
        P.op("scalar", lambda e, u=u, cur=cur: e.activation(out=EALL[:, u, 1:1 + NCHK], in_=cur[:, :], func=AF.Copy), reads=[cur], writes=[EALL])
    for u in range(8):
        for L in range(1, 9):
            eng = V if L % 2 == 0 else "gpsimd"
            P.op(eng, lambda e, u=u, L=L: e.tensor_scalar(MATS[:, u * 8 + L - 1, :], cx[:, 0, u, :], LRB[:, u, L:L + 1], None, ALU.mult), reads=[cx, LRB], writes=[MATS])
            P.op(V, lambda e, u=u, L=L: e.scalar_tensor_tensor(MATS[:, u * 8 + L - 1, :], cx[:, 1, u, :], LIN[:, u, L:L + 1], MATS[:, u * 8 + L - 1, :], ALU.mult, ALU.add),
                 reads=[cx, LIN, MATS], writes=[MATS])
    YB = W
    for bi, (n0, nb) in enumerate(NBLKS):
        ub = UB[bi % 2]; yb = YB[bi % 2]
        for s in range(2):
            P.dma("sync", ub[64 * s:64 * s + 64, 0:nb * 8], us[s, :, n0 * 8:(n0 + nb) * 8], writes=[ub])
        ud = UD[bi % 2]; yd = UB[(bi + 1) % 2]
        P.op("gpsimd", lambda e, ub=ub, ud=ud, nb=nb: e.tensor_copy(ud[:, 0:8 * nb].rearrange("p (i n) -> p i n", i=8), ub[:, 0:8 * nb].rearrange("p (n i) -> p i n", i=8)),
             reads=[ub], writes=[ud])
        for j in range(8):
            py = ps[pi % 4]
            pi += 1
            k = 0
            nmm = (j + 1) + 8
            for i in range(j + 1):
                P.op("tensor", lambda e, i=i, j=j, ud=ud, nb=nb, py=py, k=k, nmm=nmm: e.matmul(py[:, 0:nb], lhsT=KL[:, j - i, :], rhs=ud[:, i * nb:(i + 1) * nb],
                                                                                           start=(k == 0), stop=(k == nmm - 1)), reads=[KL, ud], writes=[py])
                k += 1
            for u in range(8):
                P.op("tensor", lambda e, u=u, j=j, n0=n0, nb=nb, py=py, k=k, nmm=nmm: e.matmul(py[:, 0:nb], lhsT=MATS[:, u * 8 + j, :], rhs=EALL[:, u, n0:n0 + nb],
                                                                                            start=(k == 0), stop=(k == nmm - 1)), reads=[MATS, EALL], writes=[py])
                k += 1
            if j % 2 == 0:
                P.op("scalar", lambda e, j=j, nb=nb, py=py, yd=yd: e.activation(out=yd[:, j * nb:(j + 1) * nb], in_=py[:, 0:nb], func=AF.Copy), reads=[py], writes=[yd])
            else:
                P.op(V, lambda e, j=j, nb=nb, py=py, yd=yd: e.tensor_copy(yd[:, j * nb:(j + 1) * nb], py[:, 0:nb]), reads=[py], writes=[yd])
        P.op("gpsimd", lambda e, yb=yb, yd=yd, nb=nb: e.tensor_copy(yb[:, 0:8 * nb].rearrange("p (n i) -> p i n", i=8), yd[:, 0:8 * nb].rearrange("p (i n) -> p i n", i=8)),
             reads=[yd], writes=[yb])
        for s in range(2):
            P.dma("gpsimd", ys[s, :, n0 * 8:(n0 + nb) * 8], yb[64 * s:64 * s + 64, 0:nb * 8], reads=[yb])
    return P.finish()


CW = 416
CHC = [(CW * i, CW) for i in range(5)]
NCTX = 32
FH = 2816
NFB = FH // 128
GELU_C = 2.0 * math.sqrt(2.0 / math.pi)


def build_C():
    P = Prog()
    hT = P.dram("hT", [D, NTOK], F32, "ExternalInput")
    zT = P.dram("zT", [N_IN, NTOK], F32, "ExternalInput")
    hlf = P.dram("hlf", [512, NTOK], F32, "ExternalInput"); hlb = P.dram("hlb", [512, NTOK], F32, "ExternalInput")
    ybT = P.dram("ybT", [512, NTOK], F32, "ExternalInput")
    ysf = P.dram("ysf", [512, NTOK], F32, "ExternalInput"); ysb = P.dram("ysb", [512, NTOK], F32, "ExternalInput")
    cnd = P.dram("cnd", [128, KT, 2], F32, "ExternalInput")
    wmod = P.dram("wmod", [D, 4096], F32, "ExternalInput")
    bmod = P.dram("bmod", [128, 32], F32, "ExternalInput")
    nrm2 = P.dram("nrm2", [128, KT], F32, "ExternalInput"); nrmf = P.dram("nrmf", [128, KT], F32, "ExternalInput")
    wglu = P.dram("wglu", [512, 512], F32, "ExternalInput"); bglu = P.dram("bglu", [128, 4], F32, "ExternalInput")
    wbr = P.dram("wbr", [3, 512, D], F32, "ExternalInput")
    wout = P.dram("wout", [D, D], F32, "ExternalInput")
    wfi = P.dram("wfi", [D, 2 * FH], F32, "ExternalInput")
    wfo = P.dram("wfo", [FH, D], F32, "ExternalInput")
    hN = P.dram("hN", [D, NTOK], F32, "ExternalOutput")
    hF = P.dram("hF", [D, NTOK], F32, "ExternalOutput")

    V = "vector"
    cond = P.sb([128, KT, 2]); bmod_sb = P.sb([128, 32]); n2 = P.sb([128, KT]); nf = P.sb([128, KT]); bg = P.sb([128, 4])
    ones = P.sb([128, 128]); epst = P.sb([128, 1]); modv = P.sb([128, 32, 2]); gsc = P.sb([128, KT, 2])
    WB = [P.sb([128, 4096]) for _ in range(3)]
    wbi = [0]

    def wbuf():
        w = WB[wbi[0] % 3]
        wbi[0] += 1
        return w
    H = P.sb([128, KT, CW]); XM = P.sb([128, KT, CW]); ACC = P.sb([128, KT, CW]); HO = P.sb([128, KT, CW])
    YK = [P.sb([128, 4, CW]) for _ in range(3)]
    T1 = P.sb([128, 4, CW]); T2 = P.sb([128, 4, CW])
    ZG = [P.sb([128, CW]) for _ in range(2)]
    TMP = [P.sb([128, CW]) for _ in range(2)]
    SQ = [P.sb([128, CW]) for _ in range(2)]
    RSTD = P.sb([128, CW])
    AH = P.sb([128, NFB, CW], BF16)
    XB = P.sb([128, KT, CW], BF16)
    WQ = [P.sb([128, 4096], BF16) for _ in range(2)]
    wqi = [0]
    pp = [P.ps() for _ in range(7)]
    psm = P.ps()
    pc = [0]

    def psum():
        p = pp[pc[0] % 7]
        pc[0] += 1
        return p

    for t, d in ((cond, cnd), (bmod_sb, bmod), (n2, nrm2), (nf, nrmf), (bg, bglu)):
        P.dma("sync", t[:], d, writes=[t])
    P.op(V, lambda e: e.memset(ones[:], 1.0), writes=[ones])
    P.op(V, lambda e: e.memset(epst[:], EPS), writes=[epst])
    P.op("scalar", lambda e: e.activation(out=cond[:], in_=cond[:], func=AF.Silu), reads=[cond], writes=[cond])
    WM = [Tile(WB[0].ap[:, 0:4096].rearrange("p (k c) -> p k c", k=KT), WB[0].cells), Tile(WB[1].ap[:, 0:4096].rearrange("p (k c) -> p k c", k=KT), WB[1].cells)]
    mod_vectors(P, wmod, bmod_sb, cond, 32, WM, psm, modv)
    for which in range(2):
        P.op(V, (lambda e, which=which: e.scalar_tensor_tensor(gsc[:, :, which], modv[:, 16:24, which], 1.0, n2[:, :], ALU.add, ALU.mult)),
             reads=[modv, n2], writes=[gsc])

    def gelu_tanh(dst, src, tmp, eng2="gpsimd"):
        fl = lambda t: t[:, :, :].rearrange("p a b -> p (a b)")
        P.op("scalar", lambda e: e.activation(out=fl(tmp), in_=fl(src), func=AF.Square), reads=[src], writes=[tmp])
        P.op(V, lambda e: e.tensor_scalar(fl(tmp), fl(tmp), 0.044715, 1.0, ALU.mult, ALU.add), reads=[tmp], writes=[tmp])
        P.op(eng2, lambda e: e.tensor_tensor(fl(tmp), fl(tmp), fl(src), ALU.mult), reads=[tmp, src], writes=[tmp])
        P.op("scalar", lambda e: e.activation(out=fl(tmp), in_=fl(tmp), func=AF.Sigmoid, scale=GELU_C), reads=[tmp], writes=[tmp])
        P.op(V, lambda e: e.tensor_tensor(fl(dst), fl(tmp), fl(src), ALU.mult), reads=[tmp, src], writes=[dst])

    def rstd_of(X, which_none=None):
        ps = psum()
        for kt in range(KT):
            s = SQ[kt % 2]
            P.op("scalar", lambda e, s=s, kt=kt: e.activation(out=s[:, :], in_=X[:, kt, :], func=AF.Square), reads=[X], writes=[s])
            P.op("tensor", lambda e, s=s, kt=kt, ps=ps: e.matmul(ps[:, 0:CW], lhsT=ones[:, :], rhs=s[:, :], start=(kt == 0), stop=(kt == KT - 1)),
                 reads=[s, ones], writes=[ps])
        P.op("scalar", lambda e, ps=ps: e.activation(out=RSTD[:, :], in_=ps[:, 0:CW], func=AF.Sqrt, bias=epst[:, 0:1], scale=1.0 / D), reads=[ps, epst], writes=[RSTD])
        P.op(V, lambda e: e.reciprocal(RSTD[:, :], RSTD[:, :]), reads=[RSTD], writes=[RSTD])

    for ci, (o, n) in enumerate(CHC):
        rngs = [(0, NCTX, 1), (NCTX, n, 0)] if ci == 0 else [(0, n, 0)]
        tk = slice(o, o + n)
        P.dma("gpsimd", H[:, :, :], hT[:, tk].rearrange("(kt p) t -> p kt t", p=128), writes=[H])
        P.dma("gpsimd", T1[:, :, :], zT[512:1024, tk].rearrange("(kt p) t -> p kt t", p=128), writes=[T1])
        P.dma("gpsimd", YK[0][:, :, :], hlf[:, tk].rearrange("(kt p) t -> p kt t", p=128), writes=[YK[0]])
        P.dma("gpsimd", YK[1][:, :, :], hlb[:, tk].rearrange("(kt p) t -> p kt t", p=128), writes=[YK[1]])
        gelu_tanh(T1, T1, T2)
        P.op("gpsimd", lambda e: e.tensor_tensor(YK[0][:, :, :], YK[0][:, :, :], YK[1][:, :, :], ALU.add), reads=[YK[0], YK[1]], writes=[YK[0]])
        P.op(V, lambda e: e.tensor_tensor(YK[0][:, :, :], YK[0][:, :, :], T1[:, :, :], ALU.mult), reads=[YK[0], T1], writes=[YK[0]])
        P.dma("gpsimd", YK[1][:, :, :], ybT[:, tk].rearrange("(kt p) t -> p kt t", p=128), writes=[YK[1]])
        P.dma("gpsimd", T1[:, :, :], ysf[:, tk].rearrange("(kt p) t -> p kt t", p=128), writes=[T1])
        P.dma("gpsimd", YK[2][:, :, :], ysb[:, tk].rearrange("(kt p) t -> p kt t", p=128), writes=[YK[2]])
        P.op("gpsimd", lambda e: e.tensor_tensor(T1[:, :, :], T1[:, :, :], YK[2][:, :, :], ALU.add), reads=[T1, YK[2]], writes=[T1])
        gelu_tanh(T1, T1, T2)
        wg = wbuf()
        P.dma("sync", wg[:, 0:2048].rearrange("p (k c) -> p k c", k=4), wglu.rearrange("(kt p) c -> p kt c", p=128), writes=[wg])
        for jb in range(4):
            ps = psum()
            for kt in range(4):
                P.op("tensor", lambda e, ps=ps, wg=wg, kt=kt, jb=jb: e.matmul(ps[:, 0:CW], lhsT=wg[:, kt * 512 + jb * 128:kt * 512 + jb * 128 + 128], rhs=T1[:, kt, :],
                                                                             start=(kt == 0), stop=(kt == 3)), reads=[wg, T1], writes=[ps])
            P.op("scalar", lambda e, ps=ps, jb=jb: e.activation(out=T2[:, jb, :], in_=ps[:, 0:CW], func=AF.Sigmoid, bias=bg[:, jb:jb + 1]), reads=[ps, bg], writes=[T2])
        P.op(V, lambda e: e.tensor_tensor(YK[2][:, :, :], T1[:, :, :], T2[:, :, :], ALU.mult), reads=[T1, T2], writes=[YK[2]])
        for k in range(3):
            wb = wbuf()
            P.dma("sync", wb[:, 0:4096].rearrange("p (k c) -> p k c", k=4), wbr[k].rearrange("(kt p) c -> p kt c", p=128), writes=[wb])
            for db in range(8):
                zg = ZG[(k * 8 + db) % 2]
                r0 = 3456 + k * 1024 + db * 128
                P.dma("gpsimd", zg[:, :], zT[r0:r0 + 128, tk], writes=[zg])
                P.op("scalar", lambda e, zg=zg: e.activation(out=zg[:, :], in_=zg[:, :], func=AF.Sigmoid), reads=[zg], writes=[zg])
                ps = psum()
                for kt in range(4):
                    P.op("tensor", lambda e, ps=ps, wb=wb, kt=kt, db=db, k=k: e.matmul(ps[:, 0:CW], lhsT=wb[:, kt * 1024 + db * 128:kt * 1024 + db * 128 + 128], rhs=YK[k][:, kt, :],
                                                                                    start=(kt == 0), stop=(kt == 3)), reads=[wb, YK[k]], writes=[ps])
                if k == 0:
                    P.op(V, lambda e, ps=ps, zg=zg, db=db: e.tensor_tensor(ACC[:, db, :], ps[:, 0:CW], zg[:, :], ALU.mult), reads=[ps, zg], writes=[ACC])
                else:
                    tm = TMP[db % 2]
                    P.op(V, lambda e, ps=ps, zg=zg, tm=tm: e.tensor_tensor(tm[:, :], ps[:, 0:CW], zg[:, :], ALU.mult), reads=[ps, zg], writes=[tm])
                    P.op("gpsimd", lambda e, tm=tm, db=db: e.tensor_tensor(ACC[:, db, :], ACC[:, db, :], tm[:, :], ALU.add), reads=[ACC, tm], writes=[ACC])
        for half in range(2):
            wo = wbuf()
            P.dma("sync", wo[:, 0:4096].rearrange("p (k c) -> p k c", k=8), wout[:, half * 512:(half + 1) * 512].rearrange("(kt p) c -> p kt c", p=128), writes=[wo])
            for dq in range(4):
                db = half * 4 + dq
                ps = psum()
                for kt in range(KT):
                    P.op("tensor", lambda e, ps=ps, wo=wo, kt=kt, dq=dq: e.matmul(ps[:, 0:CW], lhsT=wo[:, kt * 512 + dq * 128:kt * 512 + dq * 128 + 128], rhs=ACC[:, kt, :],
                                                                               start=(kt == 0), stop=(kt == KT - 1)), reads=[wo, ACC], writes=[ps])
                for (a, b, which) in rngs:
                    P.op(V, lambda e, ps=ps, db=db, a=a, b=b, which=which: e.scalar_tensor_tensor(H[:, db, a:b], ps[:, a:b], modv[:, db, which:which + 1], H[:, db, a:b], ALU.mult, ALU.add),
                         reads=[ps, modv, H], writes=[H])
        rstd_of(H)
        for kt in range(KT):
            eng = V if kt % 2 == 0 else "gpsimd"
            P.op(eng, lambda e, kt=kt: e.tensor_tensor(XM[:, kt, :], H[:, kt, :], RSTD[:, :], ALU.mult), reads=[H, RSTD], writes=[XM])
            for (a, b, which) in rngs:
                P.op(eng, lambda e, kt=kt, a=a, b=b, which=which: e.tensor_scalar(XB[:, kt, a:b], XM[:, kt, a:b], gsc[:, kt, which:which + 1], modv[:, 8 + kt, which:which + 1], ALU.mult, ALU.add),
                     reads=[XM, gsc, modv], writes=[XB])
        for fq in range(0, NFB, 2):
            wf = wbuf()
            for gu in range(2):
                P.dma("sync", wf[:, 0:4096].rearrange("p (k g c) -> p k g c", k=8, g=2)[:, :, gu, :],
                      wfi[:, gu * FH + fq * 128:gu * FH + fq * 128 + 256].rearrange("(kt p) c -> p kt c", p=128), writes=[wf])
            wq = WQ[wqi[0] % 2]; wqi[0] += 1
            P.op("gpsimd", lambda e, wq=wq, wf=wf: e.tensor_copy(wq[:, :], wf[:, :]), reads=[wf], writes=[wq])
            wf = wq
            for f2 in range(2):
                fb = fq + f2
                pg = psum(); pu = psum()
                for kt in range(KT):
                    P.op("tensor", lambda e, pg=pg, wf=wf, kt=kt, f2=f2: e.matmul(pg[:, 0:CW], lhsT=wf[:, kt * 512 + f2 * 128:kt * 512 + f2 * 128 + 128], rhs=XB[:, kt, :],
                                                                               start=(kt == 0), stop=(kt == KT - 1)), reads=[wf, XB], writes=[pg])
                for kt in range(KT):
                    P.op("tensor", lambda e, pu=pu, wf=wf, kt=kt, f2=f2: e.matmul(pu[:, 0:CW], lhsT=wf[:, kt * 512 + 256 + f2 * 128:kt * 512 + 256 + f2 * 128 + 128], rhs=XB[:, kt, :],
                                                                               start=(kt == 0), stop=(kt == KT - 1)), reads=[wf, XB], writes=[pu])
                tm = TMP[fb % 2]
                P.op("scalar", lambda e, pg=pg, tm=tm: e.activation(out=tm[:, :], in_=pg[:, 0:CW], func=AF.Silu), reads=[pg], writes=[tm])
                P.op(V, lambda e, pu=pu, tm=tm, fb=fb: e.tensor_tensor(AH[:, fb, :], pu[:, 0:CW], tm[:, :], ALU.mult), reads=[pu, tm], writes=[AH])
        for db in range(8):
            wo = wbuf()
            P.dma("sync", wo[:, 0:NFB * 128].rearrange("p (f c) -> p f c", f=NFB), wfo[:, db * 128:(db + 1) * 128].rearrange("(f p) c -> p f c", p=128), writes=[wo])
            wq = WQ[wqi[0] % 2]; wqi[0] += 1
            P.op("gpsimd", lambda e, wq=wq, wo=wo: e.tensor_copy(wq[:, 0:NFB * 128], wo[:, 0:NFB * 128]), reads=[wo], writes=[wq])
            wo = wq
            ps = psum()
            for fb in range(NFB):
                P.op("tensor", lambda e, ps=ps, wo=wo, fb=fb: e.matmul(ps[:, 0:CW], lhsT=wo[:, fb * 128:(fb + 1) * 128], rhs=AH[:, fb, :],
                                                                    start=(fb == 0), stop=(fb == NFB - 1)), reads=[wo, AH], writes=[ps])
            for (a, b, which) in rngs:
                P.op(V, lambda e, ps=ps, db=db, a=a, b=b, which=which: e.scalar_tensor_tensor(HO[:, db, a:b], ps[:, a:b], modv[:, 24 + db, which:which + 1], H[:, db, a:b], ALU.mult, ALU.add),
                     reads=[ps, modv, H], writes=[HO])
        P.dma("gpsimd", hN[:, tk].rearrange("(kt p) t -> p kt t", p=128), HO[:, :, :], reads=[HO])
        rstd_of(HO)
        for kt in range(KT):
            eng = V if kt % 2 == 0 else "gpsimd"
            P.op(eng, lambda e, kt=kt: e.tensor_tensor(XM[:, kt, :], HO[:, kt, :], RSTD[:, :], ALU.mult), reads=[HO, RSTD], writes=[XM])
            P.op(eng, lambda e, kt=kt: e.tensor_scalar(XM[:, kt, :], XM[:, kt, :], nf[:, kt:kt + 1], None, ALU.mult), reads=[XM, nf], writes=[XM])
        P.dma("gpsimd", hF[:, tk].rearrange("(kt p) t -> p kt t", p=128), XM[:, :, :], reads=[XM])
    return P.finish()


_PROGS = {}


def _prog(name):
    if name not in _PROGS:
        _PROGS[name] = {"A": build_A, "B1": build_B1, "B2": build_B2, "B3": build_B3, "C": build_C}[name]()
    return _PROGS[name]


def _run(name, maps):
    maps = [{k: np.ascontiguousarray(v, dtype=np.float32) for k, v in m.items()} for m in maps]
    res = run_bass_kernel_spmd(_prog(name), maps, core_ids=list(range(8)))
    return res.results


def _tok_shard(ctxT, latT, k):
    return np.concatenate([ctxT[:, 32 * k:32 * k + 32], latT[:, 2048 * k:2048 * k + 2048]], axis=1)


def _streams(zc, zl):
    return np.concatenate([zc, zl], 0).T, np.concatenate([zc[::-1], zl[::-1]], 0).T


def _col_major(t):
    return t.reshape(256, 64, -1).transpose(1, 0, 2).reshape(16384, -1)


def _raster_cm(t):
    c = t.shape[0]
    return t.reshape(c, 64, 256).transpose(0, 2, 1).reshape(c, 16384)


def _cnd(c, c_ctx):
    return np.stack([c.reshape(-1), c_ctx.reshape(-1)], axis=1).reshape(8, 128, 2).transpose(1, 0, 2)


def _pk(v):
    return v.reshape(8, 128).T


def kernel(x, c, ctx, c_ctx, w_mod, b_mod, norm1, norm2, norm_f, w_in,
           lru_conv_w, lru_conv_b, lru_wa, lru_ba, lru_wx, lru_bx, lru_lam,
           rwkv_mu, rwkv_w0, rwkv_w2, rwkv_a0, rwkv_a2, rwkv_g2, rwkv_kk, rwkv_ka, rwkv_rk,
           rwkv_lnw, rwkv_lnb,
           s5_lam_re, s5_lam_im, s5_log_step, s5_b_re, s5_b_im, s5_c_re, s5_c_im, s5_d,
           s5_w_glu, s5_b_glu,
           w_branch, w_out, w_ffn_in, w_ffn_out, _nlayers=4, _debug=None):
    f32 = np.float32
    A_ = lambda a: np.asarray(a, dtype=f32)
    x, c, ctx, c_ctx = A_(x), A_(c), A_(ctx), A_(c_ctx)
    latT = np.ascontiguousarray(x[0].T)
    ctxT = np.ascontiguousarray(ctx[0].T)
    cnd = _cnd(c, c_ctx)
    i64 = np.arange(64)
    su = (i64[None, :] > i64[:, None]).astype(f32); up = (i64[None, :] >= i64[:, None]).astype(f32)
    m1 = np.concatenate([su, up, -su, up, -su.T], 1)
    cm = np.ones((128, 1024), f32); cm[:, ::64] = 0
    cB2 = {"mask": np.concatenate([m1, m1], 0), "ident": np.eye(128, dtype=f32),
           "ident2": np.stack([np.concatenate([np.eye(64, dtype=f32)] * 2, 0)] * 2, 1), "cmask": cm}
    jm = np.zeros((128, 128), f32)
    for k in range(64):
        jm[k, k + 64] = 1.0; jm[k + 64, k] = -1.0
    cB3 = {"lagt": np.broadcast_to(np.array(LAGS, f32)[None, None, :], (128, 8, NL)).copy(),
           "sgn": np.concatenate([np.tile([[-1.0, 1.0]], (64, 1)), np.tile([[1.0, -1.0]], (64, 1))], 0).astype(f32),
           "ident": np.eye(128, dtype=f32), "jmat": jm}
    hF = None
    for li in range(_nlayers):
        wm, bm = A_(w_mod[li]), A_(b_mod[li])
        mapsA = [{"xT": _tok_shard(ctxT, latT, k), "cnd": cnd, "wmod": wm[:, 0:2048], "bmod": bm[0:2048].reshape(-1, 128).T,
                  "nrm": _pk(A_(norm1[li])), "win": A_(w_in[li])} for k in range(8)]
        zTs = [r["zT"] for r in _run("A", mapsA)]
        zc = np.concatenate([z[:, :32] for z in zTs], axis=1).T
        zl = np.concatenate([z[:, 32:] for z in zTs], axis=1).T
        xa0, xa1 = _streams(zc[:, 0:512], zl[:, 0:512])
        mapsB1 = []
        for h in range(8):
            sl = slice(64 * h, 64 * h + 64)
            w = A_(lru_conv_w[li])[:, sl]
            z0 = np.zeros(64, f32)
            cw5 = np.concatenate([np.stack([w[0], w[1], w[2], w[3], z0], 1), np.stack([z0, w[3], w[2], w[1], w[0]], 1)], 0)
            vec = lambda a: np.concatenate([A_(a[li])[0][sl], A_(a[li])[1][sl]])
            cb = np.concatenate([A_(lru_conv_b[li])[sl]] * 2)
            vecs = np.stack([cb, vec(lru_ba), vec(lru_bx), vec(lru_lam)], 1)
            wg = np.stack([np.concatenate([A_(lru_wa[li])[0][h], A_(lru_wa[li])[1][h]], 0),
                           np.concatenate([A_(lru_wx[li])[0][h], A_(lru_wx[li])[1][h]], 0)], 1)
            mapsB1.append({"xs": np.stack([xa0[sl], xa1[sl]]), "cw5": cw5, "vecs": vecs, "wg": wg})
        hs = np.stack([r["hs"] for r in _run("B1", mapsB1)])
        hlf = hs[:, 0].reshape(512, -1)
        hb = hs[:, 1].reshape(512, -1)
        hlb = np.concatenate([hb[:, :256][:, ::-1], hb[:, 256:][:, ::-1]], axis=1)
        o = 1024
        mu = A_(rwkv_mu[li])
        mapsB2 = []
        for h in range(8):
            hsl = slice(64 * h, 64 * h + 64)
            st = []
            for s in range(2):
                cols = np.r_[o + 64 * h:o + 64 * h + 64, o + 512 + 64 * h:o + 512 + 64 * h + 64, o + 1024 + 64 * h:o + 1024 + 64 * h + 64,
                             o + 1536 + 64 * s:o + 1536 + 64 * s + 64, o + 1664 + 64 * s:o + 1664 + 64 * s + 64, o + 1792:o + 1920]
                st.append(_streams(zc[:, cols], zl[:, cols])[s])
            mu5 = np.concatenate([np.stack([mu[64 * h:64 * h + 64], mu[512 + 64 * h:512 + 64 * h + 64], mu[1024 + 64 * h:1024 + 64 * h + 64],
                                            mu[1536 + 64 * s:1536 + 64 * s + 64], mu[1664 + 64 * s:1664 + 64 * s + 64]], 1) for s in range(2)], 0)
            w2a2 = np.concatenate([np.stack([A_(rwkv_w2[li])[s][:, hsl], A_(rwkv_a2[li])[s][:, hsl]], 1) for s in range(2)], 0)
            vecs = np.concatenate([np.stack([A_(rwkv_w0[li])[s][hsl], A_(rwkv_a0[li])[s][hsl], A_(rwkv_kk[li])[hsl], A_(rwkv_ka[li])[hsl],
                                             A_(rwkv_rk[li])[h]], 1) for s in range(2)], 0)
            d = {"zs": np.stack(st), "mu5": mu5, "mug": mu[1792:1920].reshape(128, 1), "w2a2": w2a2, "vecs": vecs,
                 "g2h": A_(rwkv_g2[li])[:, hsl], "lnwb": np.stack([A_(rwkv_lnw[li])[hsl], A_(rwkv_lnb[li])[hsl]], 1)}
            d.update(cB2)
            mapsB2.append(d)
        ybT = np.stack([r["yb"] for r in _run("B2", mapsB2)]).reshape(512, -1)
        us0, us1 = _streams(zc[:, 2944:3456], _col_major(zl[:, 2944:3456]))
        mapsB3 = []
        for h in range(8):
            lamp = np.zeros((128, 8, 3), f32); bx = np.zeros((128, 2, 8, 128), f32); cx = np.zeros((128, 2, 8, 128), f32)
            for s in range(2):
                for gl in range(4):
                    g = 4 * h + gl; u = s * 4 + gl; cs = slice(64 * s + 16 * gl, 64 * s + 16 * gl + 16)
                    for half in range(2):
                        r_ = slice(64 * half, 64 * half + 64)
                        lamp[r_, u, 0] = A_(s5_lam_re[li])[s][g]; lamp[r_, u, 1] = A_(s5_lam_im[li])[s][g]; lamp[r_, u, 2] = A_(s5_log_step[li])[s][g]
                    bre, bim = A_(s5_b_re[li])[s][g], A_(s5_b_im[li])[s][g]
                    cre, cim = A_(s5_c_re[li])[s][g].T, A_(s5_c_im[li])[s][g].T
                    bx[0:64, 0, u, cs] = bre; bx[64:128, 0, u, cs] = bim; bx[0:64, 1, u, cs] = bim; bx[64:128, 1, u, cs] = bre
                    cx[0:64, 0, u, cs] = cre; cx[64:128, 0, u, cs] = cim; cx[0:64, 1, u, cs] = cim; cx[64:128, 1, u, cs] = cre
            dsk = np.zeros((128, 1), f32); dsk[0:64, 0] = A_(s5_d[li])[64 * h:64 * h + 64]
            d = {"us": np.stack([us0[64 * h:64 * h + 64], us1[64 * h:64 * h + 64]]), "lamp": lamp, "bx": bx, "cx": cx, "dsk": dsk}
            d.update(cB3)
            mapsB3.append(d)
        ysr = np.stack([r["ys"] for r in _run("B3", mapsB3)])
        yf = ysr[:, 0].reshape(512, -1); yb_ = ysr[:, 1].reshape(512, -1)
        ysf = np.concatenate([yf[:, :256], _raster_cm(yf[:, 256:])], axis=1)
        ysb = np.concatenate([yb_[:, :256][:, ::-1], _raster_cm(yb_[:, 256:][:, ::-1])], axis=1)
        shard = lambda t: [np.concatenate([t[:, 32 * k:32 * k + 32], t[:, 256 + 2048 * k:256 + 2048 * k + 2048]], axis=1) for k in range(8)]
        s_hlf, s_hlb, s_yb, s_ysf, s_ysb = shard(hlf), shard(hlb), shard(ybT), shard(ysf), shard(ysb)
        mapsC = [{"hT": _tok_shard(ctxT, latT, k), "zT": zTs[k], "hlf": s_hlf[k], "hlb": s_hlb[k], "ybT": s_yb[k], "ysf": s_ysf[k], "ysb": s_ysb[k],
                  "cnd": cnd, "wmod": wm[:, 2048:6144], "bmod": bm[2048:6144].reshape(-1, 128).T, "nrm2": _pk(A_(norm2[li])), "nrmf": _pk(A_(norm_f)),
                  "wglu": A_(s5_w_glu[li]), "bglu": A_(s5_b_glu[li]).reshape(4, 128).T, "wbr": A_(w_branch[li]), "wout": A_(w_out[li]),
                  "wfi": A_(w_ffn_in[li]), "wfo": A_(w_ffn_out[li])} for k in range(8)]
        resC = _run("C", mapsC)
        ctxT = np.concatenate([r["hN"][:, :32] for r in resC], axis=1)
        latT = np.concatenate([r["hN"][:, 32:] for r in resC], axis=1)
        hF = np.concatenate([r["hF"][:, 32:] for r in resC], axis=1)
        if _debug is not None:
            _debug.append({"latT": latT, "ctxT": ctxT, "hlf": hlf, "hlb": hlb, "ybT": ybT, "ysf": ysf, "ysb": ysb})
    return np.ascontiguousarray(hF.T).reshape(1, 16384, 1024).astype(np.float32)
```
